# Optimizing a Trainium2 kernel written in Bass

```python
import math
import jax, jax.numpy as jnp
from jax import lax
import numpy as np

D_MODEL = 1024
BATCH = 4
SEQ = 4096
DEPTH = 2
DEC_BATCH = 32
DEC_SEQ = 1
PAST_LEN = 16384
PAGE_SIZE = 128

HEAD_DIM = 64
ATT_HEADS = 8
KV_HEADS = 4
GROUP = ATT_HEADS // KV_HEADS
ATT_WIDTH = ATT_HEADS * HEAD_DIM
KV_WIDTH = KV_HEADS * HEAD_DIM
IDX_HEADS = 8
IDX_DIM = 32
TOPK = 256
POOL_WINDOWS = (2, 4, 8, 16)
N_POOL_GROUPS = 4
POOL_GROUP_DIM = 64
POOL_WIDTH = N_POOL_GROUPS * POOL_GROUP_DIM
POOL_STATE = 15
MEM_HEADS = 4
MEM_WIDTH = MEM_HEADS * HEAD_DIM
N_MEM = 256
MIX_WIDTH = ATT_WIDTH + POOL_WIDTH + MEM_WIDTH
D_FF = 2816
CONV_WIDTH = 3
NUM_BUCKETS = 32
MAX_DISTANCE = 2048
QB = 128
EPS = 1e-6
IN_SIZES = (ATT_WIDTH, KV_WIDTH, KV_WIDTH, IDX_HEADS * IDX_DIM, IDX_DIM, IDX_HEADS, POOL_WIDTH, MEM_WIDTH)
N_IN = sum(IN_SIZES)

kernel_name = 'hymba_dsa_pool_memory_decoder_step'


def rmsnorm(x, g):
    x32 = x.astype(jnp.float32)
    y = x32 * lax.rsqrt(jnp.mean(x32 * x32, axis=-1, keepdims=True) + EPS)
    return (y * g.astype(jnp.float32)).astype(x.dtype)


def split_in(z):
    parts, o = [], 0
    for n in IN_SIZES:
        parts.append(z[..., o:o + n])
        o += n
    return parts


def project_mixers(h, w_in_l, q_g, k_g, mq_g):
    B, T, _ = h.shape
    q, k, v, iq, ik, iw, u, mq = split_in(h @ w_in_l)
    q = rmsnorm(q.reshape(B, T, ATT_HEADS, HEAD_DIM), q_g)
    k = rmsnorm(k.reshape(B, T, KV_HEADS, HEAD_DIM), k_g)
    v = v.reshape(B, T, KV_HEADS, HEAD_DIM)
    iq = iq.reshape(B, T, IDX_HEADS, IDX_DIM)
    mq = rmsnorm(mq.reshape(B, T, MEM_HEADS, HEAD_DIM), mq_g)
    return q, k, v, iq, ik, iw, u, mq


def t5_bucket(dist):
    max_exact = NUM_BUCKETS // 2
    d = jnp.maximum(dist, 1).astype(jnp.float32)
    large = max_exact + (jnp.log(d / max_exact) / math.log(MAX_DISTANCE / max_exact)
                         * (NUM_BUCKETS - max_exact)).astype(jnp.int32)
    large = jnp.minimum(large, NUM_BUCKETS - 1)
    return jnp.where(dist < max_exact, dist, large)


def indexer_scores(iq, ik, iw, qpos, kpos):
    s = jax.nn.relu(jnp.einsum('bqhd,bld->bqhl', iq, ik) * (IDX_DIM ** -0.5))
    score = jnp.einsum('bqhl,bqh->bql', s, iw * (IDX_HEADS ** -0.5)).astype(jnp.float32)
    return jnp.where(kpos[None, None, :] <= qpos[None, :, None], score, -jnp.inf)


def sparse_attend(q, k_sel, v_sel, idx, qpos, rel_bias):
    B, Q = q.shape[0], q.shape[1]
    K = idx.shape[-1]
    qg = q.reshape(B, Q, KV_HEADS, GROUP, HEAD_DIM)
    logits = jnp.einsum('bqngd,bqknd->bqngk', qg, k_sel).astype(jnp.float32) * (HEAD_DIM ** -0.5)
    dist = qpos[None, :, None] - idx
    bias = rel_bias[t5_bucket(jnp.maximum(dist, 0))]
    bias = bias.reshape(B, Q, K, KV_HEADS, GROUP).transpose(0, 1, 3, 4, 2).astype(jnp.float32)
    logits = jnp.where((dist >= 0)[:, :, None, None, :], logits + bias, -jnp.inf)
    p = jax.nn.softmax(logits, axis=-1).astype(v_sel.dtype)
    o = jnp.einsum('bqngk,bqknd->bqngd', p, v_sel)
    return o.reshape(B, Q, ATT_WIDTH)


def gather_rows(rows, idx):
    return jax.vmap(lambda r, i: r[i])(rows, idx)


def dsa_prompt(q, k, v, iq, ik, iw, rel_bias):
    B, S = q.shape[0], q.shape[1]
    ktop = min(TOPK, S // 4)
    nb = S // QB
    kpos = jnp.arange(S)

    def block(args):
        bi, qb, iqb, iwb = args
        qpos = bi * QB + jnp.arange(QB)
        score = indexer_scores(iqb, ik, iwb, qpos, kpos)
        _, idx = lax.top_k(score, ktop)
        return sparse_attend(qb, gather_rows(k, idx), gather_rows(v, idx), idx, qpos, rel_bias)

    to_blocks = lambda a: a.reshape((B, nb, QB) + a.shape[2:]).swapaxes(0, 1)
    out = lax.map(block, (jnp.arange(nb), to_blocks(q), to_blocks(iq), to_blocks(iw)))
    return out.swapaxes(0, 1).reshape(B, S, ATT_WIDTH)


def dsa_sample(q, k_new, v_new, iq, ik_new, iw, ck, cv, cik, page_table, rel_bias):
    Bd, T = q.shape[0], q.shape[1]
    past = page_table.shape[1] * PAGE_SIZE
    ik_past = cik[page_table].reshape(Bd, past, IDX_DIM)
    ik_all = jnp.concatenate([ik_past, ik_new], axis=1)
    L = past + T
    ktop = min(TOPK, L // 4)
    qpos = past + jnp.arange(T)
    score = indexer_scores(iq, ik_all, iw, qpos, jnp.arange(L))
    _, idx = lax.top_k(score, ktop)
    idx_p = jnp.minimum(idx, past - 1)
    phys = jax.vmap(lambda pt, i: pt[i])(page_table, idx_p // PAGE_SIZE)
    off = idx_p % PAGE_SIZE
    j_new = jnp.clip(idx - past, 0, T - 1)
    is_new = (idx >= past)[..., None, None]
    k_sel = jnp.where(is_new, gather_rows(k_new, j_new), ck[phys, off])
    v_sel = jnp.where(is_new, gather_rows(v_new, j_new), cv[phys, off])
    return sparse_attend(q, k_sel, v_sel, idx, qpos, rel_bias)


def multi_pool(u_ext, pos_ext, w_pool_l, scale_l):
    B, N, _ = u_ext.shape
    uf = u_ext.astype(jnp.float32).reshape(B, N, N_POOL_GROUPS, POOL_GROUP_DIM)
    csum = jnp.cumsum(uf, axis=1)
    outs = []
    for g, w in enumerate(POOL_WINDOWS):
        cg = csum[:, :, g]
        lag = jnp.pad(cg, ((0, 0), (w, 0), (0, 0)))[:, :N]
        cnt = jnp.minimum(pos_ext + 1, w).astype(jnp.float32)
        outs.append((cg - lag) / cnt[None, :, None] - uf[:, :, g])
    pooled = jnp.stack(outs, axis=2)
    y = jnp.einsum('bngc,gcd->bngd', pooled, w_pool_l.astype(jnp.float32)).reshape(B, N, POOL_WIDTH)
    return (y * scale_l.astype(jnp.float32)).astype(u_ext.dtype)


def mem_kv(mem, g, w, kg):
    B, M, _ = mem.shape
    kv = rmsnorm(mem, g) @ w
    mk = rmsnorm(kv[..., :MEM_WIDTH].reshape(B, M, MEM_HEADS, HEAD_DIM), kg)
    mv = kv[..., MEM_WIDTH:].reshape(B, M, MEM_HEADS, HEAD_DIM)
    return mk, mv


def mem_attend(mq, mk, mv):
    B, Q = mq.shape[0], mq.shape[1]
    logits = jnp.einsum('bqhd,bmhd->bhqm', mq, mk).astype(jnp.float32) * (HEAD_DIM ** -0.5)
    p = jax.nn.softmax(logits, axis=-1).astype(mv.dtype)
    return jnp.einsum('bhqm,bmhd->bqhd', p, mv).reshape(B, Q, MEM_WIDTH)


def conv_ffn(up_ext, cw, cb, wd):
    T = up_ext.shape[1] - (CONV_WIDTH - 1)
    c = cb
    for j in range(CONV_WIDTH):
        c = c + cw[j] * up_ext[:, j:j + T]
    g, val = c[..., :D_FF], c[..., D_FF:]
    return (jax.nn.silu(g) * val) @ wd


def setup_inputs(seed: int = 0) -> dict:
    key = jax.random.key(seed)
    ks = jax.random.split(key, 32)
    f32 = jnp.float32
    nrm = lambda k, shape, s: jax.random.normal(k, shape, f32) * s
    n_pages = PAST_LEN // PAGE_SIZE
    n_used = DEC_BATCH * n_pages
    n_pool = n_used + n_used // 4
    page_table = jax.random.permutation(ks[0], n_pool)[:n_used].reshape(DEC_BATCH, n_pages).astype(jnp.int32)
    two_f = 2 * D_FF
    return {
        'x_prompt': nrm(ks[1], (BATCH, SEQ, D_MODEL), 1.0),
        'x_sample': nrm(ks[2], (DEC_BATCH, DEC_SEQ, D_MODEL), 1.0),
        'mem_prompt': nrm(ks[3], (BATCH, N_MEM, D_MODEL), 1.0),
        'cache_k': nrm(ks[4], (DEPTH, n_pool, PAGE_SIZE, KV_HEADS, HEAD_DIM), 1.0),
        'cache_v': nrm(ks[5], (DEPTH, n_pool, PAGE_SIZE, KV_HEADS, HEAD_DIM), 1.0),
        'cache_idx_k': nrm(ks[6], (DEPTH, n_pool, PAGE_SIZE, IDX_DIM), 1.0),
        'cache_mem_k': nrm(ks[7], (DEPTH, DEC_BATCH, N_MEM, MEM_HEADS, HEAD_DIM), 1.0),
        'cache_mem_v': nrm(ks[8], (DEPTH, DEC_BATCH, N_MEM, MEM_HEADS, HEAD_DIM), 1.0),
        'state_pool': nrm(ks[9], (DEPTH, DEC_BATCH, POOL_STATE, POOL_WIDTH), 1.0),
        'state_conv': nrm(ks[10], (DEPTH, DEC_BATCH, CONV_WIDTH - 1, two_f), 1.0),
        'page_table': page_table,
        'rel_bias': nrm(ks[11], (NUM_BUCKETS, ATT_HEADS), 0.5),
        'norm1_g': 1.0 + nrm(ks[12], (DEPTH, D_MODEL), 0.02),
        'w_in': nrm(ks[13], (DEPTH, D_MODEL, N_IN), D_MODEL ** -0.5),
        'q_norm_g': 1.0 + nrm(ks[14], (DEPTH, HEAD_DIM), 0.02),
        'k_norm_g': 1.0 + nrm(ks[15], (DEPTH, HEAD_DIM), 0.02),
        'mem_norm_g': 1.0 + nrm(ks[16], (DEPTH, D_MODEL), 0.02),
        'w_mem_kv': nrm(ks[17], (DEPTH, D_MODEL, 2 * MEM_WIDTH), D_MODEL ** -0.5),
        'mq_norm_g': 1.0 + nrm(ks[18], (DEPTH, HEAD_DIM), 0.02),
        'mk_norm_g': 1.0 + nrm(ks[19], (DEPTH, HEAD_DIM), 0.02),
        'w_pool': nrm(ks[20], (DEPTH, N_POOL_GROUPS, POOL_GROUP_DIM, POOL_GROUP_DIM), POOL_GROUP_DIM ** -0.5),
        'pool_scale': 1.0 + nrm(ks[21], (DEPTH, POOL_WIDTH), 0.1),
        'w_out': nrm(ks[22], (DEPTH, MIX_WIDTH, D_MODEL), MIX_WIDTH ** -0.5),
        'norm2_g': 1.0 + nrm(ks[23], (DEPTH, D_MODEL), 0.02),
        'w_up': nrm(ks[24], (DEPTH, D_MODEL, two_f), D_MODEL ** -0.5),
        'conv_w': nrm(ks[25], (DEPTH, CONV_WIDTH, two_f), CONV_WIDTH ** -0.5),
        'conv_b': nrm(ks[26], (DEPTH, two_f), 0.02),
        'w_down': nrm(ks[27], (DEPTH, D_FF, D_MODEL), D_FF ** -0.5),
    }


def reference(x_prompt, x_sample, mem_prompt, cache_k, cache_v, cache_idx_k, cache_mem_k, cache_mem_v,
              state_pool, state_conv, page_table, rel_bias, norm1_g, w_in, q_norm_g, k_norm_g,
              mem_norm_g, w_mem_kv, mq_norm_g, mk_norm_g, w_pool, pool_scale, w_out, norm2_g,
              w_up, conv_w, conv_b, w_down):
    past = page_table.shape[1] * PAGE_SIZE
    xp, xs = x_prompt, x_sample
    kp_l, vp_l, ikp_l, poolp_l, convp_l, mkp_l, mvp_l = [], [], [], [], [], [], []
    ks_l, vs_l, iks_l, pools_l, convs_l = [], [], [], [], []
    for l in range(DEPTH):
        S = xp.shape[1]
        h = rmsnorm(xp, norm1_g[l])
        q, k, v, iq, ik, iw, u, mq = project_mixers(h, w_in[l], q_norm_g[l], k_norm_g[l], mq_norm_g[l])
        att = dsa_prompt(q, k, v, iq, ik, iw, rel_bias)
        pool = multi_pool(u, jnp.arange(S), w_pool[l], pool_scale[l])
        mk, mv = mem_kv(mem_prompt, mem_norm_g[l], w_mem_kv[l], mk_norm_g[l])
        memo = mem_attend(mq, mk, mv)
        xp = xp + jnp.concatenate([att, pool, memo], axis=-1) @ w_out[l]
        up = rmsnorm(xp, norm2_g[l]) @ w_up[l]
        up_ext = jnp.pad(up, ((0, 0), (CONV_WIDTH - 1, 0), (0, 0)))
        xp = xp + conv_ffn(up_ext, conv_w[l], conv_b[l], w_down[l])
        kp_l.append(k)
        vp_l.append(v)
        ikp_l.append(ik)
        poolp_l.append(u[:, S - POOL_STATE:])
        convp_l.append(up[:, S - (CONV_WIDTH - 1):])
        mkp_l.append(mk)
        mvp_l.append(mv)

        T = xs.shape[1]
        hs = rmsnorm(xs, norm1_g[l])
        qs, kn, vn, iqs, ikn, iws, us, mqs = project_mixers(hs, w_in[l], q_norm_g[l], k_norm_g[l], mq_norm_g[l])
        att_s = dsa_sample(qs, kn, vn, iqs, ikn, iws, cache_k[l], cache_v[l], cache_idx_k[l], page_table, rel_bias)
        u_ext = jnp.concatenate([state_pool[l].astype(us.dtype), us], axis=1)
        pos_ext = past - POOL_STATE + jnp.arange(POOL_STATE + T)
        pool_s = multi_pool(u_ext, pos_ext, w_pool[l], pool_scale[l])[:, POOL_STATE:]
        memo_s = mem_attend(mqs, cache_mem_k[l], cache_mem_v[l])
        xs = xs + jnp.concatenate([att_s, pool_s, memo_s], axis=-1) @ w_out[l]
        ups = rmsnorm(xs, norm2_g[l]) @ w_up[l]
        ups_ext = jnp.concatenate([state_conv[l].astype(ups.dtype), ups], axis=1)
        xs = xs + conv_ffn(ups_ext, conv_w[l], conv_b[l], w_down[l])
        ks_l.append(kn)
        vs_l.append(vn)
        iks_l.append(ikn)
        pools_l.append(u_ext[:, T:])
        convs_l.append(ups_ext[:, T:])

    st = lambda a: jnp.stack(a, axis=0)
    return (xp, xs, st(kp_l), st(vp_l), st(ikp_l), st(poolp_l), st(convp_l), st(mkp_l), st(mvp_l),
            st(ks_l), st(vs_l), st(iks_l), st(pools_l), st(convs_l))
```

```python
import math
import numpy as np
from contextlib import ExitStack
import concourse.bass as bass
import concourse.mybir as mybir
from concourse.bass_utils import run_bass_kernel_spmd

F32 = mybir.dt.float32
BF16 = mybir.dt.bfloat16
I32 = mybir.dt.int32
U32 = mybir.dt.uint32
AF = mybir.ActivationFunctionType
OP = mybir.AluOpType
AX = mybir.AxisListType

D = 1024
DFF = 2816
NIN = 1832
EPS = 1e-6
SHIFT = 8.0
NEG = -1.0e30
NIT = 16
ENGS = ["pe", "act", "dve", "pool", "sp"]
QPERM = [0, 2, 1, 3, 4, 6, 5, 7]
HCLAMP = 1664
HLEN = HCLAMP + 128
TABLEN = HLEN + 128


class Buf:
    def __init__(self, name, t=None):
        self.name = name
        self.t = t
        self.w = {}
        self.rs = {}

    def __getitem__(self, k):
        return self.t[k]


class Ring:
    def __init__(self, bufs):
        self.bufs = bufs
        self.i = 0

    def next(self):
        b = self.bufs[self.i % len(self.bufs)]
        self.i += 1
        return b


class Sched:
    def __init__(self, nc, es):
        self.nc = nc
        self.es_global = es
        self.es = es
        self.q = {e: [] for e in ENGS}
        self.cnt = {}
        self.sems = {}
        self.seen = {e: {} for e in ENGS}
        self.uid = 0
        for e in ENGS:
            self._sem("E_" + e)

    def _sem(self, key):
        if key not in self.sems:
            self.sems[key] = self.es_global.enter_context(self.nc.semaphore("s_" + key))
            self.cnt[key] = 0
        return self.sems[key]

    def sb(self, name, shape, dt=F32):
        self.uid += 1
        t = self.es.enter_context(self.nc.sbuf_tensor("%s_%d" % (name, self.uid), list(shape), dt))
        return Buf(name, t)

    def ps(self, name, shape, dt=F32):
        self.uid += 1
        t = self.es.enter_context(self.nc.psum_tensor("%s_%d" % (name, self.uid), list(shape), dt))
        return Buf(name, t)

    def _deps(self, eng, R, W):
        deps = {}

        def add(d):
            for k, v in d.items():
                if deps.get(k, 0) < v:
                    deps[k] = v
        for b in R:
            add(b.w)
        for b in W:
            add(b.w)
            add(b.rs)
        out = []
        for k, v in deps.items():
            if eng == "pe" and k == "E_pe":
                continue
            if self.seen[eng].get(k, 0) >= v:
                continue
            self.seen[eng][k] = v
            out.append((k, v))
        return out

    def _commit(self, tok, R, W):
        k, v = tok
        for b in R:
            if b.rs.get(k, 0) < v:
                b.rs[k] = v
        for b in W:
            if b.w.get(k, 0) < v:
                b.w[k] = v
            b.rs = {}

    def op(self, eng, fn, R=(), W=()):
        waits = self._deps(eng, R, W)
        key = "E_" + eng
        self.cnt[key] += 1
        tok = (key, self.cnt[key])
        self.q[eng].append((waits, fn, key, 1))
        self._commit(tok, R, W)

    def dma(self, eng, fn, R, W, key):
        key = "D_" + key
        self._sem(key)
        waits = self._deps(eng, R, W)
        self.cnt[key] += 16
        tok = (key, self.cnt[key])
        self.q[eng].append((waits, fn, key, 16))
        self._commit(tok, R, W)

    def barrier(self):
        for e in ENGS:
            waits = []
            for k, v in self.cnt.items():
                if v == 0:
                    continue
                if self.seen[e].get(k, 0) >= v:
                    continue
                self.seen[e][k] = v
                waits.append((k, v))
            self.q[e].append((waits, None, None, 0))

    def emit(self):
        nc = self.nc
        qs = self.q
        self.q = {e: [] for e in ENGS}
        sems = self.sems
        with nc.Block() as block:
            def run(e, items):
                for waits, fn, key, inc in items:
                    for k, v in waits:
                        e.wait_ge(sems[k], v)
                    if fn is not None:
                        ins = fn(e)
                        ins.then_inc(sems[key], inc)

            @block.tensor
            def _(e):
                run(e, qs["pe"])

            @block.scalar
            def _(e):
                run(e, qs["act"])

            @block.vector
            def _(e):
                run(e, qs["dve"])

            @block.gpsimd
            def _(e):
                run(e, qs["pool"])

            @block.sync
            def _(e):
                run(e, qs["sp"])


def t5_bucket_np(d):
    d = np.asarray(d, dtype=np.int64)
    dd = np.maximum(d, 1).astype(np.float32)
    large = 16 + (np.log(dd / np.float32(16)) / np.float32(math.log(2048 / 16)) * np.float32(16)).astype(np.int32)
    large = np.minimum(large, 31)
    return np.where(d < 16, d, large)


class K:
    def __init__(self, S, NPG, NPOOL, NS=4, with_sample_dsa=True):
        self.S, self.NPG, self.NPOOL, self.NS = S, NPG, NPOOL, NS
        self.NT = S // 128
        self.KTOP = min(256, S // 4)
        self.L_s = NPG * 128 + 1
        self.KTOP_S = min(256, self.L_s // 4)
        self.with_sample_dsa = with_sample_dsa
        import os
        self.dbg = int(os.environ.get('KDBG', '99'))

    def mm(self, ob, oap, lb, lap, rb, rap, st, sp):
        self.s.op("pe", lambda e: e.matmul(oap, lhsT=lap, rhs=rap, start=st, stop=sp), R=[lb, rb], W=[ob])

    def tr(self, ob, oap, ib, iap, idb, idap):
        self.s.op("pe", lambda e: e.transpose(out=oap, in_=iap, identity=idap), R=[ib, idb], W=[ob])

    def act(self, ob, oap, ib, iap, func, bias=None, scale=1.0, R=(), accum=None, W=()):
        def fn(e):
            kw = {}
            if bias is not None:
                kw["bias"] = bias
            if accum is not None:
                kw["accum_out"] = accum
            return e.activation(out=oap, in_=iap, func=func, scale=scale, **kw)
        self.s.op("act", fn, R=[ib] + list(R), W=[ob] + list(W))

    def acopy(self, ob, oap, ib, iap):
        self.s.op("act", lambda e: e.copy(out=oap, in_=iap), R=[ib], W=[ob])

    def vcopy(self, ob, oap, ib, iap, eng="dve"):
        self.s.op(eng, lambda e: e.tensor_copy(out=oap, in_=iap), R=[ib], W=[ob])

    def tt(self, ob, oap, ab, aap, bb, bap, op, eng="dve"):
        self.s.op(eng, lambda e: e.tensor_tensor(out=oap, in0=aap, in1=bap, op=op), R=[ab, bb], W=[ob])

    def ts(self, ob, oap, ib, iap, s1, s2, op0, op1=None, R=(), accum=None, W=(), eng="dve"):
        def fn(e):
            kw = {}
            if op1 is not None:
                kw["op1"] = op1
            if accum is not None:
                kw["accum_out"] = accum
            return e.tensor_scalar(out=oap, in0=iap, scalar1=s1, scalar2=s2, op0=op0, **kw)
        self.s.op(eng, fn, R=[ib] + list(R), W=[ob] + list(W))

    def stt(self, ob, oap, ab, aap, sc, bb, bap, op0, op1, R=()):
        self.s.op("dve", lambda e: e.scalar_tensor_tensor(out=oap, in0=aap, scalar=sc, in1=bap, op0=op0, op1=op1),
                  R=[ab, bb] + list(R), W=[ob])

    def memset(self, b, ap, val, eng="pool"):
        self.s.op(eng, lambda e: e.memset(ap, val), R=[], W=[b])

    def dma(self, q, ob, oap, ib, iap, key=None, R=(), W=()):
        self.s.dma(q, lambda e: e.dma_start(out=oap, in_=iap), R=[ib] + list(R), W=[ob] + list(W), key=key or ob.name)

    def build(self):
        S, NT, NS, NPG = self.S, self.NT, self.NS, self.NPG
        nc = bass.Bass("TRN2", target_bir_lowering=False)
        self.nc = nc
        NROWS = self.NPOOL * 128

        def din(name, shape, dt=F32):
            return Buf(name, nc.dram_tensor(name, list(shape), dt, kind="ExternalInput").ap())

        def dout(name, shape, dt=F32):
            return Buf(name, nc.dram_tensor(name, list(shape), dt, kind="ExternalOutput").ap())

        def dscr(name, shape, dt=F32):
            return Buf(name, nc.dram_tensor(name, list(shape), dt, kind="Internal").ap())

        I = {}
        I["xp"] = din("xp", [S, D])
        I["xs"] = din("xs", [NS, D])
        I["memp"] = din("memp", [256, D])
        I["ck"] = din("ck", [2 * NROWS, 256])
        I["cv"] = din("cv", [2 * NROWS, 256])
        I["cik"] = din("cik", [2 * NROWS, 32])
        I["cmk"] = din("cmk", [2, NS, 256, 256])
        I["cmv"] = din("cmv", [2, NS, 256, 256])
        I["spool"] = din("spool", [2, NS, 15, 256])
        I["sconv"] = din("sconv", [2, NS, 2, 2 * DFF])
        I["ptab"] = din("ptab", [NS, NPG], I32)
        I["rel_bias"] = din("rel_bias", [32, 8])
        I["norm1_g"] = din("norm1_g", [2, D])
        I["w_in"] = din("w_in", [2, D, NIN])
        I["q_norm_g"] = din("q_norm_g", [2, 64])
        I["k_norm_g"] = din("k_norm_g", [2, 64])
        I["mem_norm_g"] = din("mem_norm_g", [2, D])
        I["w_mem_kv"] = din("w_mem_kv", [2, D, 512])
        I["mq_norm_g"] = din("mq_norm_g", [2, 64])
        I["mk_norm_g"] = din("mk_norm_g", [2, 64])
        I["w_pool"] = din("w_pool", [2, 4, 64, 64])
        I["pool_scale"] = din("pool_scale", [2, 256])
        I["w_out"] = din("w_out", [2, D, D])
        I["norm2_g"] = din("norm2_g", [2, D])
        I["w_up"] = din("w_up", [2, D, 2 * DFF])
        I["conv_w"] = din("conv_w", [2, 3, 2 * DFF])
        I["conv_b"] = din("conv_b", [2, 2 * DFF])
        I["w_down"] = din("w_down", [2, DFF, D])
        I["c_oh"] = din("c_oh", [32, TABLEN])
        I["c_cinv"] = din("c_cinv", [128, 2, 128])
        I["c_winv"] = din("c_winv", [128, 2])
        I["c_pow2"] = din("c_pow2", [128, NIT + 1])
        I["c_thr"] = din("c_thr", [128, 31])
        I["c_iota"] = din("c_iota", [128, 128])
        I["c_smp"] = din("c_smp", [128, 167])
        self.I = I
        O = {}
        O["y_p"] = dout("y_p", [S, D])
        O["y_s"] = dout("y_s", [NS, D])
        O["k_p"] = dout("k_p", [2, S, 256])
        O["v_p"] = dout("v_p", [2, S, 256])
        O["ik_p"] = dout("ik_p", [2, S, 32])
        O["pool_p"] = dout("pool_p", [2, 15, 256])
        O["conv_p"] = dout("conv_p", [2, 2, 2 * DFF])
        O["mk_p"] = dout("mk_p", [2, 256, 256])
        O["mv_p"] = dout("mv_p", [2, 256, 256])
        O["k_s"] = dout("k_s", [2, NS, 256])
        O["v_s"] = dout("v_s", [2, NS, 256])
        O["ik_s"] = dout("ik_s", [2, NS, 32])
        O["pool_s"] = dout("pool_s", [2, NS, 15, 256])
        O["conv_s"] = dout("conv_s", [2, NS, 2, 2 * DFF])
        self.O = O
        self.XA = dscr("XA", [D, S])
        self.XB = dscr("XB", [D, S])
        self.TAB = dscr("TAB", [8, TABLEN])

        with ExitStack() as esg:
            s = Sched(nc, esg)
            self.s = s
            self.setup_consts()
            for l in range(2):
                if self.dbg <= 1:
                    break
                with ExitStack() as es1:
                    s.es = es1
                    self.phase1(l)
                    s.barrier()
                    s.emit()
                if self.dbg <= 8:
                    break
                with ExitStack() as es2:
                    s.es = es2
                    self.phase2(l)
                    s.barrier()
                    s.emit()
                s.es = esg
        return nc

    def setup_consts(self):
        s, I = self.s, self.I
        c = {}
        self.c = c
        c["ident"] = s.sb("ident", [128, 128], F32)
        c["identb"] = s.sb("identb", [128, 128], BF16)
        c["J"] = s.sb("J", [128, 128], BF16)
        c["ones"] = s.sb("ones", [128, 128], F32)
        c["onesb"] = s.sb("onesb", [128, 128], BF16)
        c["bones"] = s.sb("bones", [128, 128], F32)
        c["cm"] = s.sb("cm", [128, 128], F32)
        c["eps"] = s.sb("eps", [128, 1], F32)
        c["nshift"] = s.sb("nshift", [128, 1], F32)
        c["pow2"] = s.sb("pow2", [128, NIT + 1], F32)
        c["cinv"] = s.sb("cinv", [128, 2, 128], F32)
        c["winv"] = s.sb("winv", [128, 2], F32)
        c["xsT"] = s.sb("xsT", [128, 8, self.NS], F32)
        est = ExitStack()
        s.es = est
        ident, J = c["ident"], c["J"]
        self.memset(ident, ident[:], 0.0)
        s.op("pool", lambda e: e.affine_select(out=ident[:], in_=ident[:], pattern=[[-1, 128]], compare_op=OP.not_equal,
                                               fill=1.0, base=0, channel_multiplier=1), R=[ident], W=[ident])
        self.vcopy(c["identb"], c["identb"][:], ident, ident[:], eng="pool")
        jf = s.sb("jf", [128, 128], F32)
        self.memset(jf, jf[:], 0.0)
        s.op("pool", lambda e: e.affine_select(out=jf[:], in_=jf[:], pattern=[[1, 128]], compare_op=OP.not_equal,
                                               fill=1.0, base=-127, channel_multiplier=1), R=[jf], W=[jf])
        self.vcopy(J, J[:], jf, jf[:], eng="pool")
        self.memset(c["ones"], c["ones"][:], 1.0)
        self.memset(c["onesb"], c["onesb"][:], 1.0)
        bo = c["bones"]
        self.memset(bo, bo[:], 0.0)
        self.memset(bo, bo[0:64, 0:64], 1.0)
        self.memset(bo, bo[64:128, 64:128], 1.0)
        cm = c["cm"]
        self.memset(cm, cm[:], 0.0)
        s.op("pool", lambda e: e.affine_select(out=cm[:], in_=cm[:], pattern=[[-1, 128]], compare_op=OP.is_ge,
                                               fill=NEG, base=0, channel_multiplier=1), R=[cm], W=[cm])
        self.memset(c["eps"], c["eps"][:], EPS)
        self.memset(c["nshift"], c["nshift"][:], -SHIFT)
        self.dma("sp", c["pow2"], c["pow2"][:], I["c_pow2"], I["c_pow2"][:, :])
        self.dma("sp", c["cinv"], c["cinv"][:], I["c_cinv"], I["c_cinv"][:, :, :])
        self.dma("sp", c["winv"], c["winv"][:], I["c_winv"], I["c_winv"][:, :])
        rb = s.sb("rb", [32, 8], F32)
        oh = s.sb("oh", [32, TABLEN], F32)
        tabs = s.sb("tabs", [8, TABLEN], F32)
        self.dma("sp", rb, rb[:], I["rel_bias"], I["rel_bias"][:, :])
        self.dma("sp", oh, oh[:], I["c_oh"], I["c_oh"][:, :])
        pt = s.ps("ptab", [128, 512], F32)
        for c0 in range(0, TABLEN, 512):
            n = min(512, TABLEN - c0)
            self.mm(pt, pt[0:8, 0:n], rb, rb[:], oh, oh[:, c0:c0 + n], True, True)
            self.vcopy(tabs, tabs[:, c0:c0 + n], pt, pt[0:8, 0:n])
        self.dma("sp", self.TAB, self.TAB[:, :], tabs, tabs[:], key="tabst")
        xs_tok = s.sb("xs_tok", [self.NS, D], F32)
        self.dma("sp", xs_tok, xs_tok[:], I["xs"], I["xs"][:, :])
        for k in range(8):
            self.tr(pt, pt[:, k * 4:k * 4 + self.NS], xs_tok, xs_tok[:, k * 128:(k + 1) * 128], ident, ident[0:self.NS, 0:self.NS])
        self.vcopy(c["xsT"], c["xsT"][:].rearrange("p k s -> p (k s)"), pt, pt[:, 0:8 * self.NS])
        s.barrier()
        s.emit()
        est.close()
        s.es = s.es_global

    def rms_feature_major(self, xb, xap3, n, gvec, hT, P_N, tmp_sq, tmp_rs):
        c = self.c
        sq, rs = tmp_sq, tmp_rs
        self.act(sq, sq[:, 0:8, 0:n], xb, xap3, AF.Square)
        for k in range(8):
            self.mm(P_N, P_N[:, 0:n], c["ones"], c["ones"][:], sq, sq[:, k, 0:n], k == 0, k == 7)
        self.act(rs, rs[:, 0:n], P_N, P_N[:, 0:n], AF.Sqrt, bias=c["eps"][:, 0:1], scale=1.0 / D, R=[c["eps"]])
        self.s.op("dve", lambda e: e.reciprocal(out=rs[:, 0:n], in_=rs[:, 0:n]), R=[rs], W=[rs])
        for k in range(8):
            self.stt(hT, hT[:, k, 0:n], xb, xap3[:, k, :], gvec[:, k:k + 1], rs, rs[:, 0:n], OP.mult, OP.mult, R=[gvec])

    def headnorm(self, P_Z, zap, ncols, P_N, sq, rs):
        c = self.c
        self.act(sq, sq[:, 0:ncols], P_Z, zap, AF.Square)
        self.mm(P_N, P_N[:, 0:ncols], c["bones"], c["bones"][:], sq, sq[:, 0:ncols], True, True)
        self.act(rs, rs[:, 0:ncols], P_N, P_N[:, 0:ncols], AF.Sqrt, bias=c["eps"][:, 0:1], scale=1.0 / 64, R=[c["eps"]])
        self.s.op("dve", lambda e: e.reciprocal(out=rs[:, 0:ncols], in_=rs[:, 0:ncols]), R=[rs], W=[rs])
        return rs[:, 0:ncols]

    def load_gvec(self, name, src, l, scale=None):
        g = self.s.sb(name, [128, 8], F32)
        ap = bass.AP(src.t.tensor, l * D, [[1, 128], [128, 8]])
        self.s.dma("sp", lambda e: e.dma_start(out=g[:], in_=ap, allow_slow_non_contiguous=True), R=[src], W=[g], key=name)
        return g

    def load_hvec(self, name, src, l, scale=1.0):
        g = self.s.sb(name, [128, 1], F32)
        for h in range(2):
            ap = bass.AP(src.t.tensor, l * 64, [[1, 64], [1, 1]])
            self.s.dma("sp", lambda e, ap=ap, h=h: e.dma_start(out=g[h * 64:(h + 1) * 64, :], in_=ap), R=[src], W=[g], key=name + str(h))
        if scale != 1.0:
            self.ts(g, g[:], g, g[:], scale, None, OP.mult)
        return g

    def phase1(self, l):
        s, I, O, c = self.s, self.I, self.O, self.c
        S, NT, NS = self.S, self.NT, self.NS
        W = {}
        wsrc = I["w_in"]
        win3 = wsrc[l].rearrange("(k p) c -> p k c", p=128)
        wf = s.sb("wf", [128, 8, 14 * 128], BF16)
        W["wf"] = wf
        for pos, h in enumerate(QPERM):
            self.dma("pool", wf, wf[:, :, pos * 64:(pos + 1) * 64], wsrc, win3[:, :, h * 64:(h + 1) * 64], key=None)
        self.dma("pool", wf, wf[:, :, 512:768], wsrc, win3[:, :, 512:768], key=None)
        self.dma("pool", wf, wf[:, :, 768:1024], wsrc, win3[:, :, 1576:1832], key=None)
        self.memset(wf, wf[:, :, 1024:1408], 0.0, eng="dve")
        for h in range(8):
            d0 = 1024 + (h // 3) * 128 + (h % 3) * 32
            self.dma("pool", wf, wf[:, :, d0:d0 + 32], wsrc, win3[:, :, 1024 + h * 32:1024 + (h + 1) * 32], key=None)
        for r in range(4):
            self.dma("pool", wf, wf[:, :, 1408 + r * 32:1408 + (r + 1) * 32], wsrc, win3[:, :, 1280:1312], key=None)
        self.dma("pool", wf, wf[:, :, 1536:1792], wsrc, win3[:, :, 1320:1576], key=None)
        wt = s.sb("wt", [128, 8, 296], BF16)
        W["wt"] = wt
        self.dma("pool", wt, wt[:, :, 0:256], wsrc, win3[:, :, 768:1024], key=None)
        self.dma("pool", wt, wt[:, :, 256:296], wsrc, win3[:, :, 1280:1320], key=None)
        wo = s.sb("wo", [128, 8, D], BF16)
        wm3 = I["w_mem_kv"][l].rearrange("(k p) c -> p k c", p=128)
        self.dma("pool", wo, wo[:, :, 0:512], I["w_mem_kv"], wm3, key=None)
        g1 = self.load_gvec("g1", I["norm1_g"], l)
        gm = self.load_gvec("gm", I["mem_norm_g"], l)
        gq = self.load_hvec("gq", I["q_norm_g"], l, scale=0.125)
        gk = self.load_hvec("gk", I["k_norm_g"], l)
        gmq = self.load_hvec("gmq", I["mq_norm_g"], l, scale=0.125)
        gmk = self.load_hvec("gmk", I["mk_norm_g"], l)
        psc = s.sb("psc", [128, 2], F32)
        self.s.dma("sp", lambda e: e.dma_start(out=psc[:], in_=bass.AP(I["pool_scale"].t.tensor, l * 256, [[1, 128], [128, 2]]),
                                               allow_slow_non_contiguous=True), R=[I["pool_scale"]], W=[psc], key="psc")
        wpb = s.sb("wpb", [128, 2, 128], F32)
        self.memset(wpb, wpb[:], 0.0)
        for g in range(4):
            cc, hh = g // 2, g % 2
            self.dma("sp", wpb, wpb[hh * 64:(hh + 1) * 64, cc, hh * 64:(hh + 1) * 64], I["w_pool"], I["w_pool"][l, g, :, :], key="wpb%d" % g)
        P = [s.ps("P%d" % i, [128, 512], F32) for i in range(8)]
        self.P = P
        mkT = s.sb("mkT", [128, 2, 256], BF16)
        mvb = s.sb("mvb", [128, 2, 256], BF16)
        T = {}
        T["xT"] = s.sb("xT", [128, 8, 128], F32)
        T["rs1"] = s.sb("rs1", [128, 128], F32)
        T["hT"] = s.sb("hT", [128, 8, 128], BF16)
        T["sq"] = s.sb("sq", [128, 512], F32)
        T["rs"] = s.sb("rs", [128, 512], F32)
        T["qT"] = s.sb("qT", [128, 4, 128], BF16)
        T["mqT"] = s.sb("mqT", [128, 2, 128], BF16)
        T["mqm"] = s.sb("mqm", [128, 2, 2, 128], BF16)
        self.memset(T["mqm"], T["mqm"][:], 0.0)
        T["iqT"] = s.sb("iqT", [128, 3, 128], BF16)
        T["kn32"] = s.sb("kn32", [128, 2, 128], F32)
        T["ktok"] = s.sb("ktok", [128, 256], F32)
        T["tm"] = s.sb("tm", [128, 296], F32)
        T["iw16"] = s.sb("iw16", [128, 8], F32)
        T["mixT"] = s.sb("mixT", [128, 8, 128], BF16)
        T["x1"] = s.sb("x1", [128, 8, 128], F32)
        T["r"] = Ring([s.sb("r%d" % i, [128, 512], F32) for i in range(2)])
        T["pb"] = Ring([s.sb("pb%d" % i, [128, 512], BF16) for i in range(2)])
        T["pm"] = Ring([s.sb("pm%d" % i, [128, 512], BF16) for i in range(2)])
        T["rden"] = s.sb("rden", [128, 512], F32)
        T["bs"] = s.sb("bs", [128, 8], F32)
        T["dd"] = s.sb("dd", [128, NIT + 1], F32)
        T["pl"] = [s.sb("pl%d" % i, [128, 2, 143], F32) for i in range(2)]
        T["pooled"] = s.sb("pooled", [128, 2, 128], F32)
        self.T = T
        self.Wt = W
        esp = ExitStack()
        es_phase = s.es
        s.es = esp
        H = s.sb("H", [128, 8, HLEN], BF16)
        c["H"] = H
        for h in range(8):
            hsrc = bass.AP(self.TAB.t.tensor, h * TABLEN, [[1, 128], [1, HLEN]])
            self.dma("pool", H, H[:, h, :], self.TAB, hsrc, key="Hld")
        kT = s.sb("kT", [128, 2, S], BF16)
        vc = s.sb("vc", [128, NT, 256], BF16)
        ikT = s.sb("ikT", [128, S], BF16)
        Sc = s.sb("Sc", [128, S], F32)
        mb = s.sb("mb", [128, S], BF16)
        maskT = s.sb("maskT", [128, NT, 128], BF16)
        ubuf = s.sb("ubuf", [128, 2, 143], F32)
        self.memset(ubuf, ubuf[:], 0.0)
        T["xtok"] = s.sb("xtok", [128, D], F32)
        self.lay = dict(l=l, g1=g1, gq=gq, gk=gk, gmq=gmq, gmk=gmk, psc=psc, wpb=wpb, wo=wo, kT=kT, vc=vc, ikT=ikT, Sc=Sc,
                        mb=mb, maskT=maskT, mkT=mkT, mvb=mvb, ubuf=ubuf)

        self.mem_kv(l, gm, gmk)
        wo_src = I["w_out"]
        for j in range(4):
            for hh in range(2):
                h = QPERM[2 * j + hh]
                self.dma("pool", wo, wo[hh * 64:(hh + 1) * 64, j, :], wo_src, wo_src[l, h * 64:(h + 1) * 64, :], key=None)
        self.dma("pool", wo, wo[:, 4:8, :], wo_src, wo_src[l, 512:1024, :].rearrange("(k p) c -> p k c", p=128), key=None)

        for t in range(NT):
            self.prompt_tile(l, t)
        s.barrier()
        s.emit()
        esp.close()
        s.es = es_phase
        for kdead in ("kT", "vc", "ikT", "Sc", "mb", "maskT", "ubuf"):
            self.lay.pop(kdead)
        T.pop("xtok")
        if self.dbg <= 8:
            return
        self.sample_tile(l)

    def mem_kv(self, l, gm, gmk):
        s, I, O, c, T, P = self.s, self.I, self.O, self.c, self.T, self.P
        lay = self.lay
        wo, mkT, mvb = lay["wo"], lay["mkT"], lay["mvb"]
        xtok, xT = T["xtok"], T["xT"]
        for i in range(2):
            self.dma("sp", xtok, xtok[:], I["memp"], I["memp"][i * 128:(i + 1) * 128, :])
            for k0 in (0, 4):
                for k in range(k0, k0 + 4):
                    self.tr(P[7], P[7][:, (k - k0) * 128:(k - k0 + 1) * 128], xtok, xtok[:, k * 128:(k + 1) * 128], c["ident"], c["ident"][:])
                self.acopy(xT, xT[:, k0:k0 + 4, :], P[7], P[7][:, 0:512].rearrange("p (k t) -> p k t", k=4))
            self.rms_feature_major(xT, xT[:], 128, gm, T["hT"], P[2], T["x1"], T["rs1"])
            hT = T["hT"]
            for g in range(4):
                for k in range(8):
                    self.mm(P[0], P[0][:, g * 128:(g + 1) * 128], wo, wo[:, k, g * 128:(g + 1) * 128], hT, hT[:, k, :], k == 0, k == 7)
            rs = self.headnorm(P[0], P[0][:, 0:256], 256, P[2], T["sq"], T["rs"])
            kn = T["kn32"]
            self.stt(kn, kn[:].rearrange("p c t -> p (c t)"), P[0], P[0][:, 0:256], gmk[:, 0:1], T["rs"], rs, OP.mult, OP.mult, R=[gmk])
            self.acopy(mkT, mkT[:, :, i * 128:(i + 1) * 128], kn, kn[:])
            for cc in range(2):
                self.tr(P[7], P[7][:, cc * 128:(cc + 1) * 128], kn, kn[:, cc, :], c["ident"], c["ident"][:])
            self.vcopy(T["ktok"], T["ktok"][:], P[7], P[7][:, 0:256])
            self.dma("sp", O["mk_p"], O["mk_p"][l, i * 128:(i + 1) * 128, :], T["ktok"], T["ktok"][:], key="ktok")
            vn = T["x1"]
            self.acopy(vn, vn[:, 0:2, :].rearrange("p c t -> p (c t)"), P[0], P[0][:, 256:512])
            for cc in range(2):
                self.tr(P[7], P[7][:, 256 + cc * 128:256 + (cc + 1) * 128], vn, vn[:, cc, :], c["ident"], c["ident"][:])
            self.vcopy(T["tm"], T["tm"][:, 0:256], P[7], P[7][:, 256:512])
            self.dma("sp", O["mv_p"], O["mv_p"][l, i * 128:(i + 1) * 128, :], T["tm"], T["tm"][:, 0:256], key="tm")
            self.acopy(mvb, mvb[:, i, :], T["tm"], T["tm"][:, 0:256])

    def project(self, l, xb, xap3, n, P):
        s, c, T, lay, Wt = self.s, self.c, self.T, self.lay, self.Wt
        wf, wt = Wt["wf"], Wt["wt"]
        hT = T["hT"]
        self.rms_feature_major(xb, xap3, n, lay["g1"], hT, P[2], T["x1"], T["rs1"])

        def fm(Pb, slot, grp):
            for k in range(8):
                self.mm(Pb, Pb[:, slot * 128:slot * 128 + n], wf, wf[:, k, grp * 128:(grp + 1) * 128], hT, hT[:, k, 0:n], k == 0, k == 7)
        for j in range(4):
            fm(P[0], j, j)
        for j in range(4):
            fm(P[1], j, 4 + j)
        qT, mqT, kn = T["qT"], T["mqT"], T["kn32"]
        if n == 128:
            rs = self.headnorm(P[0], P[0][:, 0:512], 512, P[2], T["sq"], T["rs"])
            self.stt(qT, qT[:].rearrange("p c t -> p (c t)"), P[0], P[0][:, 0:512], lay["gq"][:, 0:1], T["rs"], rs, OP.mult, OP.mult, R=[lay["gq"]])
            rs = self.headnorm(P[1], P[1][:, 0:512], 512, P[2], T["sq"], T["rs"])
            self.stt(kn, kn[:].rearrange("p c t -> p (c t)"), P[1], P[1][:, 0:256], lay["gk"][:, 0:1], T["rs"], rs[:, 0:256], OP.mult, OP.mult, R=[lay["gk"]])
            self.stt(mqT, mqT[:].rearrange("p c t -> p (c t)"), P[1], P[1][:, 256:512], lay["gmq"][:, 0:1], T["rs"], rs[:, 256:512], OP.mult, OP.mult, R=[lay["gmq"]])
        else:
            for (Pb, nslot) in ((P[0], 4), (P[1], 4)):
                pass
            sq, rsb = T["sq"], T["rs"]
            for Pb in (P[0], P[1]):
                for j in range(4):
                    self.act(sq, sq[:, j * 128:j * 128 + n], Pb, Pb[:, j * 128:j * 128 + n], AF.Square)
                    self.mm(P[2], P[2][:, j * 128:j * 128 + n], c["bones"], c["bones"][:], sq, sq[:, j * 128:j * 128 + n], True, True)
                for j in range(4):
                    js = slice(j * 128, j * 128 + n)
                    self.act(rsb, rsb[:, js], P[2], P[2][:, js], AF.Sqrt, bias=c["eps"][:, 0:1], scale=1.0 / 64, R=[c["eps"]])
                    self.s.op("dve", lambda e, js=js: e.reciprocal(out=rsb[:, js], in_=rsb[:, js]), R=[rsb], W=[rsb])
                if Pb is P[0]:
                    for j in range(4):
                        self.stt(qT, qT[:, j, 0:n], Pb, Pb[:, j * 128:j * 128 + n], lay["gq"][:, 0:1], rsb, rsb[:, j * 128:j * 128 + n], OP.mult, OP.mult, R=[lay["gq"]])
                else:
                    for j in range(2):
                        self.stt(kn, kn[:, j, 0:n], Pb, Pb[:, j * 128:j * 128 + n], lay["gk"][:, 0:1], rsb, rsb[:, j * 128:j * 128 + n], OP.mult, OP.mult, R=[lay["gk"]])
                    for j in range(2):
                        self.stt(mqT, mqT[:, j, 0:n], Pb, Pb[:, (2 + j) * 128:(2 + j) * 128 + n], lay["gmq"][:, 0:1], rsb, rsb[:, (2 + j) * 128:(2 + j) * 128 + n], OP.mult, OP.mult, R=[lay["gmq"]])
        mqm = T["mqm"]
        self.vcopy(mqm, mqm[0:64, 0, :, 0:n], mqT, mqT[0:64, :, 0:n], eng="pool")
        self.vcopy(mqm, mqm[64:128, 1, :, 0:n], mqT, mqT[64:128, :, 0:n], eng="pool")
        for j in range(4):
            fm(P[0], j, 8 + j)
        for j in range(2):
            fm(P[1], j, 12 + j)
        iqT = T["iqT"]
        for j in range(3):
            self.acopy(iqT, iqT[:, j, 0:n], P[0], P[0][:, j * 128:j * 128 + n])
        for k in range(8):
            self.mm(P[7], P[7][0:n, 0:296], hT, hT[:, k, 0:n], wt, wt[:, k, :], k == 0, k == 7)
        tm = T["tm"]
        self.vcopy(tm, tm[0:n, :], P[7], P[7][0:n, 0:296])
        self.ts(T["iw16"], T["iw16"][0:n, :], tm, tm[0:n, 288:296], 1.0 / 16.0, None, OP.mult)

    def prompt_tile(self, l, t):
        s, I, O, c, T, P, lay = self.s, self.I, self.O, self.c, self.T, self.P, self.lay
        S, NT = self.S, self.NT
        cols = slice(t * 128, (t + 1) * 128)
        xT = T["xT"]
        if l == 0:
            xtok = T["xtok"]
            self.dma("sp", xtok, xtok[:], I["xp"], I["xp"][cols, :])
            for k0 in (0, 4):
                for k in range(k0, k0 + 4):
                    self.tr(P[7], P[7][:, (k - k0) * 128:(k - k0 + 1) * 128], xtok, xtok[:, k * 128:(k + 1) * 128], c["ident"], c["ident"][:])
                self.acopy(xT, xT[:, k0:k0 + 4, :], P[7], P[7][:, 0:512].rearrange("p (k t) -> p k t", k=4))
        else:
            self.dma("sp", xT, xT[:], self.XB, self.XB[:, cols].rearrange("(k p) t -> p k t", p=128))
        if self.dbg <= 4:
            return
        self.project(l, xT, xT[:], 128, P)
        if self.dbg <= 5:
            return
        kT, vc, ikT = lay["kT"], lay["vc"], lay["ikT"]
        kn, tm = T["kn32"], T["tm"]
        self.acopy(kT, kT[:, :, cols], kn, kn[:])
        self.acopy(ikT, ikT[:, cols], P[0], P[0][:, 384:512])
        self.acopy(vc, vc[:, t, :], tm, tm[:, 0:256])
        for cc in range(2):
            self.tr(P[7], P[7][:, cc * 128:(cc + 1) * 128], kn, kn[:, cc, :], c["ident"], c["ident"][:])
        self.vcopy(T["ktok"], T["ktok"][:], P[7], P[7][:, 0:256])
        self.dma("sp", O["k_p"], O["k_p"][l, cols, :], T["ktok"], T["ktok"][:], key="ktok")
        self.dma("sp", O["v_p"], O["v_p"][l, cols, :], tm, tm[:, 0:256], key="tm")
        self.dma("sp", O["ik_p"], O["ik_p"][l, cols, :], tm, tm[:, 256:288], key="tm2")
        ubuf = lay["ubuf"]
        self.acopy(ubuf, ubuf[:, :, 15:143], P[1], P[1][:, 0:256].rearrange("p (c t) -> p c t", c=2))
        if t == NT - 1:
            for cc in range(2):
                self.tr(P[7], P[7][:, 256 + cc * 128:256 + (cc + 1) * 128], ubuf, ubuf[:, cc, 15:143], c["ident"], c["ident"][:])
            self.vcopy(T["xtok"], T["xtok"][:, 0:256], P[7], P[7][:, 256:512])
            self.dma("sp", O["pool_p"], O["pool_p"][l, :, :], T["xtok"], T["xtok"][113:128, 0:256], key="xtok_o")
        if self.dbg <= 6:
            return
        self.dsa_prompt(l, t)
        if self.dbg <= 7:
            return
        self.pool_mix(t == 0, 128, ubuf)
        if self.dbg == 72:
            return
        self.mem_attend(128)
        if self.dbg == 73:
            return
        self.vcopy(T["pl"][0], T["pl"][0][:, :, 0:15], ubuf, ubuf[:, :, 128:143])
        self.vcopy(ubuf, ubuf[:, :, 0:15], T["pl"][0], T["pl"][0][:, :, 0:15])
        self.out_proj(xT, xT[:], 128)
        if self.dbg == 74:
            return
        self.dma("sp", self.XA, self.XA[:, cols].rearrange("(k p) t -> p k t", p=128), T["x1"], T["x1"][:], key="x1st")

    def out_proj(self, xb, xap3, n):
        T, P, lay = self.T, self.P, self.lay
        wo, mixT, x1 = lay["wo"], T["mixT"], T["x1"]
        for half in range(2):
            Pb = P[7] if half == 0 else P[2]
            for j in range(4):
                dm = half * 4 + j
                for k in range(8):
                    self.mm(Pb, Pb[:, j * 128:j * 128 + n], wo, wo[:, k, dm * 128:(dm + 1) * 128], mixT, mixT[:, k, 0:n], k == 0, k == 7)
            if n == 128:
                self.tt(x1, x1[:, half * 4:half * 4 + 4, :].rearrange("p c t -> p (c t)"), Pb, Pb[:, 0:512],
                        xb, xap3[:, half * 4:half * 4 + 4, :].rearrange("p c t -> p (c t)"), OP.add)
            else:
                for j in range(4):
                    dm = half * 4 + j
                    self.tt(x1, x1[:, dm, 0:n], Pb, Pb[:, j * 128:j * 128 + n], xb, xap3[:, dm, :], OP.add)

    def dsa_prompt(self, l, t):
        s, c, T, P, lay = self.s, self.c, self.T, self.P, self.lay
        Sc, mb, maskT, ikT, kT, vc = lay["Sc"], lay["mb"], lay["maskT"], lay["ikT"], lay["kT"], lay["vc"]
        iqT, iw, qT, bs, dd = T["iqT"], T["iw16"], T["qT"], T["bs"], T["dd"]
        Wd = (t + 1) * 128
        chunks = [(c0, min(512, Wd - c0)) for c0 in range(0, Wd, 512)]
        xi = 0
        for h in range(8):
            pb = (h % 3) * 32
            for (c0, n) in chunks:
                Px = P[xi % 2]
                xi += 1
                self.mm(Px, Px[:, 0:n], iqT, iqT[pb:pb + 32, h // 3, :], ikT, ikT[pb:pb + 32, c0:c0 + n], True, True)
                r = T["r"].next()
                self.act(r, r[:, 0:n], Px, Px[:, 0:n], AF.Relu)
                if h == 0:
                    self.ts(Sc, Sc[:, c0:c0 + n], r, r[:, 0:n], iw[:, 0:1], None, OP.mult, R=[iw])
                else:
                    self.stt(Sc, Sc[:, c0:c0 + n], r, r[:, 0:n], iw[:, h:h + 1], Sc, Sc[:, c0:c0 + n], OP.mult, OP.add, R=[iw])
        s.op("dve", lambda e: e.tensor_reduce(out=bs[:, 0:1], in_=Sc[:, 0:Wd], axis=AX.X, op=OP.min), R=[Sc], W=[bs])
        s.op("dve", lambda e: e.tensor_reduce(out=bs[:, 1:2], in_=Sc[:, 0:Wd], axis=AX.X, op=OP.max), R=[Sc], W=[bs])
        self.tt(Sc, Sc[:, t * 128:Wd], Sc, Sc[:, t * 128:Wd], c["cm"], c["cm"][:], OP.add)
        self.tt(bs, bs[:, 2:3], bs, bs[:, 1:2], bs, bs[:, 0:1], OP.subtract)
        self.ts(dd, dd[:], c["pow2"], c["pow2"][:], bs[:, 2:3], None, OP.mult, R=[bs])
        self.tt(bs, bs[:, 3:4], bs, bs[:, 0:1], dd, dd[:, 1:2], OP.add)
        kk = float(self.KTOP) - 0.5
        for i in range(NIT):
            self.ts(mb, mb[:, 0:Wd], Sc, Sc[:, 0:Wd], bs[:, 3:4], None, OP.is_ge, OP.add, R=[bs], accum=bs[:, 4:5], W=[bs])
            self.ts(bs, bs[:, 5:6], bs, bs[:, 4:5], kk, -0.5, OP.is_ge, OP.add)
            self.stt(bs, bs[:, 3:4], bs, bs[:, 5:6], dd[:, i + 1:i + 2], bs, bs[:, 3:4], OP.mult, OP.add, R=[dd])
        self.stt(bs, bs[:, 6:7], dd, dd[:, NIT:NIT + 1], -1.0, bs, bs[:, 3:4], OP.mult, OP.add)
        self.ts(mb, mb[:, 0:Wd], Sc, Sc[:, 0:Wd], bs[:, 6:7], None, OP.is_ge, R=[bs])
        Pm = P[2]
        for j0 in range(0, t + 1, 8):
            nj = min(8, t + 1 - j0)
            pmv = Pm[:].bitcast(BF16)
            for j in range(j0, j0 + nj):
                self.tr(Pm, pmv[:, (j - j0) * 128:(j - j0 + 1) * 128], mb, mb[:, j * 128:(j + 1) * 128], c["identb"], c["identb"][:])
            self.acopy(maskT, maskT[:, j0:j0 + nj, :], Pm, pmv[:, 0:nj * 128].rearrange("p (j q) -> p j q", j=nj))
        mixT, H, J = T["mixT"], c["H"], c["J"]
        for half in range(2):
            PO, PD = P[5], P[6]
            for j in range(t + 1):
                Pst = P[3 + (j % 2)]
                kc = slice(j * 128, (j + 1) * 128)
                d0 = min((t - j) * 128, HCLAMP)
                for jj in range(2):
                    ch = 2 * half + jj
                    for hh in range(2):
                        hq = QPERM[2 * ch + hh]
                        n = hq // 2
                        col = (jj * 2 + hh) * 128
                        self.mm(Pst, Pst[:, col:col + 128], kT, kT[hh * 64:(hh + 1) * 64, n // 2, kc],
                                qT, qT[hh * 64:(hh + 1) * 64, ch, :], True, False)
                        self.mm(Pst, Pst[:, col:col + 128], J, J[:], H, H[:, hq, d0:d0 + 128], False, True)
                pb = T["pb"].next()
                self.act(pb, pb[:], Pst, Pst[:], AF.Exp, bias=c["nshift"][:, 0:1], scale=1.0, R=[c["nshift"]])
                pm = T["pm"].next()
                self.tt(pm, pm[:].rearrange("p (h q) -> p h q", h=4), pb, pb[:].rearrange("p (h q) -> p h q", h=4),
                        maskT, maskT[:, j, :].unsqueeze(1).to_broadcast([128, 4, 128]), OP.mult)
                for jj in range(2):
                    ch = 2 * half + jj
                    nb = 2 * (ch // 2) * 64
                    for hh in range(2):
                        col = (jj * 2 + hh) * 128
                        first = (j == 0 and jj == 0 and hh == 0)
                        lastm = (j == t and jj == 1 and hh == 1)
                        self.mm(PO, PO[:, col:col + 128], vc, vc[:, j, nb:nb + 128], pm, pm[:, col:col + 128], first, lastm)
                self.mm(PD, PD[:], c["onesb"], c["onesb"][:], pm, pm[:], j == 0, j == t)
            rden = T["rden"]
            s.op("dve", lambda e, PD=PD: e.reciprocal(out=rden[:], in_=PD[:]), R=[PD], W=[rden])
            for jj in range(2):
                ch = 2 * half + jj
                for hh in range(2):
                    col = (jj * 2 + hh) * 128
                    pr = slice(hh * 64, (hh + 1) * 64)
                    self.tt(mixT, mixT[pr, ch, :], PO, PO[pr, col:col + 128], rden, rden[pr, col:col + 128], OP.mult)

    def pool_mix(self, first, n, ubuf, oc=0):
        s, c, T, P, lay = self.s, self.c, self.T, self.P, self.lay
        A, B = T["pl"]
        Wn = 15 + n
        self.tt(A, A[:, :, 1:Wn], ubuf, ubuf[:, :, 1:Wn], ubuf, ubuf[:, :, 0:Wn - 1], OP.add)
        self.tt(B, B[:, :, 3:Wn], A, A[:, :, 3:Wn], A, A[:, :, 1:Wn - 2], OP.add)
        pooled = T["pooled"]
        self.vcopy(pooled, pooled[0:64, 0, 0:n], A, A[0:64, 0, 15:Wn])
        self.vcopy(pooled, pooled[64:128, 0, 0:n], B, B[64:128, 0, 15:Wn])
        self.tt(A, A[:, 1, 7:Wn], B, B[:, 1, 7:Wn], B, B[:, 1, 3:Wn - 4], OP.add)
        self.vcopy(pooled, pooled[0:64, 1, 0:n], A, A[0:64, 1, 15:Wn])
        self.tt(B, B[64:128, 1, 15:Wn], A, A[64:128, 1, 15:Wn], A, A[64:128, 1, 7:Wn - 8], OP.add)
        self.vcopy(pooled, pooled[64:128, 1, 0:n], B, B[64:128, 1, 15:Wn])
        if first:
            self.tt(pooled, pooled[:, :, 0:n], pooled, pooled[:, :, 0:n], c["cinv"], c["cinv"][:, :, 0:n], OP.mult)
            self.tt(pooled, pooled[:, :, 0:n], pooled, pooled[:, :, 0:n], ubuf, ubuf[:, :, 15:Wn], OP.subtract)
        else:
            for cc in range(2):
                self.stt(pooled, pooled[:, cc, 0:n], pooled, pooled[:, cc, 0:n], c["winv"][:, cc:cc + 1], ubuf, ubuf[:, cc, 15:Wn],
                         OP.mult, OP.subtract, R=[c["winv"]])
        wpb, psc, mixT = lay["wpb"], lay["psc"], T["mixT"]
        for cc in range(2):
            self.mm(P[7], P[7][:, cc * 128:cc * 128 + n], wpb, wpb[:, cc, :], pooled, pooled[:, cc, 0:n], True, True)
            self.ts(mixT, mixT[:, 4 + cc, oc:oc + n], P[7], P[7][:, cc * 128:cc * 128 + n], psc[:, cc:cc + 1], None, OP.mult, R=[psc])

    def mem_attend(self, n, mkT=None, mvb=None, qc=0):
        s, c, T, P, lay = self.s, self.c, self.T, self.P, self.lay
        mkT = mkT or lay["mkT"]
        mvb = mvb or lay["mvb"]
        mqm, mixT = T["mqm"], T["mixT"]
        PO, PD = P[5], P[6]
        for h in range(4):
            for i in range(2):
                Pst = P[i]
                cc, hh = h // 2, h % 2
                self.mm(Pst, Pst[:, h * 128:h * 128 + n], mkT, mkT[:, cc, i * 128:(i + 1) * 128],
                        mqm, mqm[:, hh, cc, qc:qc + n], True, True)
        for i in range(2):
            Pst = P[i]
            import os
            km = int(os.environ.get("KM", "9")) if n == 128 else 9
            if km <= 1:
                continue
            pb = T["pb"].next()
            if n == 128:
                self.act(pb, pb[:], Pst, Pst[:], AF.Exp, bias=c["nshift"][:, 0:1], scale=1.0, R=[c["nshift"]])
            else:
                for h in range(4):
                    self.act(pb, pb[:, h * 128:h * 128 + n], Pst, Pst[:, h * 128:h * 128 + n], AF.Exp, bias=c["nshift"][:, 0:1], scale=1.0, R=[c["nshift"]])
            if km <= 2:
                continue
            for h in range(4):
                cc = h // 2
                first = (i == 0 and h == 0)
                lastm = (i == 1 and h == 3)
                self.mm(PO, PO[:, h * 128:h * 128 + n], mvb, mvb[:, i, cc * 128:(cc + 1) * 128], pb, pb[:, h * 128:h * 128 + n], first, lastm)
                self.mm(PD, PD[:, h * 128:h * 128 + n], c["onesb"], c["onesb"][:], pb, pb[:, h * 128:h * 128 + n], first, lastm)
        rden = T["rden"]
        if km <= 3:
            return
        for h in range(4):
            cc, hh = h // 2, h % 2
            pr = slice(hh * 64, (hh + 1) * 64)
            cs = slice(h * 128, h * 128 + n)
            s.op("dve", lambda e, pr=pr, cs=cs: e.reciprocal(out=rden[pr, cs], in_=PD[pr, cs]), R=[PD], W=[rden])
            self.tt(mixT, mixT[pr, 6 + cc, qc:qc + n], PO, PO[pr, cs], rden, rden[pr, cs], OP.mult)

    def sample_tile(self, l):
        s, I, O, c, T, P, lay = self.s, self.I, self.O, self.c, self.T, self.P, self.lay
        NS = self.NS
        xsT = c["xsT"]
        hT, wf = T["hT"], self.Wt["wf"]
        self.project(l, xsT, xsT[:], NS, P)
        kn, tm = T["kn32"], T["tm"]
        usT = s.sb("usT", [128, 2, NS], F32)
        self.vcopy(usT, usT[:], P[1], P[1][:, 0:256].rearrange("p (c t) -> p c t", c=2)[:, :, 0:NS])
        ikn = s.sb("ikn", [128, NS], BF16)
        self.vcopy(ikn, ikn[:], P[0], P[0][:, 384:384 + NS])
        self.smp = dict(usT=usT, ikn=ikn)
        ks = s.sb("ks_tok", [NS, 256], F32)
        for cc in range(2):
            self.tr(P[3], P[3][0:NS, cc * 128:(cc + 1) * 128], kn, kn[:, cc, 0:NS], c["ident"], c["ident"][:])
        self.vcopy(ks, ks[:], P[3], P[3][0:NS, 0:256])
        self.smp["ks"] = ks
        self.dma("sp", O["k_s"], O["k_s"][l, :, :], ks, ks[:], key="ks")
        self.dma("sp", O["v_s"], O["v_s"][l, :, :], tm, tm[0:NS, 0:256], key="tm")
        self.dma("sp", O["ik_s"], O["ik_s"][l, :, :], tm, tm[0:NS, 256:288], key="tm2")
        for k in range(8):
            self.mm(P[4], P[4][0:NS, 0:256], hT, hT[:, k, 0:NS], wf, wf[:, k, 1536:1792], k == 0, k == 7)
        us = s.sb("us_tok", [NS, 256], F32)
        self.vcopy(us, us[:], P[4], P[4][0:NS, 0:256])
        mixT = T["mixT"]
        self.memset(mixT, mixT[:, 0:4, 0:NS], 0.0, eng="dve")
        st = s.sb("st_tok", [15, 256], F32)
        ubs = s.sb("ubs", [128, 2, 16], F32)
        mkt = s.sb("mkt", [128, 2, 256], BF16)
        mvs = s.sb("mvs", [128, 2, 256], BF16)
        mks = s.sb("mks", [128, 2, 256], BF16)
        for si in range(NS):
            self.dma("sp", st, st[:], I["spool"], I["spool"][l, si, :, :], key="st")
            self.dma("sp", O["pool_s"], O["pool_s"][l, si, 0:14, :], st, st[1:15, :], key="st_o")
            self.dma("sp", O["pool_s"], O["pool_s"][l, si, 14:15, :], us, us[si:si + 1, :], key="us_o")
            for cc in range(2):
                self.tr(P[7], P[7][:, cc * 16:cc * 16 + 15], st, st[:, cc * 128:(cc + 1) * 128], c["ident"], c["ident"][0:15, 0:15])
            self.vcopy(ubs, ubs[:, :, 0:15], P[7], P[7][:, 0:32].rearrange("p (c t) -> p c t", c=2)[:, :, 0:15])
            self.vcopy(ubs, ubs[:, :, 15:16], usT, usT[:, :, si:si + 1])
            self.pool_mix(False, 1, ubs, oc=si)
            self.dma("pool", mkt, mkt[:], I["cmk"], I["cmk"][l, si].rearrange("(i p) c -> p i c", p=128), key="mkt")
            self.dma("pool", mvs, mvs[:], I["cmv"], I["cmv"][l, si].rearrange("(i p) c -> p i c", p=128), key="mvs")
            pmv = P[2][:].bitcast(BF16)
            for i in range(2):
                for cc in range(2):
                    self.tr(P[2], pmv[:, (i * 2 + cc) * 128:(i * 2 + cc + 1) * 128], mkt, mkt[:, i, cc * 128:(cc + 1) * 128], c["identb"], c["identb"][:])
            for cc in range(2):
                for i in range(2):
                    self.acopy(mks, mks[:, cc, i * 128:(i + 1) * 128], P[2], pmv[:, (i * 2 + cc) * 128:(i * 2 + cc + 1) * 128])
            self.mem_attend(1, mks, mvs, qc=si)
        if self.with_sample_dsa:
            self.dsa_sample(l)
        self.out_proj(xsT, xsT[:], NS)
        self.vcopy(xsT, xsT[:], T["x1"], T["x1"][:, :, 0:NS])

    def getbuf(self, alias, dt, shape, name):
        n = int(np.prod(shape))
        nbytes = 0
        if alias is not None:
            ap = alias.t[:]
            nd = len(ap.shape)
            if nd == 3:
                ap = ap.rearrange("p a b -> p (a b)")
            elif nd == 4:
                ap = ap.rearrange("p a b c -> p (a b c)")
            nbytes = ap.shape[1] * mybir.dt.size(ap.dtype)
        if nbytes >= n * mybir.dt.size(dt):
            ap = ap.bitcast(dt)[:, 0:n]
            buf = alias
        else:
            buf = self.s.sb(name, [128, n], dt)
            ap = buf.t[:]
        if len(shape) == 2:
            ap = ap.rearrange("p (a b) -> p a b", a=shape[0])
        elif len(shape) == 3:
            ap = ap.rearrange("p (a b c) -> p a b c", a=shape[0], b=shape[1])
        return buf, ap

    def dsa_sample(self, l):
        s, I, O, c, T, P, lay, smp = self.s, self.I, self.O, self.c, self.T, self.P, self.lay, self.smp
        NS, NPG, NPOOL = self.NS, self.NPG, self.NPOOL
        KT = float(self.KTOP_S) - 0.5
        ident, identb, ones, onesb = c["ident"], c["identb"], c["ones"], c["onesb"]
        cs = s.sb("c_smp", [128, 167], F32)
        self.dma("sp", cs, cs[:], I["c_smp"], I["c_smp"][:, :])
        fold = cs[:, 0:128]
        bmask = cs[:, 128:152].rearrange("p (g h) -> p g h", g=3)
        rmask = cs[:, 152:156]
        dbase = cs[:, 156:157]
        nsel = cs[0:8, 157:161]
        even, odd = cs[0:8, 161:162], cs[0:8, 162:163]
        pairsel = cs[0:8, 163:167]
        thr = s.sb("thr", [128, 31], F32)
        self.dma("sp", thr, thr[:], I["c_thr"], I["c_thr"][:, :])
        rbb = s.sb("rb_bc", [128, 32, 8], F32)
        self.dma("sp", rbb, rbb[:].rearrange("p b h -> p (b h)"), I["rel_bias"], bass.AP(I["rel_bias"].t.tensor, 0, [[0, 128], [1, 256]]))
        drel = s.sb("drel", [128, 8, 31], F32)
        rb0 = s.sb("rb0", [128, 8], F32)
        for pos in range(8):
            hq = QPERM[pos]
            self.tt(drel, drel[:, pos, :], rbb, rbb[:, 1:32, hq], rbb, rbb[:, 0:31, hq], OP.subtract)
            self.vcopy(rb0, rb0[:, pos:pos + 1], rbb, rbb[:, 0, hq:hq + 1])
        sel = s.sb("sel", [NS, NS, 128], F32)
        for si in range(NS):
            self.vcopy(sel, sel[0:NS, si, :], ident, ident[0:NS, si:si + 1].to_broadcast([NS, 128]))
        ptT = s.sb("ptT", [128, NS], I32)
        self.memset(ptT, ptT[:], 0, eng="dve")
        self.s.dma("sp", lambda e: e.dma_start(out=ptT[0:NPG, :], in_=bass.AP(I["ptab"].t.tensor, 0, [[1, NPG], [NPG, NS]]),
                                               allow_slow_non_contiguous=True), R=[I["ptab"]], W=[ptT], key="ptT")
        gidx = s.sb("gidx", [128, NS], I32)
        self.ts(gidx, gidx[:], ptT, ptT[:], float(l * NPOOL), None, OP.add)
        pb128 = s.sb("pb128", [128, NS], F32)
        self.ts(pb128, pb128[:], gidx, gidx[:], 128.0, None, OP.mult)
        Scs = s.sb("Scs", [128, NS, 129], F32)
        self.memset(Scs, Scs[:], NEG, eng="dve")
        PGb, PG = self.getbuf(None, F32, [4096], "PGd")
        IKb, IKT = self.getbuf(None, BF16, [32, NPG], "IKTd")
        Rb, R = self.getbuf(T["x1"], F32, [128, 8], "Rd")
        cikp = I["cik"][:, :].rearrange("(g t) d -> g (t d)", t=128)
        rhs3 = s.sb("rhs3", [128, 3, 8], F32)
        IQm = s.sb("IQm", [128, 4, 8], BF16)
        iwb = s.sb("iwb", [128, 8], F32)
        t8 = s.sb("t8", [1, 8], F32)
        iqT, iw16, ikn = T["iqT"], T["iw16"], smp["ikn"]
        for si in range(NS):
            self.s.dma("pool", lambda e, si=si: e.indirect_dma_start(
                out=PG[0:NPG, :], out_offset=None, in_=cikp,
                in_offset=bass.IndirectOffsetOnAxis(ap=gidx[0:NPG, si:si + 1], axis=0)),
                R=[I["cik"], gidx], W=[PGb], key="PG")
            for g0 in range(0, 32, 4):
                Pb = P[(g0 // 4) % 2]
                for g in range(g0, g0 + 4):
                    self.tr(Pb, Pb[:, (g - g0) * NPG:(g - g0 + 1) * NPG], PGb, PG[0:NPG, g * 128:(g + 1) * 128], ident, ident[0:NPG, 0:NPG])
                self.acopy(IKb, IKT[:, g0:g0 + 4, :], Pb, Pb[:, 0:4 * NPG].rearrange("p (g n) -> p g n", g=4))
            for g in range(3):
                self.ts(rhs3, rhs3[:, g, :], cs, bmask[:, g, :], iqT[:, g, si:si + 1], None, OP.mult, R=[iqT])
            for g in range(3):
                self.mm(P[2], P[2][:, 0:8], cs, fold, rhs3, rhs3[:, g, :], g == 0, g == 2)
            for r in range(4):
                self.ts(IQm, IQm[:, r, :], P[2], P[2][:, 0:8], rmask[:, r:r + 1], None, OP.mult, R=[cs])
            for t in range(128):
                g, r = t // 4, t % 4
                Pb = P[3 + t // 64]
                col = (t % 64) * 8
                self.mm(Pb, Pb[0:NPG, col:col + 8], IKb, IKT[:, g, :], IQm, IQm[:, r, :], True, True)
            for hf in range(2):
                self.act(Rb, R[0:NPG, hf * 64:(hf + 1) * 64, :], P[3 + hf], P[3 + hf][0:NPG, 0:512].rearrange("p (t h) -> p t h", h=8), AF.Relu)
            self.mm(P[2], P[2][:, 8:16], sel, sel[0:NS, si, :], iw16, iw16[0:NS, 0:8], True, True)
            self.vcopy(iwb, iwb[:], P[2], P[2][:, 8:16])
            self.tt(Rb, R[0:NPG, :, :], Rb, R[0:NPG, :, :], iwb, iwb[0:NPG, :].unsqueeze(1).to_broadcast([NPG, 128, 8]), OP.mult)
            s.op("dve", lambda e, si=si: e.tensor_reduce(out=Scs[0:NPG, si, 0:128], in_=R[0:NPG, :, :], axis=AX.X, op=OP.add), R=[Rb], W=[Scs])
            self.mm(P[2], P[2][0:1, 16:24], ikn, ikn[:, si:si + 1], IQm, IQm[:, 0, :], True, True)
            self.act(t8, t8[:], P[2], P[2][0:1, 16:24], AF.Relu)
            self.tt(t8, t8[:], t8, t8[:], iwb, iwb[0:1, :], OP.mult)
            s.op("dve", lambda e, si=si: e.tensor_reduce(out=Scs[0:1, si, 128:129], in_=t8[:], axis=AX.X, op=OP.add), R=[t8], W=[Scs])
        mnp = s.sb("mnp", [128, NS], F32)
        mxp = s.sb("mxp", [128, NS], F32)
        s.op("dve", lambda e: e.tensor_reduce(out=mnp[0:NPG, :], in_=Scs[0:NPG, :, 0:128], axis=AX.X, op=OP.min), R=[Scs], W=[mnp])
        s.op("dve", lambda e: e.tensor_reduce(out=mxp[0:NPG, :], in_=Scs[0:NPG, :, 0:128], axis=AX.X, op=OP.max), R=[Scs], W=[mxp])
        self.tr(P[2], P[2][0:NS, 0:NPG], mnp, mnp[0:NPG, :], ident, ident[0:NPG, 0:NPG])
        self.tr(P[2], P[2][0:NS, 128:128 + NPG], mxp, mxp[0:NPG, :], ident, ident[0:NPG, 0:NPG])
        v3 = s.sb("v3", [NS, 4], F32)
        s.op("dve", lambda e: e.tensor_reduce(out=v3[:, 0:1], in_=P[2][0:NS, 0:NPG], axis=AX.X, op=OP.min), R=[P[2]], W=[v3])
        s.op("dve", lambda e: e.tensor_reduce(out=v3[:, 1:2], in_=P[2][0:NS, 128:128 + NPG], axis=AX.X, op=OP.max), R=[P[2]], W=[v3])
        self.tt(v3, v3[:, 2:3], v3, v3[:, 1:2], v3, v3[:, 0:1], OP.subtract)
        dg = s.sb("dg", [NS, 2, NS], F32)
        self.ts(dg, dg[:, 0, :], ident, ident[0:NS, 0:NS], v3[:, 0:1], None, OP.mult, R=[v3])
        self.ts(dg, dg[:, 1, :], ident, ident[0:NS, 0:NS], v3[:, 2:3], None, OP.mult, R=[v3])
        self.mm(P[2], P[2][:, 256:256 + 2 * NS], ones, ones[0:NS, :], dg, dg[:].rearrange("p a b -> p (a b)"), True, True)
        mr = s.sb("mr", [128, 2, NS], F32)
        self.vcopy(mr, mr[:].rearrange("p a b -> p (a b)"), P[2], P[2][:, 256:256 + 2 * NS])
        ddS = s.sb("ddS", [128, NS, NIT + 1], F32)
        self.tt(ddS, ddS[:], c["pow2"], c["pow2"][:].unsqueeze(1).to_broadcast([128, NS, NIT + 1]),
                mr, mr[:, 1, :].unsqueeze(2).to_broadcast([128, NS, NIT + 1]), OP.mult)
        mid = s.sb("mid", [128, NS], F32)
        self.tt(mid, mid[:], mr, mr[:, 0, :], ddS, ddS[:, :, 1], OP.add)
        jb, junk = self.getbuf(None, F32, [NS, 129], "junkd")
        cntp = s.sb("cntp", [128, NS], F32)
        sgn = s.sb("sgn", [128, NS], F32)
        for i in range(NIT):
            self.tt(jb, junk, Scs, Scs[:], mid, mid[:].unsqueeze(2).to_broadcast([128, NS, 129]), OP.is_ge)
            s.op("dve", lambda e: e.tensor_reduce(out=cntp[:], in_=junk, axis=AX.X, op=OP.add), R=[jb], W=[cntp])
            self.mm(P[2], P[2][:, 320:320 + NS], ones, ones[:], cntp, cntp[:], True, True)
            self.ts(sgn, sgn[:], P[2], P[2][:, 320:320 + NS], KT, -0.5, OP.is_ge, OP.add)
            self.tt(sgn, sgn[:], sgn, sgn[:], ddS, ddS[:, :, i + 1], OP.mult)
            self.tt(mid, mid[:], mid, mid[:], sgn, sgn[:], OP.add)
        lob = s.sb("lob", [128, NS], F32)
        self.tt(lob, lob[:], mid, mid[:], ddS, ddS[:, :, NIT], OP.subtract)
        Kcb, Kc = self.getbuf(None, F32, [16, 256], "Kcd")
        Vcb, Vc = self.getbuf(None, F32, [16, 256], "Vcd")
        tmb, tmp = self.getbuf(None, F32, [16, 64], "tmpd")
        geb, ge = self.getbuf(T["sq"], F32, [16, 31], "ged")
        tbb, tb = self.getbuf(T["rs"], F32, [16, 31], "tbd")
        qbb, q_bc = self.getbuf(T["rden"], F32, [512], "qbd")
        Wk = s.sb("Wk", [128, 129], F32)
        Wk2 = s.sb("Wk2", [128, 129], F32)
        m8 = s.sb("m8", [128, 16], F32)
        i8 = s.sb("i8", [128, 16], U32)
        cf = s.sb("cf", [128, 6, 16], F32)
        rowf = s.sb("rowf", [128, 16], F32)
        rowi = s.sb("rowi", [128, 16], I32)
        vnew = s.sb("vnew", [128, 1], F32)
        dgq = s.sb("dgq", [128, 128], BF16)
        dgk = s.sb("dgk", [128, 128], F32)
        knb = s.sb("knb", [128, 256], F32)
        vnb = s.sb("vnb", [128, 256], F32)
        lgr = s.sb("lgr", [128, 8, 16], F32)
        lg = s.sb("lg", [128, 8, 16], F32)
        lgn = s.sb("lgn", [128, 8], F32)
        tmn = s.sb("tmn", [128, 8, 64], F32)
        bp = s.sb("bp", [128, 16], F32)
        o4 = s.sb("o4", [8, 4, 64], F32)
        osel = s.sb("osel", [8, 64], F32)
        rd = s.sb("rd", [8, 1], F32)
        A2 = s.sb("A2", [8, 128], F32)
        ckv, cvv = I["ck"], I["cv"]
        qT, kn, tm, mixT = T["qT"], T["kn32"], T["tm"], T["mixT"]
        for si in range(NS):
            lo_s = lob[:, si:si + 1]
            self.vcopy(Wk, Wk[:], Scs, Scs[:, si, :])
            s.op("dve", lambda e: e.max(out=m8[:, 0:8], in_=Wk[:]), R=[Wk], W=[m8])
            s.op("dve", lambda e: e.max_index(out=i8[:, 0:8], in_max=m8[:, 0:8], in_values=Wk[:]), R=[Wk, m8], W=[i8])
            s.op("dve", lambda e: e.match_replace(out=Wk2[:], in_to_replace=m8[:, 0:8], in_values=Wk[:], imm_value=NEG), R=[Wk, m8], W=[Wk2])
            s.op("dve", lambda e: e.max(out=m8[:, 8:16], in_=Wk2[:]), R=[Wk2], W=[m8])
            s.op("dve", lambda e: e.max_index(out=i8[:, 8:16], in_max=m8[:, 8:16], in_values=Wk2[:]), R=[Wk2, m8], W=[i8])
            self.vcopy(cf, cf[:, 0, :], i8, i8[:])
            self.ts(cf, cf[:, 1, :], m8, m8[:], lo_s, None, OP.is_ge, R=[lob])
            self.ts(cf, cf[:, 2, :], cf, cf[:, 0, :], 127.5, None, OP.is_le)
            self.tt(cf, cf[:, 3, :], cf, cf[:, 1, :], cf, cf[:, 2, :], OP.mult)
            self.ts(vnew, vnew[:], Scs, Scs[:, si, 128:129], lo_s, None, OP.is_ge, R=[lob])
            self.ts(cf, cf[:, 4, :], cf, cf[:, 0, :], 127.0, None, OP.min)
            self.ts(rowf, rowf[:], cf, cf[:, 4, :], pb128[:, si:si + 1], None, OP.add, R=[pb128])
            self.vcopy(rowi, rowi[:], rowf, rowf[:])
            self.ts(cf, cf[:, 5, :], cf, cf[:, 4, :], -1.0, dbase, OP.mult, OP.add, R=[cs])
            for i in range(16):
                self.s.dma("pool", lambda e, i=i: e.indirect_dma_start(
                    out=Kc[:, i, :], out_offset=None, in_=ckv[:, :],
                    in_offset=bass.IndirectOffsetOnAxis(ap=rowi[:, i:i + 1], axis=0)), R=[ckv, rowi], W=[Kcb], key="Kc")
                self.s.dma("pool", lambda e, i=i: e.indirect_dma_start(
                    out=Vc[:, i, :], out_offset=None, in_=cvv[:, :],
                    in_offset=bass.IndirectOffsetOnAxis(ap=rowi[:, i:i + 1], axis=0)), R=[cvv, rowi], W=[Vcb], key="Vc")
            for ch in range(4):
                self.ts(dgq, dgq[:], identb, identb[:], qT[:, ch, si:si + 1], None, OP.mult, R=[qT])
                self.mm(P[3], P[3][:, ch * 128:(ch + 1) * 128], onesb, onesb[:], dgq, dgq[:], True, True)
            self.vcopy(qbb, q_bc, P[3], P[3][:, 0:512])
            for cc in range(2):
                self.ts(dgk, dgk[:], ident, ident[:], kn[:, cc, si:si + 1], None, OP.mult, R=[kn])
                self.mm(P[4], P[4][:, cc * 128:(cc + 1) * 128], ones, ones[:], dgk, dgk[:], True, True)
            self.mm(P[4], P[4][:, 256:512], sel, sel[0:NS, si, :], tm, tm[0:NS, 0:256], True, True)
            self.vcopy(knb, knb[:], P[4], P[4][:, 0:256])
            self.vcopy(vnb, vnb[:], P[4], P[4][:, 256:512])
            for pos in range(8):
                n_ = QPERM[pos] // 2
                self.tt(tmb, tmp, Kcb, Kc[:, :, n_ * 64:(n_ + 1) * 64], qbb,
                        q_bc[:, pos * 64:(pos + 1) * 64].unsqueeze(1).to_broadcast([128, 16, 64]), OP.mult)
                s.op("dve", lambda e, pos=pos: e.tensor_reduce(out=lgr[:, pos, :], in_=tmp, axis=AX.X, op=OP.add), R=[tmb], W=[lgr])
            for A in range(2):
                q4 = q_bc[:, A * 256:(A + 1) * 256].rearrange("p (b c d) -> p b c d", b=2, c=2)
                k2 = knb[:, A * 128:(A + 1) * 128].rearrange("p (c d) -> p c d", c=2).unsqueeze(1).to_broadcast([128, 2, 2, 64])
                self.tt(tmn, tmn[:, A * 4:(A + 1) * 4, :].rearrange("p (b c) d -> p b c d", b=2), qbb, q4, knb, k2, OP.mult)
            s.op("dve", lambda e: e.tensor_reduce(out=lgn[:], in_=tmn[:], axis=AX.X, op=OP.add), R=[tmn], W=[lgn])
            self.tt(geb, ge, cf, cf[:, 5, :].unsqueeze(2).to_broadcast([128, 16, 31]), thr, thr[:].unsqueeze(1).to_broadcast([128, 16, 31]), OP.is_ge)
            for pos in range(8):
                self.tt(tbb, tb, geb, ge, drel, drel[:, pos, :].unsqueeze(1).to_broadcast([128, 16, 31]), OP.mult)
                s.op("dve", lambda e: e.tensor_reduce(out=bp[:], in_=tb, axis=AX.X, op=OP.add), R=[tbb], W=[bp])
                self.stt(lg, lg[:, pos, :], bp, bp[:], rb0[:, pos:pos + 1], lgr, lgr[:, pos, :], OP.add, OP.add, R=[rb0])
            self.tt(lgn, lgn[:], lgn, lgn[:], rb0, rb0[:], OP.add)
            self.act(lg, lg[:], lg, lg[:], AF.Exp, bias=c["nshift"][:, 0:1], scale=1.0, R=[c["nshift"]])
            self.tt(lg, lg[:], lg, lg[:], cf, cf[:, 3, :].unsqueeze(1).to_broadcast([128, 8, 16]), OP.mult)
            self.act(lgn, lgn[:], lgn, lgn[:], AF.Exp, bias=c["nshift"][:, 0:1], scale=1.0, R=[c["nshift"]])
            self.ts(lgn, lgn[:], lgn, lgn[:], vnew[:, 0:1], None, OP.mult, R=[vnew])
            for i in range(16):
                self.mm(P[5], P[5][0:8, 0:256], lg, lg[:, :, i], Vcb, Vc[:, i, :], i == 0, False)
                self.mm(P[6], P[6][0:8, 0:1], lg, lg[:, :, i], ones, ones[:, 0:1], i == 0, False)
            self.mm(P[5], P[5][0:8, 0:256], lgn, lgn[:], vnb, vnb[:], False, True)
            self.mm(P[6], P[6][0:8, 0:1], lgn, lgn[:], ones, ones[:, 0:1], False, True)
            self.tt(o4, o4[:], P[5], P[5][0:8, 0:256].rearrange("p (n d) -> p n d", n=4), cs, nsel.unsqueeze(2).to_broadcast([8, 4, 64]), OP.mult)
            s.op("dve", lambda e: e.tensor_reduce(out=osel[:], in_=o4[:].rearrange("p n d -> p d n"), axis=AX.X, op=OP.add), R=[o4], W=[osel])
            s.op("dve", lambda e: e.reciprocal(out=rd[:], in_=P[6][0:8, 0:1]), R=[P[6]], W=[rd])
            self.ts(A2, A2[:, 0:64], osel, osel[:], rd[:, 0:1], even, OP.mult, OP.mult, R=[rd, cs])
            self.ts(A2, A2[:, 64:128], osel, osel[:], rd[:, 0:1], odd, OP.mult, OP.mult, R=[rd, cs])
            self.mm(P[7], P[7][:, 0:4], A2, A2[:], cs, pairsel, True, True)
            self.vcopy(mixT, mixT[:, 0:4, si], P[7], P[7][:, 0:4])

    def phase2(self, l):
        s, I, O, c = self.s, self.I, self.O, self.c
        S, NS = self.S, self.NS
        NF = DFF // 128
        WN = 256
        wu = s.sb("wu", [128, 8, 2 * DFF], BF16)
        wd = s.sb("wd", [128, NF, D], BF16)
        for k in range(8):
            for c0 in range(0, 2 * DFF, 2048):
                n = min(2048, 2 * DFF - c0)
                self.dma("pool", wu, wu[:, k, c0:c0 + n], I["w_up"], I["w_up"][l, k * 128:(k + 1) * 128, c0:c0 + n], key=None)
        for f in range(NF):
            self.dma("pool", wd, wd[:, f, :], I["w_down"], I["w_down"][l, f * 128:(f + 1) * 128, :], key=None)
        g2 = self.load_gvec("g2", I["norm2_g"], l)
        cw = s.sb("cw", [128, 44, 3], F32)
        cb = s.sb("cb", [128, 44], F32)
        for j in range(3):
            ap = bass.AP(I["conv_w"].t.tensor, (l * 3 + j) * 2 * DFF, [[1, 128], [128, 44]])
            self.s.dma("sp", lambda e, ap=ap, j=j: e.dma_start(out=cw[:, :, j], in_=ap, allow_slow_non_contiguous=True), R=[I["conv_w"]], W=[cw], key="cw%d" % j)
        apb = bass.AP(I["conv_b"].t.tensor, l * 2 * DFF, [[1, 128], [128, 44]])
        self.s.dma("sp", lambda e: e.dma_start(out=cb[:], in_=apb, allow_slow_non_contiguous=True), R=[I["conv_b"]], W=[cb], key="cb")
        P = [s.ps("Q%d" % i, [128, 512], F32) for i in range(8)]
        xw = s.sb("xw", [128, 8, WN], F32)
        sq = s.sb("sq8b", [128, 8, WN], F32)
        rs = s.sb("rsb", [128, WN], F32)
        h2 = s.sb("h2", [128, 8, WN], BF16)
        aT = s.sb("aT", [128, NF, WN], BF16)
        t1 = Ring([s.sb("t1_%d" % i, [128, WN], F32) for i in range(2)])
        t2 = Ring([s.sb("t2_%d" % i, [128, WN], F32) for i in range(2)])
        cg = Ring([s.sb("cg_%d" % i, [128, WN], F32) for i in range(2)])
        cv = Ring([s.sb("cv_%d" % i, [128, WN], F32) for i in range(2)])
        sg = Ring([s.sb("sg_%d" % i, [128, WN], F32) for i in range(2)])
        xo = Ring([s.sb("xo_%d" % i, [128, WN], F32) for i in range(2)])
        ytok = s.sb("ytok", [128, D], F32)
        ctk = Ring([s.sb("ctk_%d" % i, [NS, 512], F32) for i in range(2)])
        ident = c["ident"]

        def up_pair(f, n, Pg, Pv):
            for (Pb, ff) in ((Pg, f), (Pv, NF + f)):
                for k in range(8):
                    self.mm(Pb, Pb[:, 0:n], wu, wu[:, k, ff * 128:(ff + 1) * 128], h2, h2[:, k, 0:n], k == 0, k == 7)

        def norm2(xb, xap3, n):
            self.rms_feature_major(xb, xap3, n, g2, h2, P[6], sq, rs)

        def up_rows_out(col0, ncols, dst, dst_ap_fn):
            for ci, c0 in enumerate(range(0, 2 * DFF, 512)):
                Pb = P[4 + ci % 2]
                for k in range(8):
                    self.mm(Pb, Pb[0:ncols, 0:512], h2, h2[:, k, col0:col0 + ncols], wu, wu[:, k, c0:c0 + 512], k == 0, k == 7)
                ct = ctk.next()
                self.vcopy(ct, ct[0:ncols, :], Pb, Pb[0:ncols, 0:512])
                self.dma("sp", dst, dst_ap_fn(c0), ct, ct[0:ncols, :], key=ct.name)

        step = WN - 2
        starts = list(range(0, S, step))
        for wi, st0 in enumerate(starts):
            nnew = min(step, S - st0)
            n = nnew + 2
            if st0 == 0:
                self.memset(xw, xw[:, :, 0:2], 0.0, eng="dve")
                self.dma("sp", xw, xw[:, :, 2:n], self.XA, self.XA[:, 0:nnew].rearrange("(k p) t -> p k t", p=128))
            else:
                self.dma("sp", xw, xw[:, :, 0:n], self.XA, self.XA[:, st0 - 2:st0 + nnew].rearrange("(k p) t -> p k t", p=128))
            norm2(xw, xw[:, :, 0:n], n)
            if wi == len(starts) - 1:
                up_rows_out(n - 2, 2, O["conv_p"], lambda c0: O["conv_p"][l, :, c0:c0 + 512])
            for f in range(NF):
                Pg, Pv = P[(f % 2) * 2], P[(f % 2) * 2 + 1]
                up_pair(f, n, Pg, Pv)
                outs = []
                for (Pb, ff, ring) in ((Pg, f, cg), (Pv, NF + f, cv)):
                    a1, a2, cc_ = t1.next(), t2.next(), ring.next()
                    self.act(a1, a1[:, 0:nnew], Pb, Pb[:, 2:n], AF.Identity, bias=cb[:, ff:ff + 1], scale=cw[:, ff, 2:3], R=[cb, cw])
                    self.stt(a2, a2[:, 0:nnew], Pb, Pb[:, 1:n - 1], cw[:, ff, 1:2], a1, a1[:, 0:nnew], OP.mult, OP.add, R=[cw])
                    self.stt(cc_, cc_[:, 0:nnew], Pb, Pb[:, 0:n - 2], cw[:, ff, 0:1], a2, a2[:, 0:nnew], OP.mult, OP.add, R=[cw])
                    outs.append(cc_)
                sgt = sg.next()
                self.act(sgt, sgt[:, 0:nnew], outs[0], outs[0][:, 0:nnew], AF.Silu)
                self.tt(aT, aT[:, f, 0:nnew], sgt, sgt[:, 0:nnew], outs[1], outs[1][:, 0:nnew], OP.mult)
            for dm in range(8):
                Pb = P[4 + dm % 2]
                for f in range(NF):
                    self.mm(Pb, Pb[:, 0:nnew], wd, wd[:, f, dm * 128:(dm + 1) * 128], aT, aT[:, f, 0:nnew], f == 0, f == NF - 1)
                if l == 0:
                    xt = xo.next()
                    self.tt(xt, xt[:, 0:nnew], Pb, Pb[:, 0:nnew], xw, xw[:, dm, 2:n], OP.add)
                    self.dma("sp", self.XB, self.XB[dm * 128:(dm + 1) * 128, st0:st0 + nnew], xt, xt[:, 0:nnew], key=xt.name)
                else:
                    self.tt(xw, xw[:, dm, 2:n], Pb, Pb[:, 0:nnew], xw, xw[:, dm, 2:n], OP.add)
            if l == 1:
                for b0 in range(0, nnew, 128):
                    nb = min(128, nnew - b0)
                    for k0 in (0, 4):
                        Pb = P[6 + (k0 // 4)]
                        for k in range(k0, k0 + 4):
                            self.tr(Pb, Pb[0:nb, (k - k0) * 128:(k - k0 + 1) * 128], xw, xw[:, k, 2 + b0:2 + b0 + nb], ident, ident[:])
                        self.acopy(ytok, ytok[0:nb, k0 * 128:(k0 + 4) * 128], Pb, Pb[0:nb, 0:512])
                    self.dma("sp", O["y_p"], O["y_p"][st0 + b0:st0 + b0 + nb, :], ytok, ytok[0:nb, :], key="ytok")

        xsT = c["xsT"]
        norm2(xsT, xsT[:], NS)
        up_rows_out(0, NS, O["conv_s"], lambda c0: O["conv_s"][l, :, 1, c0:c0 + 512])
        sT = [s.sb("sT%d" % i, [128, 44, NS], F32) for i in range(2)]
        for i in range(2):
            for ci, c0 in enumerate(range(0, 2 * DFF, 512)):
                ct = ctk.next()
                self.dma("sp", ct, ct[0:NS, :], I["sconv"], I["sconv"][l, :, i, c0:c0 + 512], key=ct.name)
                if i == 1:
                    self.dma("sp", O["conv_s"], O["conv_s"][l, :, 0, c0:c0 + 512], ct, ct[0:NS, :], key=ct.name + "o")
                Pb = P[6 + ci % 2]
                for j in range(4):
                    self.tr(Pb, Pb[:, j * NS:(j + 1) * NS], ct, ct[0:NS, j * 128:(j + 1) * 128], ident, ident[0:NS, 0:NS])
                f0 = c0 // 128
                self.vcopy(sT[i], sT[i][:, f0:f0 + 4, :], Pb, Pb[:, 0:4 * NS].rearrange("p (f s) -> p f s", s=NS))
        for f in range(NF):
            Pg, Pv = P[(f % 2) * 2], P[(f % 2) * 2 + 1]
            up_pair(f, NS, Pg, Pv)
            outs = []
            for (Pb, ff, ring) in ((Pg, f, cg), (Pv, NF + f, cv)):
                a1, a2, cc_ = t1.next(), t2.next(), ring.next()
                self.act(a1, a1[:, 0:NS], Pb, Pb[:, 0:NS], AF.Identity, bias=cb[:, ff:ff + 1], scale=cw[:, ff, 2:3], R=[cb, cw])
                self.stt(a2, a2[:, 0:NS], sT[1], sT[1][:, ff, :], cw[:, ff, 1:2], a1, a1[:, 0:NS], OP.mult, OP.add, R=[cw])
                self.stt(cc_, cc_[:, 0:NS], sT[0], sT[0][:, ff, :], cw[:, ff, 0:1], a2, a2[:, 0:NS], OP.mult, OP.add, R=[cw])
                outs.append(cc_)
            sgt = sg.next()
            self.act(sgt, sgt[:, 0:NS], outs[0], outs[0][:, 0:NS], AF.Silu)
            self.tt(aT, aT[:, f, 0:NS], sgt, sgt[:, 0:NS], outs[1], outs[1][:, 0:NS], OP.mult)
        for dm in range(8):
            Pb = P[4 + dm % 2]
            for f in range(NF):
                self.mm(Pb, Pb[:, 0:NS], wd, wd[:, f, dm * 128:(dm + 1) * 128], aT, aT[:, f, 0:NS], f == 0, f == NF - 1)
            self.tt(xsT, xsT[:, dm, :], Pb, Pb[:, 0:NS], xsT, xsT[:, dm, :], OP.add)
        if l == 1:
            for k0 in (0, 4):
                for k in range(k0, k0 + 4):
                    self.tr(P[6], P[6][0:NS, (k - k0) * 128:(k - k0 + 1) * 128], xsT, xsT[:, k, :], ident, ident[:])
                self.acopy(ytok, ytok[0:NS, k0 * 128:(k0 + 4) * 128], P[6], P[6][0:NS, 0:512])
            self.dma("sp", O["y_s"], O["y_s"][:, :], ytok, ytok[0:NS, :], key="ytok")


def make_consts(S, NPG=128):
    m = np.arange(TABLEN)
    d = m - 127
    oh = np.zeros((32, TABLEN), np.float32)
    b = t5_bucket_np(np.maximum(d, 0))
    valid = d >= 0
    oh[b[valid], m[valid]] = 1.0
    w = np.array([2, 4, 8, 16], np.float32)
    cinv = np.zeros((128, 2, 128), np.float32)
    winv = np.zeros((128, 2), np.float32)
    pos = np.arange(128, dtype=np.float32)
    for g in range(4):
        cc, hh = g // 2, g % 2
        cinv[hh * 64:(hh + 1) * 64, cc, :] = 1.0 / np.minimum(pos + 1.0, w[g])[None, :]
        winv[hh * 64:(hh + 1) * 64, cc] = 1.0 / w[g]
    pow2 = np.tile((2.0 ** -np.arange(NIT + 1, dtype=np.float64)).astype(np.float32)[None, :], (128, 1))
    dd = np.arange(0, 4096)
    bb = t5_bucket_np(dd)
    thr = np.array([dd[bb >= k].min() for k in range(1, 32)], np.float32)
    thr = np.tile(thr[None, :], (128, 1))
    iota = np.tile(np.arange(128, dtype=np.float32)[None, :], (128, 1))
    smp = np.zeros((128, 167), np.float32)
    k = np.arange(128)
    smp[:, 0:128] = (k[:, None] % 32 == k[None, :] % 32)
    for g in range(3):
        for h in range(8):
            smp[:, 128 + g * 8 + h] = (k < 96) & (h // 3 == g) & (k // 32 == h % 3)
    for r in range(4):
        smp[:, 152 + r] = (k // 32 == r)
    smp[:, 156] = np.where(k < NPG, (NPG - k) * 128, 0)
    for pos in range(8):
        for n in range(4):
            smp[pos, 157 + n] = float(n == QPERM[pos] // 2)
        smp[pos, 161] = float(pos % 2 == 0)
        smp[pos, 162] = float(pos % 2 == 1)
        for ch in range(4):
            smp[pos, 163 + ch] = float(pos // 2 == ch)
    return dict(c_oh=oh, c_cinv=cinv, c_winv=winv, c_pow2=pow2, c_thr=thr, c_iota=iota, c_smp=smp)


def core_inputs(inp, core, NS, consts):
    b = core % 4
    f = lambda a: np.ascontiguousarray(a)
    sl = slice(core * NS, (core + 1) * NS)
    m = {}
    m["xp"] = f(inp["x_prompt"][b])
    m["xs"] = f(inp["x_sample"][sl, 0])
    m["memp"] = f(inp["mem_prompt"][b])
    ck = inp["cache_k"]
    m["ck"] = ck.reshape(ck.shape[0] * ck.shape[1] * ck.shape[2], 256)
    cv = inp["cache_v"]
    m["cv"] = cv.reshape(cv.shape[0] * cv.shape[1] * cv.shape[2], 256)
    ci = inp["cache_idx_k"]
    m["cik"] = ci.reshape(ci.shape[0] * ci.shape[1] * ci.shape[2], 32)
    m["cmk"] = f(inp["cache_mem_k"][:, sl].reshape(2, NS, 256, 256))
    m["cmv"] = f(inp["cache_mem_v"][:, sl].reshape(2, NS, 256, 256))
    m["spool"] = f(inp["state_pool"][:, sl])
    m["sconv"] = f(inp["state_conv"][:, sl])
    m["ptab"] = f(inp["page_table"][sl].astype(np.int32))
    for k in ("rel_bias", "norm1_g", "w_in", "q_norm_g", "k_norm_g", "mem_norm_g", "w_mem_kv", "mq_norm_g", "mk_norm_g",
              "w_pool", "pool_scale", "w_out", "norm2_g", "w_up", "conv_w", "conv_b", "w_down"):
        m[k] = f(inp[k])
    m.update(consts)
    return m


_CACHE = {}


def run(inp, n_cores=8, NS=4, with_sample_dsa=True):
    inp = {k: np.asarray(v) for k, v in inp.items()}
    S = inp["x_prompt"].shape[1]
    NPG = inp["page_table"].shape[1]
    NPOOL = inp["cache_k"].shape[1]
    key = (S, NPG, NPOOL, NS, with_sample_dsa)
    if key not in _CACHE:
        _CACHE[key] = K(S, NPG, NPOOL, NS, with_sample_dsa).build()
    nc = _CACHE[key]
    consts = make_consts(S, NPG)
    maps = [core_inputs(inp, c, NS, consts) for c in range(n_cores)]
    res = run_bass_kernel_spmd(nc, maps, core_ids=list(range(n_cores))).results
    B = inp["x_prompt"].shape[0]
    nb = min(B, n_cores)
    DB = n_cores * NS
    st = lambda name, shp: np.stack([res[c][name] for c in range(nb)], axis=0)
    y_p = st("y_p", None)
    y_s = np.concatenate([res[c]["y_s"] for c in range(n_cores)], axis=0)[:, None, :]
    k_p = st("k_p", None).transpose(1, 0, 2, 3).reshape(2, nb, S, 4, 64)
    v_p = st("v_p", None).transpose(1, 0, 2, 3).reshape(2, nb, S, 4, 64)
    ik_p = st("ik_p", None).transpose(1, 0, 2, 3)
    pool_p = st("pool_p", None).transpose(1, 0, 2, 3)
    conv_p = st("conv_p", None).transpose(1, 0, 2, 3)
    mk_p = st("mk_p", None).transpose(1, 0, 2, 3).reshape(2, nb, 256, 4, 64)
    mv_p = st("mv_p", None).transpose(1, 0, 2, 3).reshape(2, nb, 256, 4, 64)
    cat = lambda name: np.concatenate([res[c][name] for c in range(n_cores)], axis=1)
    k_s = cat("k_s").reshape(2, DB, 1, 4, 64)
    v_s = cat("v_s").reshape(2, DB, 1, 4, 64)
    ik_s = cat("ik_s").reshape(2, DB, 1, 32)
    pool_s = cat("pool_s")
    conv_s = cat("conv_s")
    outs = (y_p, y_s, k_p, v_p, ik_p, pool_p, conv_p, mk_p, mv_p, k_s, v_s, ik_s, pool_s, conv_s)
    return tuple(np.ascontiguousarray(o.astype(np.float32)) for o in outs)


def kernel(**inputs):
    return run(inputs, n_cores=8, NS=4)
```

```python
import math
import numpy as np
from contextlib import ExitStack
import concourse.bass as bass
import concourse.mybir as mybir
from concourse.bass_utils import run_bass_kernel_spmd

F32 = mybir.dt.float32
BF16 = mybir.dt.bfloat16
I32 = mybir.dt.int32
U32 = mybir.dt.uint32
AF = mybir.ActivationFunctionType
OP = mybir.AluOpType
AX = mybir.AxisListType

D = 1024
DFF = 2816
NIN = 1832
EPS = 1e-6
SHIFT = 8.0
NEG = -1.0e30
NIT = 16
ENGS = ["pe", "act", "dve", "pool", "sp"]
QPERM = [0, 2, 1, 3, 4, 6, 5, 7]
HCLAMP = 1664
HLEN = HCLAMP + 128
TABLEN = HLEN + 128
IXENG = "dve"
MKENG = "pool"


class Buf:
    def __init__(self, name, t=None):
        self.name = name
        self.t = t
        self.w = {}
        self.rs = {}

    def __getitem__(self, k):
        return self.t[k]


class Ring:
    def __init__(self, bufs):
        self.bufs = bufs
        self.i = 0

    def next(self):
        b = self.bufs[self.i % len(self.bufs)]
        self.i += 1
        return b


class Sched:
    def __init__(self, nc, es):
        self.nc = nc
        self.es_global = es
        self.es = es
        self.q = {e: [] for e in ENGS}
        self.cnt = {}
        self.sems = {}
        self.seen = {e: {} for e in ENGS}
        self.uid = 0
        for e in ENGS:
            self._sem("E_" + e)

    def _sem(self, key):
        if key not in self.sems:
            self.sems[key] = self.es_global.enter_context(self.nc.semaphore("s_" + key))
            self.cnt[key] = 0
        return self.sems[key]

    def sb(self, name, shape, dt=F32):
        self.uid += 1
        t = self.es.enter_context(self.nc.sbuf_tensor("%s_%d" % (name, self.uid), list(shape), dt))
        return Buf(name, t)

    def ps(self, name, shape, dt=F32):
        self.uid += 1
        t = self.es.enter_context(self.nc.psum_tensor("%s_%d" % (name, self.uid), list(shape), dt))
        return Buf(name, t)

    def _deps(self, eng, R, W):
        deps = {}

        def add(d):
            for k, v in d.items():
                if deps.get(k, 0) < v:
                    deps[k] = v
        for b in R:
            add(b.w)
        for b in W:
            add(b.w)
            add(b.rs)
        out = []
        for k, v in deps.items():
            if eng == "pe" and k == "E_pe":
                continue
            if self.seen[eng].get(k, 0) >= v:
                continue
            self.seen[eng][k] = v
            out.append((k, v))
        return out

    def _commit(self, tok, R, W):
        k, v = tok
        for b in R:
            if b.rs.get(k, 0) < v:
                b.rs[k] = v
        for b in W:
            if b.w.get(k, 0) < v:
                b.w[k] = v
            b.rs = {}

    def op(self, eng, fn, R=(), W=()):
        waits = self._deps(eng, R, W)
        key = "E_" + eng
        self.cnt[key] += 1
        tok = (key, self.cnt[key])
        self.q[eng].append((waits, fn, key, 1))
        self._commit(tok, R, W)

    def dma(self, eng, fn, R, W, key):
        key = "D_" + key
        self._sem(key)
        waits = self._deps(eng, R, W)
        self.cnt[key] += 16
        tok = (key, self.cnt[key])
        self.q[eng].append((waits, fn, key, 16))
        self._commit(tok, R, W)

    def barrier(self):
        for e in ENGS:
            waits = []
            for k, v in self.cnt.items():
                if v == 0:
                    continue
                if self.seen[e].get(k, 0) >= v:
                    continue
                self.seen[e][k] = v
                waits.append((k, v))
            self.q[e].append((waits, None, None, 0))

    def emit(self):
        nc = self.nc
        qs = self.q
        self.q = {e: [] for e in ENGS}
        sems = self.sems
        with nc.Block() as block:
            def run(e, items):
                for waits, fn, key, inc in items:
                    for k, v in waits:
                        e.wait_ge(sems[k], v)
                    if fn is not None:
                        ins = fn(e)
                        ins.then_inc(sems[key], inc)

            @block.tensor
            def _(e):
                run(e, qs["pe"])

            @block.scalar
            def _(e):
                run(e, qs["act"])

            @block.vector
            def _(e):
                run(e, qs["dve"])

            @block.gpsimd
            def _(e):
                run(e, qs["pool"])

            @block.sync
            def _(e):
                run(e, qs["sp"])


def t5_bucket_np(d):
    d = np.asarray(d, dtype=np.int64)
    dd = np.maximum(d, 1).astype(np.float32)
    large = 16 + (np.log(dd / np.float32(16)) / np.float32(math.log(2048 / 16)) * np.float32(16)).astype(np.int32)
    large = np.minimum(large, 31)
    return np.where(d < 16, d, large)


class K:
    def __init__(self, S, NPG, NPOOL, NS=4, with_sample_dsa=True):
        self.S, self.NPG, self.NPOOL, self.NS = S, NPG, NPOOL, NS
        self.NT = S // 128
        self.KTOP = min(256, S // 4)
        self.L_s = NPG * 128 + 1
        self.KTOP_S = min(256, self.L_s // 4)
        self.with_sample_dsa = with_sample_dsa
        import os
        self.dbg = int(os.environ.get('KDBG', '99'))

    def mm(self, ob, oap, lb, lap, rb, rap, st, sp):
        self.s.op("pe", lambda e: e.matmul(oap, lhsT=lap, rhs=rap, start=st, stop=sp), R=[lb, rb], W=[ob])

    def tr(self, ob, oap, ib, iap, idb, idap):
        self.s.op("pe", lambda e: e.transpose(out=oap, in_=iap, identity=idap), R=[ib, idb], W=[ob])

    def act(self, ob, oap, ib, iap, func, bias=None, scale=1.0, R=(), accum=None, W=()):
        def fn(e):
            kw = {}
            if bias is not None:
                kw["bias"] = bias
            if accum is not None:
                kw["accum_out"] = accum
            return e.activation(out=oap, in_=iap, func=func, scale=scale, **kw)
        self.s.op("act", fn, R=[ib] + list(R), W=[ob] + list(W))

    def acopy(self, ob, oap, ib, iap):
        self.s.op("act", lambda e: e.copy(out=oap, in_=iap), R=[ib], W=[ob])

    def vcopy(self, ob, oap, ib, iap, eng="dve"):
        self.s.op(eng, lambda e: e.tensor_copy(out=oap, in_=iap), R=[ib], W=[ob])

    def tt(self, ob, oap, ab, aap, bb, bap, op, eng="dve"):
        self.s.op(eng, lambda e: e.tensor_tensor(out=oap, in0=aap, in1=bap, op=op), R=[ab, bb], W=[ob])

    def ts(self, ob, oap, ib, iap, s1, s2, op0, op1=None, R=(), accum=None, W=(), eng="dve"):
        def fn(e):
            kw = {}
            if op1 is not None:
                kw["op1"] = op1
            if accum is not None:
                kw["accum_out"] = accum
            return e.tensor_scalar(out=oap, in0=iap, scalar1=s1, scalar2=s2, op0=op0, **kw)
        self.s.op(eng, fn, R=[ib] + list(R), W=[ob] + list(W))

    def stt(self, ob, oap, ab, aap, sc, bb, bap, op0, op1, R=(), eng="dve"):
        self.s.op(eng, lambda e: e.scalar_tensor_tensor(out=oap, in0=aap, scalar=sc, in1=bap, op0=op0, op1=op1),
                  R=[ab, bb] + list(R), W=[ob])

    def memset(self, b, ap, val, eng="pool"):
        self.s.op(eng, lambda e: e.memset(ap, val), R=[], W=[b])

    def dma(self, q, ob, oap, ib, iap, key=None, R=(), W=()):
        self.s.dma(q, lambda e: e.dma_start(out=oap, in_=iap), R=[ib] + list(R), W=[ob] + list(W), key=key or ob.name)

    def build(self):
        S, NT, NS, NPG = self.S, self.NT, self.NS, self.NPG
        nc = bass.Bass("TRN2", target_bir_lowering=False)
        self.nc = nc
        NROWS = self.NPOOL * 128

        def din(name, shape, dt=F32):
            return Buf(name, nc.dram_tensor(name, list(shape), dt, kind="ExternalInput").ap())

        def dout(name, shape, dt=F32):
            return Buf(name, nc.dram_tensor(name, list(shape), dt, kind="ExternalOutput").ap())

        def dscr(name, shape, dt=F32):
            return Buf(name, nc.dram_tensor(name, list(shape), dt, kind="Internal").ap())

        I = {}
        I["xp"] = din("xp", [S, D])
        I["xs"] = din("xs", [NS, D])
        I["memp"] = din("memp", [256, D])
        I["ck"] = din("ck", [2 * NROWS, 256])
        I["cv"] = din("cv", [2 * NROWS, 256])
        I["cik"] = din("cik", [2 * NROWS, 32])
        I["cmk"] = din("cmk", [2, NS, 256, 256])
        I["cmv"] = din("cmv", [2, NS, 256, 256])
        I["spool"] = din("spool", [2, NS, 15, 256])
        I["sconv"] = din("sconv", [2, NS, 2, 2 * DFF])
        I["ptab"] = din("ptab", [NS, NPG], I32)
        I["rel_bias"] = din("rel_bias", [32, 8])
        I["norm1_g"] = din("norm1_g", [2, D])
        I["w_in"] = din("w_in", [2, D, NIN])
        I["q_norm_g"] = din("q_norm_g", [2, 64])
        I["k_norm_g"] = din("k_norm_g", [2, 64])
        I["mem_norm_g"] = din("mem_norm_g", [2, D])
        I["w_mem_kv"] = din("w_mem_kv", [2, D, 512])
        I["mq_norm_g"] = din("mq_norm_g", [2, 64])
        I["mk_norm_g"] = din("mk_norm_g", [2, 64])
        I["w_pool"] = din("w_pool", [2, 4, 64, 64])
        I["pool_scale"] = din("pool_scale", [2, 256])
        I["w_out"] = din("w_out", [2, D, D])
        I["norm2_g"] = din("norm2_g", [2, D])
        I["w_up"] = din("w_up", [2, D, 2 * DFF])
        I["conv_w"] = din("conv_w", [2, 3, 2 * DFF])
        I["conv_b"] = din("conv_b", [2, 2 * DFF])
        I["w_down"] = din("w_down", [2, DFF, D])
        I["c_oh"] = din("c_oh", [32, TABLEN])
        I["c_cinv"] = din("c_cinv", [128, 2, 128])
        I["c_winv"] = din("c_winv", [128, 2])
        I["c_pow2"] = din("c_pow2", [128, NIT + 1])
        I["c_thr"] = din("c_thr", [128, 31])
        I["c_iota"] = din("c_iota", [128, 128])
        I["c_smp"] = din("c_smp", [128, 167])
        self.I = I
        O = {}
        O["y_p"] = dout("y_p", [S, D])
        O["y_s"] = dout("y_s", [NS, D])
        O["k_p"] = dout("k_p", [2, S, 256])
        O["v_p"] = dout("v_p", [2, S, 256])
        O["ik_p"] = dout("ik_p", [2, S, 32])
        O["pool_p"] = dout("pool_p", [2, 15, 256])
        O["conv_p"] = dout("conv_p", [2, 2, 2 * DFF])
        O["mk_p"] = dout("mk_p", [2, 256, 256])
        O["mv_p"] = dout("mv_p", [2, 256, 256])
        O["k_s"] = dout("k_s", [2, NS, 256])
        O["v_s"] = dout("v_s", [2, NS, 256])
        O["ik_s"] = dout("ik_s", [2, NS, 32])
        O["pool_s"] = dout("pool_s", [2, NS, 15, 256])
        O["conv_s"] = dout("conv_s", [2, NS, 2, 2 * DFF])
        self.O = O
        self.XA = dscr("XA", [D, S])
        self.XB = dscr("XB", [D, S])
        self.TAB = dscr("TAB", [8, TABLEN])

        with ExitStack() as esg:
            s = Sched(nc, esg)
            self.s = s
            self.setup_consts()
            for l in range(2):
                if self.dbg <= 1:
                    break
                with ExitStack() as es1:
                    s.es = es1
                    self.phase1(l)
                    s.barrier()
                    s.emit()
                if self.dbg <= 8:
                    break
                with ExitStack() as es2:
                    s.es = es2
                    self.phase2(l)
                    s.barrier()
                    s.emit()
                s.es = esg
        return nc

    def setup_consts(self):
        s, I = self.s, self.I
        c = {}
        self.c = c
        c["ident"] = s.sb("ident", [128, 128], F32)
        c["identb"] = s.sb("identb", [128, 128], BF16)
        c["J"] = s.sb("J", [128, 128], BF16)
        c["ones"] = s.sb("ones", [128, 128], F32)
        c["onesb"] = s.sb("onesb", [128, 128], BF16)
        c["bones"] = s.sb("bones", [128, 128], F32)
        c["cm"] = s.sb("cm", [128, 128], F32)
        c["eps"] = s.sb("eps", [128, 1], F32)
        c["nshift"] = s.sb("nshift", [128, 1], F32)
        c["pow2"] = s.sb("pow2", [128, NIT + 1], F32)
        c["cinv"] = s.sb("cinv", [128, 2, 128], F32)
        c["winv"] = s.sb("winv", [128, 2], F32)
        c["xsT"] = s.sb("xsT", [128, 8, self.NS], F32)
        est = ExitStack()
        s.es = est
        ident, J = c["ident"], c["J"]
        self.memset(ident, ident[:], 0.0)
        s.op("pool", lambda e: e.affine_select(out=ident[:], in_=ident[:], pattern=[[-1, 128]], compare_op=OP.not_equal,
                                               fill=1.0, base=0, channel_multiplier=1), R=[ident], W=[ident])
        self.vcopy(c["identb"], c["identb"][:], ident, ident[:], eng="pool")
        jf = s.sb("jf", [128, 128], F32)
        self.memset(jf, jf[:], 0.0)
        s.op("pool", lambda e: e.affine_select(out=jf[:], in_=jf[:], pattern=[[1, 128]], compare_op=OP.not_equal,
                                               fill=1.0, base=-127, channel_multiplier=1), R=[jf], W=[jf])
        self.vcopy(J, J[:], jf, jf[:], eng="pool")
        self.memset(c["ones"], c["ones"][:], 1.0)
        self.memset(c["onesb"], c["onesb"][:], 1.0)
        bo = c["bones"]
        self.memset(bo, bo[:], 0.0)
        self.memset(bo, bo[0:64, 0:64], 1.0)
        self.memset(bo, bo[64:128, 64:128], 1.0)
        cm = c["cm"]
        self.memset(cm, cm[:], 0.0)
        s.op("pool", lambda e: e.affine_select(out=cm[:], in_=cm[:], pattern=[[-1, 128]], compare_op=OP.is_ge,
                                               fill=NEG, base=0, channel_multiplier=1), R=[cm], W=[cm])
        self.memset(c["eps"], c["eps"][:], EPS)
        self.memset(c["nshift"], c["nshift"][:], -SHIFT)
        self.dma("sp", c["pow2"], c["pow2"][:], I["c_pow2"], I["c_pow2"][:, :])
        self.dma("sp", c["cinv"], c["cinv"][:], I["c_cinv"], I["c_cinv"][:, :, :])
        self.dma("sp", c["winv"], c["winv"][:], I["c_winv"], I["c_winv"][:, :])
        rb = s.sb("rb", [32, 8], F32)
        oh = s.sb("oh", [32, TABLEN], F32)
        tabs = s.sb("tabs", [8, TABLEN], F32)
        self.dma("sp", rb, rb[:], I["rel_bias"], I["rel_bias"][:, :])
        self.dma("sp", oh, oh[:], I["c_oh"], I["c_oh"][:, :])
        pt = s.ps("ptab", [128, 512], F32)
        for c0 in range(0, TABLEN, 512):
            n = min(512, TABLEN - c0)
            self.mm(pt, pt[0:8, 0:n], rb, rb[:], oh, oh[:, c0:c0 + n], True, True)
            self.vcopy(tabs, tabs[:, c0:c0 + n], pt, pt[0:8, 0:n])
        self.dma("sp", self.TAB, self.TAB[:, :], tabs, tabs[:], key="tabst")
        xs_tok = s.sb("xs_tok", [self.NS, D], F32)
        self.dma("sp", xs_tok, xs_tok[:], I["xs"], I["xs"][:, :])
        for k in range(8):
            self.tr(pt, pt[:, k * 4:k * 4 + self.NS], xs_tok, xs_tok[:, k * 128:(k + 1) * 128], ident, ident[0:self.NS, 0:self.NS])
        self.vcopy(c["xsT"], c["xsT"][:].rearrange("p k s -> p (k s)"), pt, pt[:, 0:8 * self.NS])
        s.barrier()
        s.emit()
        est.close()
        s.es = s.es_global

    def rms_feature_major(self, xb, xap3, n, gvec, hT, P_N, tmp_sq, tmp_rs):
        c = self.c
        sq, rs = tmp_sq, tmp_rs
        self.act(sq, sq[:, 0:8, 0:n], xb, xap3, AF.Square)
        for k in range(8):
            self.mm(P_N, P_N[:, 0:n], c["ones"], c["ones"][:], sq, sq[:, k, 0:n], k == 0, k == 7)
        self.act(rs, rs[:, 0:n], P_N, P_N[:, 0:n], AF.Sqrt, bias=c["eps"][:, 0:1], scale=1.0 / D, R=[c["eps"]])
        self.s.op("dve", lambda e: e.reciprocal(out=rs[:, 0:n], in_=rs[:, 0:n]), R=[rs], W=[rs])
        for k in range(8):
            self.stt(hT, hT[:, k, 0:n], xb, xap3[:, k, :], gvec[:, k:k + 1], rs, rs[:, 0:n], OP.mult, OP.mult, R=[gvec])

    def headnorm(self, P_Z, zap, ncols, P_N, sq, rs):
        c = self.c
        self.act(sq, sq[:, 0:ncols], P_Z, zap, AF.Square)
        self.mm(P_N, P_N[:, 0:ncols], c["bones"], c["bones"][:], sq, sq[:, 0:ncols], True, True)
        self.act(rs, rs[:, 0:ncols], P_N, P_N[:, 0:ncols], AF.Sqrt, bias=c["eps"][:, 0:1], scale=1.0 / 64, R=[c["eps"]])
        self.s.op("dve", lambda e: e.reciprocal(out=rs[:, 0:ncols], in_=rs[:, 0:ncols]), R=[rs], W=[rs])
        return rs[:, 0:ncols]

    def load_gvec(self, name, src, l, scale=None):
        g = self.s.sb(name, [128, 8], F32)
        ap = bass.AP(src.t.tensor, l * D, [[1, 128], [128, 8]])
        self.s.dma("sp", lambda e: e.dma_start(out=g[:], in_=ap, allow_slow_non_contiguous=True), R=[src], W=[g], key=name)
        return g

    def load_hvec(self, name, src, l, scale=1.0):
        g = self.s.sb(name, [128, 1], F32)
        for h in range(2):
            ap = bass.AP(src.t.tensor, l * 64, [[1, 64], [1, 1]])
            self.s.dma("sp", lambda e, ap=ap, h=h: e.dma_start(out=g[h * 64:(h + 1) * 64, :], in_=ap), R=[src], W=[g], key=name + str(h))
        if scale != 1.0:
            self.ts(g, g[:], g, g[:], scale, None, OP.mult)
        return g

    def phase1(self, l):
        s, I, O, c = self.s, self.I, self.O, self.c
        S, NT, NS = self.S, self.NT, self.NS
        W = {}
        wsrc = I["w_in"]
        win3 = wsrc[l].rearrange("(k p) c -> p k c", p=128)
        wf = s.sb("wf", [128, 8, 14 * 128], BF16)
        W["wf"] = wf
        for pos, h in enumerate(QPERM):
            self.dma("pool", wf, wf[:, :, pos * 64:(pos + 1) * 64], wsrc, win3[:, :, h * 64:(h + 1) * 64], key=None)
        self.dma("pool", wf, wf[:, :, 512:768], wsrc, win3[:, :, 512:768], key=None)
        self.dma("pool", wf, wf[:, :, 768:1024], wsrc, win3[:, :, 1576:1832], key=None)
        self.memset(wf, wf[:, :, 1024:1408], 0.0, eng="dve")
        for h in range(8):
            d0 = 1024 + (h // 3) * 128 + (h % 3) * 32
            self.dma("pool", wf, wf[:, :, d0:d0 + 32], wsrc, win3[:, :, 1024 + h * 32:1024 + (h + 1) * 32], key=None)
        for r in range(4):
            self.dma("pool", wf, wf[:, :, 1408 + r * 32:1408 + (r + 1) * 32], wsrc, win3[:, :, 1280:1312], key=None)
        self.dma("pool", wf, wf[:, :, 1536:1792], wsrc, win3[:, :, 1320:1576], key=None)
        wt = s.sb("wt", [128, 8, 296], BF16)
        W["wt"] = wt
        self.dma("pool", wt, wt[:, :, 0:256], wsrc, win3[:, :, 768:1024], key=None)
        self.dma("pool", wt, wt[:, :, 256:296], wsrc, win3[:, :, 1280:1320], key=None)
        wo = s.sb("wo", [128, 8, D], BF16)
        wm3 = I["w_mem_kv"][l].rearrange("(k p) c -> p k c", p=128)
        self.dma("pool", wo, wo[:, :, 0:512], I["w_mem_kv"], wm3, key=None)
        g1 = self.load_gvec("g1", I["norm1_g"], l)
        gm = self.load_gvec("gm", I["mem_norm_g"], l)
        gq = self.load_hvec("gq", I["q_norm_g"], l, scale=0.125)
        gk = self.load_hvec("gk", I["k_norm_g"], l)
        gmq = self.load_hvec("gmq", I["mq_norm_g"], l, scale=0.125)
        gmk = self.load_hvec("gmk", I["mk_norm_g"], l)
        psc = s.sb("psc", [128, 2], F32)
        self.s.dma("sp", lambda e: e.dma_start(out=psc[:], in_=bass.AP(I["pool_scale"].t.tensor, l * 256, [[1, 128], [128, 2]]),
                                               allow_slow_non_contiguous=True), R=[I["pool_scale"]], W=[psc], key="psc")
        wpb = s.sb("wpb", [128, 2, 128], F32)
        self.memset(wpb, wpb[:], 0.0)
        for g in range(4):
            cc, hh = g // 2, g % 2
            self.dma("sp", wpb, wpb[hh * 64:(hh + 1) * 64, cc, hh * 64:(hh + 1) * 64], I["w_pool"], I["w_pool"][l, g, :, :], key="wpb%d" % g)
        P = [s.ps("P%d" % i, [128, 512], F32) for i in range(8)]
        self.P = P
        mkT = s.sb("mkT", [128, 2, 256], BF16)
        mvb = s.sb("mvb", [128, 2, 256], BF16)
        T = {}
        T["xT2"] = [s.sb("xT%d" % i, [128, 8, 128], F32) for i in range(2)]
        T["xT"] = T["xT2"][0]
        T["rs1"] = s.sb("rs1", [128, 128], F32)
        T["hT"] = s.sb("hT", [128, 8, 128], BF16)
        T["sq"] = s.sb("sq", [128, 512], F32)
        T["rs"] = s.sb("rs", [128, 512], F32)
        T["qT2"] = [s.sb("qT%d" % i, [128, 4, 128], BF16) for i in range(2)]
        T["qT"] = T["qT2"][0]
        T["mqT"] = s.sb("mqT", [128, 2, 128], BF16)
        T["mqm2"] = [s.sb("mqm%d" % i, [128, 2, 2, 128], BF16) for i in range(2)]
        for mq_ in T["mqm2"]:
            self.memset(mq_, mq_[:], 0.0)
        T["mqm"] = T["mqm2"][0]
        T["iqT"] = s.sb("iqT", [128, 3, 128], BF16)
        T["kn32"] = s.sb("kn32", [128, 2, 128], F32)
        T["ktok"] = s.sb("ktok", [128, 256], F32)
        T["tm"] = s.sb("tm", [128, 296], F32)
        T["iw16"] = s.sb("iw16", [128, 8], F32)
        T["mixT"] = s.sb("mixT", [128, 8, 128], BF16)
        T["x1"] = s.sb("x1", [128, 8, 128], F32)
        T["r"] = Ring([s.sb("r%d" % i, [128, 512], F32) for i in range(2)])
        T["pb"] = Ring([s.sb("pb%d" % i, [128, 512], BF16) for i in range(2)])
        T["pm"] = Ring([s.sb("pm%d" % i, [128, 512], BF16) for i in range(2)])
        T["rden"] = s.sb("rden", [128, 512], F32)
        T["bs"] = s.sb("bs", [128, 8], F32)
        T["dd"] = s.sb("dd", [128, NIT + 1], F32)
        T["pl"] = [s.sb("pl%d" % i, [128, 2, 143], F32) for i in range(2)]
        T["pooled"] = s.sb("pooled", [128, 2, 128], F32)
        self.T = T
        self.Wt = W
        esp = ExitStack()
        es_phase = s.es
        s.es = esp
        H = s.sb("H", [128, 8, HLEN], BF16)
        c["H"] = H
        for h in range(8):
            hsrc = bass.AP(self.TAB.t.tensor, h * TABLEN, [[1, 128], [1, HLEN]])
            self.dma("pool", H, H[:, h, :], self.TAB, hsrc, key="Hld")
        kT = s.sb("kT", [128, 2, S], BF16)
        vc = s.sb("vc", [128, NT, 256], BF16)
        ikT = s.sb("ikT", [128, S], BF16)
        Sc = s.sb("Sc", [128, S], F32)
        mb = s.sb("mb", [128, S], BF16)
        maskT = s.sb("maskT", [128, NT, 128], BF16)
        ubuf2 = [s.sb("ubuf%d" % i, [128, 2, 143], F32) for i in range(2)]
        for ub_ in ubuf2:
            self.memset(ub_, ub_[:], 0.0)
        kTt = [Buf("kTt%d" % i, kT.t) for i in range(NT)]
        vct = [Buf("vct%d" % i, vc.t) for i in range(NT)]
        ikTt = [Buf("ikTt%d" % i, ikT.t) for i in range(NT)]
        T["xtok"] = s.sb("xtok", [128, D], F32)
        self.lay = dict(l=l, g1=g1, gq=gq, gk=gk, gmq=gmq, gmk=gmk, psc=psc, wpb=wpb, wo=wo, kT=kT, vc=vc, ikT=ikT, Sc=Sc,
                        mb=mb, maskT=maskT, mkT=mkT, mvb=mvb, ubuf2=ubuf2, kTt=kTt, vct=vct, ikTt=ikTt)

        self.mem_kv(l, gm, gmk)
        wo_src = I["w_out"]
        for j in range(4):
            for hh in range(2):
                h = QPERM[2 * j + hh]
                self.dma("pool", wo, wo[hh * 64:(hh + 1) * 64, j, :], wo_src, wo_src[l, h * 64:(h + 1) * 64, :], key=None)
        self.dma("pool", wo, wo[:, 4:8, :], wo_src, wo_src[l, 512:1024, :].rearrange("(k p) c -> p k c", p=128), key=None)

        for (_c, fn, _t) in self.prompt_A(l, 0):
            fn()
        for t in range(NT):
            UB = self.prompt_B(l, t)
            UA = self.prompt_A(l, t + 1) if t + 1 < NT else []
            self.merge_run(UB, UA)
        s.barrier()
        s.emit()
        esp.close()
        s.es = es_phase
        for kdead in ("kT", "vc", "ikT", "Sc", "mb", "maskT", "ubuf2", "kTt", "vct", "ikTt"):
            self.lay.pop(kdead)
        T.pop("xtok")
        T["xT"], T["qT"], T["mqm"] = T["xT2"][0], T["qT2"][0], T["mqm2"][0]
        if self.dbg <= 8:
            return
        self.sample_tile(l)

    def mem_kv(self, l, gm, gmk):
        s, I, O, c, T, P = self.s, self.I, self.O, self.c, self.T, self.P
        lay = self.lay
        wo, mkT, mvb = lay["wo"], lay["mkT"], lay["mvb"]
        xtok, xT = T["xtok"], T["xT"]
        for i in range(2):
            self.dma("sp", xtok, xtok[:], I["memp"], I["memp"][i * 128:(i + 1) * 128, :])
            for k0 in (0, 4):
                for k in range(k0, k0 + 4):
                    self.tr(P[7], P[7][:, (k - k0) * 128:(k - k0 + 1) * 128], xtok, xtok[:, k * 128:(k + 1) * 128], c["ident"], c["ident"][:])
                self.acopy(xT, xT[:, k0:k0 + 4, :], P[7], P[7][:, 0:512].rearrange("p (k t) -> p k t", k=4))
            self.rms_feature_major(xT, xT[:], 128, gm, T["hT"], P[2], T["x1"], T["rs1"])
            hT = T["hT"]
            for g in range(4):
                for k in range(8):
                    self.mm(P[0], P[0][:, g * 128:(g + 1) * 128], wo, wo[:, k, g * 128:(g + 1) * 128], hT, hT[:, k, :], k == 0, k == 7)
            rs = self.headnorm(P[0], P[0][:, 0:256], 256, P[2], T["sq"], T["rs"])
            kn = T["kn32"]
            self.stt(kn, kn[:].rearrange("p c t -> p (c t)"), P[0], P[0][:, 0:256], gmk[:, 0:1], T["rs"], rs, OP.mult, OP.mult, R=[gmk])
            self.acopy(mkT, mkT[:, :, i * 128:(i + 1) * 128], kn, kn[:])
            for cc in range(2):
                self.tr(P[7], P[7][:, cc * 128:(cc + 1) * 128], kn, kn[:, cc, :], c["ident"], c["ident"][:])
            self.vcopy(T["ktok"], T["ktok"][:], P[7], P[7][:, 0:256])
            self.dma("sp", O["mk_p"], O["mk_p"][l, i * 128:(i + 1) * 128, :], T["ktok"], T["ktok"][:], key="ktok")
            vn = T["x1"]
            self.acopy(vn, vn[:, 0:2, :].rearrange("p c t -> p (c t)"), P[0], P[0][:, 256:512])
            for cc in range(2):
                self.tr(P[7], P[7][:, 256 + cc * 128:256 + (cc + 1) * 128], vn, vn[:, cc, :], c["ident"], c["ident"][:])
            self.vcopy(T["tm"], T["tm"][:, 0:256], P[7], P[7][:, 256:512])
            self.dma("sp", O["mv_p"], O["mv_p"][l, i * 128:(i + 1) * 128, :], T["tm"], T["tm"][:, 0:256], key="tm")
            self.acopy(mvb, mvb[:, i, :], T["tm"], T["tm"][:, 0:256])

    def project(self, l, xb, xap3, n, P):
        s, c, T, lay, Wt = self.s, self.c, self.T, self.lay, self.Wt
        wf, wt = Wt["wf"], Wt["wt"]
        hT = T["hT"]
        self.rms_feature_major(xb, xap3, n, lay["g1"], hT, P[2], T["x1"], T["rs1"])

        def fm(Pb, slot, grp):
            for k in range(8):
                self.mm(Pb, Pb[:, slot * 128:slot * 128 + n], wf, wf[:, k, grp * 128:(grp + 1) * 128], hT, hT[:, k, 0:n], k == 0, k == 7)
        for j in range(4):
            fm(P[0], j, j)
        for j in range(4):
            fm(P[1], j, 4 + j)
        qT, mqT, kn = T["qT"], T["mqT"], T["kn32"]
        if n == 128:
            rs = self.headnorm(P[0], P[0][:, 0:512], 512, P[2], T["sq"], T["rs"])
            self.stt(qT, qT[:].rearrange("p c t -> p (c t)"), P[0], P[0][:, 0:512], lay["gq"][:, 0:1], T["rs"], rs, OP.mult, OP.mult, R=[lay["gq"]])
            rs = self.headnorm(P[1], P[1][:, 0:512], 512, P[2], T["sq"], T["rs"])
            self.stt(kn, kn[:].rearrange("p c t -> p (c t)"), P[1], P[1][:, 0:256], lay["gk"][:, 0:1], T["rs"], rs[:, 0:256], OP.mult, OP.mult, R=[lay["gk"]])
            self.stt(mqT, mqT[:].rearrange("p c t -> p (c t)"), P[1], P[1][:, 256:512], lay["gmq"][:, 0:1], T["rs"], rs[:, 256:512], OP.mult, OP.mult, R=[lay["gmq"]])
        else:
            for (Pb, nslot) in ((P[0], 4), (P[1], 4)):
                pass
            sq, rsb = T["sq"], T["rs"]
            for Pb in (P[0], P[1]):
                for j in range(4):
                    self.act(sq, sq[:, j * 128:j * 128 + n], Pb, Pb[:, j * 128:j * 128 + n], AF.Square)
                    self.mm(P[2], P[2][:, j * 128:j * 128 + n], c["bones"], c["bones"][:], sq, sq[:, j * 128:j * 128 + n], True, True)
                for j in range(4):
                    js = slice(j * 128, j * 128 + n)
                    self.act(rsb, rsb[:, js], P[2], P[2][:, js], AF.Sqrt, bias=c["eps"][:, 0:1], scale=1.0 / 64, R=[c["eps"]])
                    self.s.op("dve", lambda e, js=js: e.reciprocal(out=rsb[:, js], in_=rsb[:, js]), R=[rsb], W=[rsb])
                if Pb is P[0]:
                    for j in range(4):
                        self.stt(qT, qT[:, j, 0:n], Pb, Pb[:, j * 128:j * 128 + n], lay["gq"][:, 0:1], rsb, rsb[:, j * 128:j * 128 + n], OP.mult, OP.mult, R=[lay["gq"]])
                else:
                    for j in range(2):
                        self.stt(kn, kn[:, j, 0:n], Pb, Pb[:, j * 128:j * 128 + n], lay["gk"][:, 0:1], rsb, rsb[:, j * 128:j * 128 + n], OP.mult, OP.mult, R=[lay["gk"]])
                    for j in range(2):
                        self.stt(mqT, mqT[:, j, 0:n], Pb, Pb[:, (2 + j) * 128:(2 + j) * 128 + n], lay["gmq"][:, 0:1], rsb, rsb[:, (2 + j) * 128:(2 + j) * 128 + n], OP.mult, OP.mult, R=[lay["gmq"]])
        mqm = T["mqm"]
        self.vcopy(mqm, mqm[0:64, 0, :, 0:n], mqT, mqT[0:64, :, 0:n], eng="pool")
        self.vcopy(mqm, mqm[64:128, 1, :, 0:n], mqT, mqT[64:128, :, 0:n], eng="pool")
        for j in range(4):
            fm(P[0], j, 8 + j)
        for j in range(2):
            fm(P[1], j, 12 + j)
        iqT = T["iqT"]
        for j in range(3):
            self.acopy(iqT, iqT[:, j, 0:n], P[0], P[0][:, j * 128:j * 128 + n])
        for k in range(8):
            self.mm(P[7], P[7][0:n, 0:296], hT, hT[:, k, 0:n], wt, wt[:, k, :], k == 0, k == 7)
        tm = T["tm"]
        self.vcopy(tm, tm[0:n, :], P[7], P[7][0:n, 0:296])
        self.ts(T["iw16"], T["iw16"][0:n, :], tm, tm[0:n, 288:296], 1.0 / 16.0, None, OP.mult)

    def merge_run(self, UB, UA):
        tb = sum(u[0] for u in UB) or 1.0
        ta = sum(u[0] for u in UA) or 1.0
        ia = ib = 0
        ca = cb = 0.0
        while ia < len(UA) or ib < len(UB):
            pick_a = ib >= len(UB) or (ia < len(UA) and ca / ta < cb / tb)
            if pick_a and UA[ia][2] == "mask":
                while any(u[2] == "att" for u in UB[ib:]):
                    cb += UB[ib][0]
                    UB[ib][1]()
                    ib += 1
            if pick_a:
                ca += UA[ia][0]
                UA[ia][1]()
                ia += 1
            else:
                cb += UB[ib][0]
                UB[ib][1]()
                ib += 1

    def prompt_A(self, l, t):
        s, I, O, c, T, P, lay = self.s, self.I, self.O, self.c, self.T, self.P, self.lay
        S, NT = self.S, self.NT
        U = []
        add = lambda cost, fn, tag=None: U.append((cost, fn, tag))
        p = t % 2
        cols = slice(t * 128, (t + 1) * 128)
        xT = T["xT2"][p]
        ub = lay["ubuf2"][p]
        kT, vc, ikT = lay["kT"], lay["vc"], lay["ikT"]
        kTt, vct, ikTt = lay["kTt"], lay["vct"], lay["ikTt"]
        Sc, mb, maskT = lay["Sc"], lay["mb"], lay["maskT"]

        def load():
            if l == 0:
                xtok = T["xtok"]
                self.dma("sp", xtok, xtok[:], I["xp"], I["xp"][cols, :])
                for k0 in (0, 4):
                    for k in range(k0, k0 + 4):
                        self.tr(P[7], P[7][:, (k - k0) * 128:(k - k0 + 1) * 128], xtok, xtok[:, k * 128:(k + 1) * 128], c["ident"], c["ident"][:])
                    self.acopy(xT, xT[:, k0:k0 + 4, :], P[7], P[7][:, 0:512].rearrange("p (k t) -> p k t", k=4))
            else:
                self.dma("sp", xT, xT[:], self.XB, self.XB[:, cols].rearrange("(k p) t -> p k t", p=128))
        add(6.0, load)

        def proj():
            T["qT"], T["mqm"] = T["qT2"][p], T["mqm2"][p]
            self.project(l, xT, xT[:], 128, P)
        add(40.0, proj)

        def caches():
            kn, tm = T["kn32"], T["tm"]
            s.op("act", lambda e: e.copy(out=kT[:, :, cols], in_=kn[:]), R=[kn], W=[kTt[t]])
            s.op("act", lambda e: e.copy(out=ikT[:, cols], in_=P[0][:, 384:512]), R=[P[0]], W=[ikTt[t]])
            s.op("act", lambda e: e.copy(out=vc[:, t, :], in_=tm[:, 0:256]), R=[tm], W=[vct[t]])
            for cc in range(2):
                self.tr(P[7], P[7][:, cc * 128:(cc + 1) * 128], kn, kn[:, cc, :], c["ident"], c["ident"][:])
            self.vcopy(T["ktok"], T["ktok"][:], P[7], P[7][:, 0:256])
            self.dma("sp", O["k_p"], O["k_p"][l, cols, :], T["ktok"], T["ktok"][:], key="ktok")
            self.dma("sp", O["v_p"], O["v_p"][l, cols, :], tm, tm[:, 0:256], key="tm")
            self.dma("sp", O["ik_p"], O["ik_p"][l, cols, :], tm, tm[:, 256:288], key="tm2")
            self.acopy(ub, ub[:, :, 15:143], P[1], P[1][:, 0:256].rearrange("p (c t) -> p c t", c=2))
            if t > 0:
                ubp = lay["ubuf2"][1 - p]
                self.vcopy(ub, ub[:, :, 0:15], ubp, ubp[:, :, 128:143])
            if t == NT - 1:
                for cc in range(2):
                    self.tr(P[7], P[7][:, 256 + cc * 128:256 + (cc + 1) * 128], ub, ub[:, cc, 15:143], c["ident"], c["ident"][:])
                self.vcopy(T["xtok"], T["xtok"][:, 0:256], P[7], P[7][:, 256:512])
                self.dma("sp", O["pool_p"], O["pool_p"][l, :, :], T["xtok"], T["xtok"][113:128, 0:256], key="xtok_o")
        add(8.0, caches)

        iqT, iw, bs, dd = T["iqT"], T["iw16"], T["bs"], T["dd"]
        Wd = (t + 1) * 128
        chunks = [(c0, min(512, Wd - c0)) for c0 in range(0, Wd, 512)]
        xi = 0
        for h in range(8):
            for (c0, n) in chunks:
                def ix(h=h, c0=c0, n=n, xi=xi):
                    pbs = (h % 3) * 32
                    Px = P[xi % 2]
                    tl = [ikTt[j] for j in range(c0 // 128, (c0 + n) // 128)]
                    s.op("pe", lambda e: e.matmul(Px[:, 0:n], lhsT=iqT[pbs:pbs + 32, h // 3, :], rhs=ikT[pbs:pbs + 32, c0:c0 + n],
                                                  start=True, stop=True), R=[iqT] + tl, W=[Px])
                    r = T["r"].next()
                    self.act(r, r[:, 0:n], Px, Px[:, 0:n], AF.Relu)
                    if h == 0:
                        self.ts(Sc, Sc[:, c0:c0 + n], r, r[:, 0:n], iw[:, 0:1], None, OP.mult, R=[iw], eng=IXENG)
                    else:
                        self.stt(Sc, Sc[:, c0:c0 + n], r, r[:, 0:n], iw[:, h:h + 1], Sc, Sc[:, c0:c0 + n], OP.mult, OP.add, R=[iw], eng=IXENG)
                add(0.15 + 0.8 * n / 512.0, ix)
                xi += 1

        def rng():
            s.op("dve", lambda e: e.tensor_reduce(out=bs[:, 0:1], in_=Sc[:, 0:Wd], axis=AX.X, op=OP.min), R=[Sc], W=[bs])
            s.op("dve", lambda e: e.tensor_reduce(out=bs[:, 1:2], in_=Sc[:, 0:Wd], axis=AX.X, op=OP.max), R=[Sc], W=[bs])
            self.tt(Sc, Sc[:, t * 128:Wd], Sc, Sc[:, t * 128:Wd], c["cm"], c["cm"][:], OP.add)
            self.tt(bs, bs[:, 2:3], bs, bs[:, 1:2], bs, bs[:, 0:1], OP.subtract)
            self.ts(dd, dd[:], c["pow2"], c["pow2"][:], bs[:, 2:3], None, OP.mult, R=[bs])
            self.tt(bs, bs[:, 3:4], bs, bs[:, 0:1], dd, dd[:, 1:2], OP.add)
        add(3.0 + 2.8 * Wd / 1000.0, rng)
        kk = float(self.KTOP) - 0.5
        for i in range(NIT):
            def bis(i=i):
                self.ts(mb, mb[:, 0:Wd], Sc, Sc[:, 0:Wd], bs[:, 3:4], None, OP.is_ge, OP.add, R=[bs], accum=bs[:, 4:5], W=[bs])
                self.ts(bs, bs[:, 5:6], bs, bs[:, 4:5], kk, -0.5, OP.is_ge, OP.add)
                self.stt(bs, bs[:, 3:4], bs, bs[:, 5:6], dd[:, i + 1:i + 2], bs, bs[:, 3:4], OP.mult, OP.add, R=[dd])
            add(1.5 + 1.4 * Wd / 1000.0, bis)

        def fin():
            self.stt(bs, bs[:, 6:7], dd, dd[:, NIT:NIT + 1], -1.0, bs, bs[:, 3:4], OP.mult, OP.add)
            self.ts(mb, mb[:, 0:Wd], Sc, Sc[:, 0:Wd], bs[:, 6:7], None, OP.is_ge, R=[bs])
        add(1.0 + 1.4 * Wd / 1000.0, fin)
        Pm = P[2]
        for j0 in range(0, t + 1, 8):
            def mtr(j0=j0):
                nj = min(8, t + 1 - j0)
                pmv = Pm[:].bitcast(BF16)
                for j in range(j0, j0 + nj):
                    self.tr(Pm, pmv[:, (j - j0) * 128:(j - j0 + 1) * 128], mb, mb[:, j * 128:(j + 1) * 128], c["identb"], c["identb"][:])
                self.acopy(maskT, maskT[:, j0:j0 + nj, :], Pm, pmv[:, 0:nj * 128].rearrange("p (j q) -> p j q", j=nj))
            add(1.5, mtr, "mask")
        return U

    def prompt_B(self, l, t):
        s, I, O, c, T, P, lay = self.s, self.I, self.O, self.c, self.T, self.P, self.lay
        U = []
        add = lambda cost, fn, tag=None: U.append((cost, fn, tag))
        p = t % 2
        cols = slice(t * 128, (t + 1) * 128)
        xT, qT, mqm, ub = T["xT2"][p], T["qT2"][p], T["mqm2"][p], lay["ubuf2"][p]
        kT, vc, maskT = lay["kT"], lay["vc"], lay["maskT"]
        kTt, vct = lay["kTt"], lay["vct"]
        mixT, H, J = T["mixT"], c["H"], c["J"]
        PO, PD = P[5], P[6]
        items = [(half, j) for half in range(2) for j in range(t + 1)]

        def qk(i):
            half, j = items[i]
            Pst = P[3 + (i % 2)]
            kc = slice(j * 128, (j + 1) * 128)
            d0 = min((t - j) * 128, HCLAMP)
            for jj in range(2):
                ch = 2 * half + jj
                for hh in range(2):
                    hq = QPERM[2 * ch + hh]
                    n = hq // 2
                    col = (jj * 2 + hh) * 128
                    self.mm(Pst, Pst[:, col:col + 128], kTt[j], kT[hh * 64:(hh + 1) * 64, n // 2, kc],
                            qT, qT[hh * 64:(hh + 1) * 64, ch, :], True, False)
                    self.mm(Pst, Pst[:, col:col + 128], J, J[:], H, H[:, hq, d0:d0 + 128], False, True)

        def rest(i):
            half, j = items[i]
            Pst = P[3 + (i % 2)]
            pb = T["pb"].next()
            self.act(pb, pb[:], Pst, Pst[:], AF.Exp, bias=c["nshift"][:, 0:1], scale=1.0, R=[c["nshift"]])
            pm = T["pm"].next()
            self.tt(pm, pm[:].rearrange("p (h q) -> p h q", h=4), pb, pb[:].rearrange("p (h q) -> p h q", h=4),
                    maskT, maskT[:, j, :].unsqueeze(1).to_broadcast([128, 4, 128]), OP.mult, eng=MKENG)
            for jj in range(2):
                ch = 2 * half + jj
                nb = 2 * (ch // 2) * 64
                for hh in range(2):
                    col = (jj * 2 + hh) * 128
                    first = (j == 0 and jj == 0 and hh == 0)
                    lastm = (j == t and jj == 1 and hh == 1)
                    self.mm(PO, PO[:, col:col + 128], vct[j], vc[:, j, nb:nb + 128], pm, pm[:, col:col + 128], first, lastm)
            self.mm(PD, PD[:], c["onesb"], c["onesb"][:], pm, pm[:], j == 0, j == t)
            if j == t:
                rden = T["rden"]
                s.op("dve", lambda e: e.reciprocal(out=rden[:], in_=PD[:]), R=[PD], W=[rden])
                for jj in range(2):
                    ch = 2 * half + jj
                    for hh in range(2):
                        col = (jj * 2 + hh) * 128
                        pr = slice(hh * 64, (hh + 1) * 64)
                        self.tt(mixT, mixT[pr, ch, :], PO, PO[pr, col:col + 128], rden, rden[pr, col:col + 128], OP.mult)

        add(1.8, lambda: qk(0), "att")
        for i in range(len(items)):
            def au(i=i):
                if i + 1 < len(items):
                    qk(i + 1)
                rest(i)
            add(2.9, au, "att")
        add(10.0, lambda: self.pool_mix(t == 0, 128, ub, bank=3))
        add(15.0, lambda: self.mem_attend(128, mqm=mqm, pst=(3, 4)))

        def outp():
            self.out_proj(xT, xT[:], 128, banks=(3, 4), dst=xT)
            self.dma("sp", self.XA, self.XA[:, cols].rearrange("(k p) t -> p k t", p=128), xT, xT[:], key="x1st")
        add(14.0, outp)
        return U


    def out_proj(self, xb, xap3, n, banks=(7, 2), dst=None):
        T, P, lay = self.T, self.P, self.lay
        wo, mixT, x1 = lay["wo"], T["mixT"], (dst or T["x1"])
        for half in range(2):
            Pb = P[banks[half]]
            for j in range(4):
                dm = half * 4 + j
                for k in range(8):
                    self.mm(Pb, Pb[:, j * 128:j * 128 + n], wo, wo[:, k, dm * 128:(dm + 1) * 128], mixT, mixT[:, k, 0:n], k == 0, k == 7)
            if n == 128:
                self.tt(x1, x1[:, half * 4:half * 4 + 4, :].rearrange("p c t -> p (c t)"), Pb, Pb[:, 0:512],
                        xb, xap3[:, half * 4:half * 4 + 4, :].rearrange("p c t -> p (c t)"), OP.add)
            else:
                for j in range(4):
                    dm = half * 4 + j
                    self.tt(x1, x1[:, dm, 0:n], Pb, Pb[:, j * 128:j * 128 + n], xb, xap3[:, dm, :], OP.add)

    def pool_mix(self, first, n, ubuf, oc=0, bank=7):
        s, c, T, P, lay = self.s, self.c, self.T, self.P, self.lay
        A, B = T["pl"]
        Wn = 15 + n
        self.tt(A, A[:, :, 1:Wn], ubuf, ubuf[:, :, 1:Wn], ubuf, ubuf[:, :, 0:Wn - 1], OP.add)
        self.tt(B, B[:, :, 3:Wn], A, A[:, :, 3:Wn], A, A[:, :, 1:Wn - 2], OP.add)
        pooled = T["pooled"]
        self.vcopy(pooled, pooled[0:64, 0, 0:n], A, A[0:64, 0, 15:Wn])
        self.vcopy(pooled, pooled[64:128, 0, 0:n], B, B[64:128, 0, 15:Wn])
        self.tt(A, A[:, 1, 7:Wn], B, B[:, 1, 7:Wn], B, B[:, 1, 3:Wn - 4], OP.add)
        self.vcopy(pooled, pooled[0:64, 1, 0:n], A, A[0:64, 1, 15:Wn])
        self.tt(B, B[64:128, 1, 15:Wn], A, A[64:128, 1, 15:Wn], A, A[64:128, 1, 7:Wn - 8], OP.add)
        self.vcopy(pooled, pooled[64:128, 1, 0:n], B, B[64:128, 1, 15:Wn])
        if first:
            self.tt(pooled, pooled[:, :, 0:n], pooled, pooled[:, :, 0:n], c["cinv"], c["cinv"][:, :, 0:n], OP.mult)
            self.tt(pooled, pooled[:, :, 0:n], pooled, pooled[:, :, 0:n], ubuf, ubuf[:, :, 15:Wn], OP.subtract)
        else:
            for cc in range(2):
                self.stt(pooled, pooled[:, cc, 0:n], pooled, pooled[:, cc, 0:n], c["winv"][:, cc:cc + 1], ubuf, ubuf[:, cc, 15:Wn],
                         OP.mult, OP.subtract, R=[c["winv"]])
        wpb, psc, mixT = lay["wpb"], lay["psc"], T["mixT"]
        for cc in range(2):
            self.mm(P[bank], P[bank][:, cc * 128:cc * 128 + n], wpb, wpb[:, cc, :], pooled, pooled[:, cc, 0:n], True, True)
            self.ts(mixT, mixT[:, 4 + cc, oc:oc + n], P[bank], P[bank][:, cc * 128:cc * 128 + n], psc[:, cc:cc + 1], None, OP.mult, R=[psc])

    def mem_attend(self, n, mkT=None, mvb=None, qc=0, mqm=None, pst=(0, 1)):
        s, c, T, P, lay = self.s, self.c, self.T, self.P, self.lay
        mkT = mkT or lay["mkT"]
        mvb = mvb or lay["mvb"]
        mqm, mixT = (mqm or T["mqm"]), T["mixT"]
        PO, PD = P[5], P[6]
        for h in range(4):
            for i in range(2):
                Pst = P[pst[i]]
                cc, hh = h // 2, h % 2
                self.mm(Pst, Pst[:, h * 128:h * 128 + n], mkT, mkT[:, cc, i * 128:(i + 1) * 128],
                        mqm, mqm[:, hh, cc, qc:qc + n], True, True)
        km = 9
        for i in range(2):
            Pst = P[pst[i]]
            pb = T["pb"].next()
            if n == 128:
                self.act(pb, pb[:], Pst, Pst[:], AF.Exp, bias=c["nshift"][:, 0:1], scale=1.0, R=[c["nshift"]])
            else:
                for h in range(4):
                    self.act(pb, pb[:, h * 128:h * 128 + n], Pst, Pst[:, h * 128:h * 128 + n], AF.Exp, bias=c["nshift"][:, 0:1], scale=1.0, R=[c["nshift"]])
            for h in range(4):
                cc = h // 2
                first = (i == 0 and h == 0)
                lastm = (i == 1 and h == 3)
                self.mm(PO, PO[:, h * 128:h * 128 + n], mvb, mvb[:, i, cc * 128:(cc + 1) * 128], pb, pb[:, h * 128:h * 128 + n], first, lastm)
                self.mm(PD, PD[:, h * 128:h * 128 + n], c["onesb"], c["onesb"][:], pb, pb[:, h * 128:h * 128 + n], first, lastm)
        rden = T["rden"]
        for h in range(4):
            cc, hh = h // 2, h % 2
            pr = slice(hh * 64, (hh + 1) * 64)
            cs = slice(h * 128, h * 128 + n)
            s.op("dve", lambda e, pr=pr, cs=cs: e.reciprocal(out=rden[pr, cs], in_=PD[pr, cs]), R=[PD], W=[rden])
            self.tt(mixT, mixT[pr, 6 + cc, qc:qc + n], PO, PO[pr, cs], rden, rden[pr, cs], OP.mult)

    def sample_tile(self, l):
        s, I, O, c, T, P, lay = self.s, self.I, self.O, self.c, self.T, self.P, self.lay
        NS = self.NS
        xsT = c["xsT"]
        hT, wf = T["hT"], self.Wt["wf"]
        self.project(l, xsT, xsT[:], NS, P)
        kn, tm = T["kn32"], T["tm"]
        usT = s.sb("usT", [128, 2, NS], F32)
        self.vcopy(usT, usT[:], P[1], P[1][:, 0:256].rearrange("p (c t) -> p c t", c=2)[:, :, 0:NS])
        ikn = s.sb("ikn", [128, NS], BF16)
        self.vcopy(ikn, ikn[:], P[0], P[0][:, 384:384 + NS])
        self.smp = dict(usT=usT, ikn=ikn)
        ks = s.sb("ks_tok", [NS, 256], F32)
        for cc in range(2):
            self.tr(P[3], P[3][0:NS, cc * 128:(cc + 1) * 128], kn, kn[:, cc, 0:NS], c["ident"], c["ident"][:])
        self.vcopy(ks, ks[:], P[3], P[3][0:NS, 0:256])
        self.smp["ks"] = ks
        self.dma("sp", O["k_s"], O["k_s"][l, :, :], ks, ks[:], key="ks")
        self.dma("sp", O["v_s"], O["v_s"][l, :, :], tm, tm[0:NS, 0:256], key="tm")
        self.dma("sp", O["ik_s"], O["ik_s"][l, :, :], tm, tm[0:NS, 256:288], key="tm2")
        for k in range(8):
            self.mm(P[4], P[4][0:NS, 0:256], hT, hT[:, k, 0:NS], wf, wf[:, k, 1536:1792], k == 0, k == 7)
        us = s.sb("us_tok", [NS, 256], F32)
        self.vcopy(us, us[:], P[4], P[4][0:NS, 0:256])
        mixT = T["mixT"]
        self.memset(mixT, mixT[:, 0:4, 0:NS], 0.0, eng="dve")
        st = s.sb("st_tok", [15, 256], F32)
        ubs = s.sb("ubs", [128, 2, 16], F32)
        mkt = s.sb("mkt", [128, 2, 256], BF16)
        mvs = s.sb("mvs", [128, 2, 256], BF16)
        mks = s.sb("mks", [128, 2, 256], BF16)
        for si in range(NS):
            self.dma("sp", st, st[:], I["spool"], I["spool"][l, si, :, :], key="st")
            self.dma("sp", O["pool_s"], O["pool_s"][l, si, 0:14, :], st, st[1:15, :], key="st_o")
            self.dma("sp", O["pool_s"], O["pool_s"][l, si, 14:15, :], us, us[si:si + 1, :], key="us_o")
            for cc in range(2):
                self.tr(P[7], P[7][:, cc * 16:cc * 16 + 15], st, st[:, cc * 128:(cc + 1) * 128], c["ident"], c["ident"][0:15, 0:15])
            self.vcopy(ubs, ubs[:, :, 0:15], P[7], P[7][:, 0:32].rearrange("p (c t) -> p c t", c=2)[:, :, 0:15])
            self.vcopy(ubs, ubs[:, :, 15:16], usT, usT[:, :, si:si + 1])
            self.pool_mix(False, 1, ubs, oc=si)
            self.dma("pool", mkt, mkt[:], I["cmk"], I["cmk"][l, si].rearrange("(i p) c -> p i c", p=128), key="mkt")
            self.dma("pool", mvs, mvs[:], I["cmv"], I["cmv"][l, si].rearrange("(i p) c -> p i c", p=128), key="mvs")
            pmv = P[2][:].bitcast(BF16)
            for i in range(2):
                for cc in range(2):
                    self.tr(P[2], pmv[:, (i * 2 + cc) * 128:(i * 2 + cc + 1) * 128], mkt, mkt[:, i, cc * 128:(cc + 1) * 128], c["identb"], c["identb"][:])
            for cc in range(2):
                for i in range(2):
                    self.acopy(mks, mks[:, cc, i * 128:(i + 1) * 128], P[2], pmv[:, (i * 2 + cc) * 128:(i * 2 + cc + 1) * 128])
            self.mem_attend(1, mks, mvs, qc=si)
        if self.with_sample_dsa:
            self.dsa_sample(l)
        self.out_proj(xsT, xsT[:], NS)
        self.vcopy(xsT, xsT[:], T["x1"], T["x1"][:, :, 0:NS])

    def getbuf(self, alias, dt, shape, name):
        n = int(np.prod(shape))
        nbytes = 0
        if alias is not None:
            ap = alias.t[:]
            nd = len(ap.shape)
            if nd == 3:
                ap = ap.rearrange("p a b -> p (a b)")
            elif nd == 4:
                ap = ap.rearrange("p a b c -> p (a b c)")
            nbytes = ap.shape[1] * mybir.dt.size(ap.dtype)
        if nbytes >= n * mybir.dt.size(dt):
            ap = ap.bitcast(dt)[:, 0:n]
            buf = alias
        else:
            buf = self.s.sb(name, [128, n], dt)
            ap = buf.t[:]
        if len(shape) == 2:
            ap = ap.rearrange("p (a b) -> p a b", a=shape[0])
        elif len(shape) == 3:
            ap = ap.rearrange("p (a b c) -> p a b c", a=shape[0], b=shape[1])
        return buf, ap

    def dsa_sample(self, l):
        s, I, O, c, T, P, lay, smp = self.s, self.I, self.O, self.c, self.T, self.P, self.lay, self.smp
        NS, NPG, NPOOL = self.NS, self.NPG, self.NPOOL
        KT = float(self.KTOP_S) - 0.5
        ident, identb, ones, onesb = c["ident"], c["identb"], c["ones"], c["onesb"]
        cs = s.sb("c_smp", [128, 167], F32)
        self.dma("sp", cs, cs[:], I["c_smp"], I["c_smp"][:, :])
        fold = cs[:, 0:128]
        bmask = cs[:, 128:152].rearrange("p (g h) -> p g h", g=3)
        rmask = cs[:, 152:156]
        dbase = cs[:, 156:157]
        nsel = cs[0:8, 157:161]
        even, odd = cs[0:8, 161:162], cs[0:8, 162:163]
        pairsel = cs[0:8, 163:167]
        thr = s.sb("thr", [128, 31], F32)
        self.dma("sp", thr, thr[:], I["c_thr"], I["c_thr"][:, :])
        rbb = s.sb("rb_bc", [128, 32, 8], F32)
        self.dma("sp", rbb, rbb[:].rearrange("p b h -> p (b h)"), I["rel_bias"], bass.AP(I["rel_bias"].t.tensor, 0, [[0, 128], [1, 256]]))
        drel = s.sb("drel", [128, 8, 31], F32)
        rb0 = s.sb("rb0", [128, 8], F32)
        for pos in range(8):
            hq = QPERM[pos]
            self.tt(drel, drel[:, pos, :], rbb, rbb[:, 1:32, hq], rbb, rbb[:, 0:31, hq], OP.subtract)
            self.vcopy(rb0, rb0[:, pos:pos + 1], rbb, rbb[:, 0, hq:hq + 1])
        sel = s.sb("sel", [NS, NS, 128], F32)
        for si in range(NS):
            self.vcopy(sel, sel[0:NS, si, :], ident, ident[0:NS, si:si + 1].to_broadcast([NS, 128]))
        ptT = s.sb("ptT", [128, NS], I32)
        self.memset(ptT, ptT[:], 0, eng="dve")
        self.s.dma("sp", lambda e: e.dma_start(out=ptT[0:NPG, :], in_=bass.AP(I["ptab"].t.tensor, 0, [[1, NPG], [NPG, NS]]),
                                               allow_slow_non_contiguous=True), R=[I["ptab"]], W=[ptT], key="ptT")
        gidx = s.sb("gidx", [128, NS], I32)
        self.ts(gidx, gidx[:], ptT, ptT[:], float(l * NPOOL), None, OP.add)
        pb128 = s.sb("pb128", [128, NS], F32)
        self.ts(pb128, pb128[:], gidx, gidx[:], 128.0, None, OP.mult)
        Scs = s.sb("Scs", [128, NS, 129], F32)
        self.memset(Scs, Scs[:], NEG, eng="dve")
        PGb, PG = self.getbuf(None, F32, [4096], "PGd")
        IKb, IKT = self.getbuf(None, BF16, [32, NPG], "IKTd")
        Rb, R = self.getbuf(T["x1"], F32, [128, 8], "Rd")
        cikp = I["cik"][:, :].rearrange("(g t) d -> g (t d)", t=128)
        rhs3 = s.sb("rhs3", [128, 3, 8], F32)
        IQm = s.sb("IQm", [128, 4, 8], BF16)
        iwb = s.sb("iwb", [128, 8], F32)
        t8 = s.sb("t8", [1, 8], F32)
        iqT, iw16, ikn = T["iqT"], T["iw16"], smp["ikn"]
        for si in range(NS):
            self.s.dma("pool", lambda e, si=si: e.indirect_dma_start(
                out=PG[0:NPG, :], out_offset=None, in_=cikp,
                in_offset=bass.IndirectOffsetOnAxis(ap=gidx[0:NPG, si:si + 1], axis=0)),
                R=[I["cik"], gidx], W=[PGb], key="PG")
            for g0 in range(0, 32, 4):
                Pb = P[(g0 // 4) % 2]
                for g in range(g0, g0 + 4):
                    self.tr(Pb, Pb[:, (g - g0) * NPG:(g - g0 + 1) * NPG], PGb, PG[0:NPG, g * 128:(g + 1) * 128], ident, ident[0:NPG, 0:NPG])
                self.acopy(IKb, IKT[:, g0:g0 + 4, :], Pb, Pb[:, 0:4 * NPG].rearrange("p (g n) -> p g n", g=4))
            for g in range(3):
                self.ts(rhs3, rhs3[:, g, :], cs, bmask[:, g, :], iqT[:, g, si:si + 1], None, OP.mult, R=[iqT])
            for g in range(3):
                self.mm(P[2], P[2][:, 0:8], cs, fold, rhs3, rhs3[:, g, :], g == 0, g == 2)
            for r in range(4):
                self.ts(IQm, IQm[:, r, :], P[2], P[2][:, 0:8], rmask[:, r:r + 1], None, OP.mult, R=[cs])
            for t in range(128):
                g, r = t // 4, t % 4
                Pb = P[3 + t // 64]
                col = (t % 64) * 8
                self.mm(Pb, Pb[0:NPG, col:col + 8], IKb, IKT[:, g, :], IQm, IQm[:, r, :], True, True)
            for hf in range(2):
                self.act(Rb, R[0:NPG, hf * 64:(hf + 1) * 64, :], P[3 + hf], P[3 + hf][0:NPG, 0:512].rearrange("p (t h) -> p t h", h=8), AF.Relu)
            self.mm(P[2], P[2][:, 8:16], sel, sel[0:NS, si, :], iw16, iw16[0:NS, 0:8], True, True)
            self.vcopy(iwb, iwb[:], P[2], P[2][:, 8:16])
            self.tt(Rb, R[0:NPG, :, :], Rb, R[0:NPG, :, :], iwb, iwb[0:NPG, :].unsqueeze(1).to_broadcast([NPG, 128, 8]), OP.mult)
            s.op("dve", lambda e, si=si: e.tensor_reduce(out=Scs[0:NPG, si, 0:128], in_=R[0:NPG, :, :], axis=AX.X, op=OP.add), R=[Rb], W=[Scs])
            self.mm(P[2], P[2][0:1, 16:24], ikn, ikn[:, si:si + 1], IQm, IQm[:, 0, :], True, True)
            self.act(t8, t8[:], P[2], P[2][0:1, 16:24], AF.Relu)
            self.tt(t8, t8[:], t8, t8[:], iwb, iwb[0:1, :], OP.mult)
            s.op("dve", lambda e, si=si: e.tensor_reduce(out=Scs[0:1, si, 128:129], in_=t8[:], axis=AX.X, op=OP.add), R=[t8], W=[Scs])
        mnp = s.sb("mnp", [128, NS], F32)
        mxp = s.sb("mxp", [128, NS], F32)
        s.op("dve", lambda e: e.tensor_reduce(out=mnp[0:NPG, :], in_=Scs[0:NPG, :, 0:128], axis=AX.X, op=OP.min), R=[Scs], W=[mnp])
        s.op("dve", lambda e: e.tensor_reduce(out=mxp[0:NPG, :], in_=Scs[0:NPG, :, 0:128], axis=AX.X, op=OP.max), R=[Scs], W=[mxp])
        self.tr(P[2], P[2][0:NS, 0:NPG], mnp, mnp[0:NPG, :], ident, ident[0:NPG, 0:NPG])
        self.tr(P[2], P[2][0:NS, 128:128 + NPG], mxp, mxp[0:NPG, :], ident, ident[0:NPG, 0:NPG])
        v3 = s.sb("v3", [NS, 4], F32)
        s.op("dve", lambda e: e.tensor_reduce(out=v3[:, 0:1], in_=P[2][0:NS, 0:NPG], axis=AX.X, op=OP.min), R=[P[2]], W=[v3])
        s.op("dve", lambda e: e.tensor_reduce(out=v3[:, 1:2], in_=P[2][0:NS, 128:128 + NPG], axis=AX.X, op=OP.max), R=[P[2]], W=[v3])
        self.tt(v3, v3[:, 2:3], v3, v3[:, 1:2], v3, v3[:, 0:1], OP.subtract)
        dg = s.sb("dg", [NS, 2, NS], F32)
        self.ts(dg, dg[:, 0, :], ident, ident[0:NS, 0:NS], v3[:, 0:1], None, OP.mult, R=[v3])
        self.ts(dg, dg[:, 1, :], ident, ident[0:NS, 0:NS], v3[:, 2:3], None, OP.mult, R=[v3])
        self.mm(P[2], P[2][:, 256:256 + 2 * NS], ones, ones[0:NS, :], dg, dg[:].rearrange("p a b -> p (a b)"), True, True)
        mr = s.sb("mr", [128, 2, NS], F32)
        self.vcopy(mr, mr[:].rearrange("p a b -> p (a b)"), P[2], P[2][:, 256:256 + 2 * NS])
        ddS = s.sb("ddS", [128, NS, NIT + 1], F32)
        self.tt(ddS, ddS[:], c["pow2"], c["pow2"][:].unsqueeze(1).to_broadcast([128, NS, NIT + 1]),
                mr, mr[:, 1, :].unsqueeze(2).to_broadcast([128, NS, NIT + 1]), OP.mult)
        mid = s.sb("mid", [128, NS], F32)
        self.tt(mid, mid[:], mr, mr[:, 0, :], ddS, ddS[:, :, 1], OP.add)
        jb, junk = self.getbuf(None, F32, [NS, 129], "junkd")
        cntp = s.sb("cntp", [128, NS], F32)
        sgn = s.sb("sgn", [128, NS], F32)
        for i in range(NIT):
            self.tt(jb, junk, Scs, Scs[:], mid, mid[:].unsqueeze(2).to_broadcast([128, NS, 129]), OP.is_ge)
            s.op("dve", lambda e: e.tensor_reduce(out=cntp[:], in_=junk, axis=AX.X, op=OP.add), R=[jb], W=[cntp])
            self.mm(P[2], P[2][:, 320:320 + NS], ones, ones[:], cntp, cntp[:], True, True)
            self.ts(sgn, sgn[:], P[2], P[2][:, 320:320 + NS], KT, -0.5, OP.is_ge, OP.add)
            self.tt(sgn, sgn[:], sgn, sgn[:], ddS, ddS[:, :, i + 1], OP.mult)
            self.tt(mid, mid[:], mid, mid[:], sgn, sgn[:], OP.add)
        lob = s.sb("lob", [128, NS], F32)
        self.tt(lob, lob[:], mid, mid[:], ddS, ddS[:, :, NIT], OP.subtract)
        Kcb, Kc = self.getbuf(None, F32, [16, 256], "Kcd")
        Vcb, Vc = self.getbuf(None, F32, [16, 256], "Vcd")
        tmb, tmp = self.getbuf(None, F32, [16, 64], "tmpd")
        geb, ge = self.getbuf(T["sq"], F32, [16, 31], "ged")
        tbb, tb = self.getbuf(T["rs"], F32, [16, 31], "tbd")
        qbb, q_bc = self.getbuf(T["rden"], F32, [512], "qbd")
        Wk = s.sb("Wk", [128, 129], F32)
        Wk2 = s.sb("Wk2", [128, 129], F32)
        m8 = s.sb("m8", [128, 16], F32)
        i8 = s.sb("i8", [128, 16], U32)
        cf = s.sb("cf", [128, 6, 16], F32)
        rowf = s.sb("rowf", [128, 16], F32)
        rowi = s.sb("rowi", [128, 16], I32)
        vnew = s.sb("vnew", [128, 1], F32)
        dgq = s.sb("dgq", [128, 128], BF16)
        dgk = s.sb("dgk", [128, 128], F32)
        knb = s.sb("knb", [128, 256], F32)
        vnb = s.sb("vnb", [128, 256], F32)
        lgr = s.sb("lgr", [128, 8, 16], F32)
        lg = s.sb("lg", [128, 8, 16], F32)
        lgn = s.sb("lgn", [128, 8], F32)
        tmn = s.sb("tmn", [128, 8, 64], F32)
        bp = s.sb("bp", [128, 16], F32)
        o4 = s.sb("o4", [8, 4, 64], F32)
        osel = s.sb("osel", [8, 64], F32)
        rd = s.sb("rd", [8, 1], F32)
        A2 = s.sb("A2", [8, 128], F32)
        ckv, cvv = I["ck"], I["cv"]
        qT, kn, tm, mixT = T["qT"], T["kn32"], T["tm"], T["mixT"]
        for si in range(NS):
            lo_s = lob[:, si:si + 1]
            self.vcopy(Wk, Wk[:], Scs, Scs[:, si, :])
            s.op("dve", lambda e: e.max(out=m8[:, 0:8], in_=Wk[:]), R=[Wk], W=[m8])
            s.op("dve", lambda e: e.max_index(out=i8[:, 0:8], in_max=m8[:, 0:8], in_values=Wk[:]), R=[Wk, m8], W=[i8])
            s.op("dve", lambda e: e.match_replace(out=Wk2[:], in_to_replace=m8[:, 0:8], in_values=Wk[:], imm_value=NEG), R=[Wk, m8], W=[Wk2])
            s.op("dve", lambda e: e.max(out=m8[:, 8:16], in_=Wk2[:]), R=[Wk2], W=[m8])
            s.op("dve", lambda e: e.max_index(out=i8[:, 8:16], in_max=m8[:, 8:16], in_values=Wk2[:]), R=[Wk2, m8], W=[i8])
            self.vcopy(cf, cf[:, 0, :], i8, i8[:])
            self.ts(cf, cf[:, 1, :], m8, m8[:], lo_s, None, OP.is_ge, R=[lob])
            self.ts(cf, cf[:, 2, :], cf, cf[:, 0, :], 127.5, None, OP.is_le)
            self.tt(cf, cf[:, 3, :], cf, cf[:, 1, :], cf, cf[:, 2, :], OP.mult)
            self.ts(vnew, vnew[:], Scs, Scs[:, si, 128:129], lo_s, None, OP.is_ge, R=[lob])
            self.ts(cf, cf[:, 4, :], cf, cf[:, 0, :], 127.0, None, OP.min)
            self.ts(rowf, rowf[:], cf, cf[:, 4, :], pb128[:, si:si + 1], None, OP.add, R=[pb128])
            self.vcopy(rowi, rowi[:], rowf, rowf[:])
            self.ts(cf, cf[:, 5, :], cf, cf[:, 4, :], -1.0, dbase, OP.mult, OP.add, R=[cs])
            for i in range(16):
                self.s.dma("pool", lambda e, i=i: e.indirect_dma_start(
                    out=Kc[:, i, :], out_offset=None, in_=ckv[:, :],
                    in_offset=bass.IndirectOffsetOnAxis(ap=rowi[:, i:i + 1], axis=0)), R=[ckv, rowi], W=[Kcb], key="Kc")
                self.s.dma("pool", lambda e, i=i: e.indirect_dma_start(
                    out=Vc[:, i, :], out_offset=None, in_=cvv[:, :],
                    in_offset=bass.IndirectOffsetOnAxis(ap=rowi[:, i:i + 1], axis=0)), R=[cvv, rowi], W=[Vcb], key="Vc")
            for ch in range(4):
                self.ts(dgq, dgq[:], identb, identb[:], qT[:, ch, si:si + 1], None, OP.mult, R=[qT])
                self.mm(P[3], P[3][:, ch * 128:(ch + 1) * 128], onesb, onesb[:], dgq, dgq[:], True, True)
            self.vcopy(qbb, q_bc, P[3], P[3][:, 0:512])
            for cc in range(2):
                self.ts(dgk, dgk[:], ident, ident[:], kn[:, cc, si:si + 1], None, OP.mult, R=[kn])
                self.mm(P[4], P[4][:, cc * 128:(cc + 1) * 128], ones, ones[:], dgk, dgk[:], True, True)
            self.mm(P[4], P[4][:, 256:512], sel, sel[0:NS, si, :], tm, tm[0:NS, 0:256], True, True)
            self.vcopy(knb, knb[:], P[4], P[4][:, 0:256])
            self.vcopy(vnb, vnb[:], P[4], P[4][:, 256:512])
            for pos in range(8):
                n_ = QPERM[pos] // 2
                self.tt(tmb, tmp, Kcb, Kc[:, :, n_ * 64:(n_ + 1) * 64], qbb,
                        q_bc[:, pos * 64:(pos + 1) * 64].unsqueeze(1).to_broadcast([128, 16, 64]), OP.mult)
                s.op("dve", lambda e, pos=pos: e.tensor_reduce(out=lgr[:, pos, :], in_=tmp, axis=AX.X, op=OP.add), R=[tmb], W=[lgr])
            for A in range(2):
                q4 = q_bc[:, A * 256:(A + 1) * 256].rearrange("p (b c d) -> p b c d", b=2, c=2)
                k2 = knb[:, A * 128:(A + 1) * 128].rearrange("p (c d) -> p c d", c=2).unsqueeze(1).to_broadcast([128, 2, 2, 64])
                self.tt(tmn, tmn[:, A * 4:(A + 1) * 4, :].rearrange("p (b c) d -> p b c d", b=2), qbb, q4, knb, k2, OP.mult)
            s.op("dve", lambda e: e.tensor_reduce(out=lgn[:], in_=tmn[:], axis=AX.X, op=OP.add), R=[tmn], W=[lgn])
            self.tt(geb, ge, cf, cf[:, 5, :].unsqueeze(2).to_broadcast([128, 16, 31]), thr, thr[:].unsqueeze(1).to_broadcast([128, 16, 31]), OP.is_ge)
            for pos in range(8):
                self.tt(tbb, tb, geb, ge, drel, drel[:, pos, :].unsqueeze(1).to_broadcast([128, 16, 31]), OP.mult)
                s.op("dve", lambda e: e.tensor_reduce(out=bp[:], in_=tb, axis=AX.X, op=OP.add), R=[tbb], W=[bp])
                self.stt(lg, lg[:, pos, :], bp, bp[:], rb0[:, pos:pos + 1], lgr, lgr[:, pos, :], OP.add, OP.add, R=[rb0])
            self.tt(lgn, lgn[:], lgn, lgn[:], rb0, rb0[:], OP.add)
            self.act(lg, lg[:], lg, lg[:], AF.Exp, bias=c["nshift"][:, 0:1], scale=1.0, R=[c["nshift"]])
            self.tt(lg, lg[:], lg, lg[:], cf, cf[:, 3, :].unsqueeze(1).to_broadcast([128, 8, 16]), OP.mult)
            self.act(lgn, lgn[:], lgn, lgn[:], AF.Exp, bias=c["nshift"][:, 0:1], scale=1.0, R=[c["nshift"]])
            self.ts(lgn, lgn[:], lgn, lgn[:], vnew[:, 0:1], None, OP.mult, R=[vnew])
            for i in range(16):
                self.mm(P[5], P[5][0:8, 0:256], lg, lg[:, :, i], Vcb, Vc[:, i, :], i == 0, False)
                self.mm(P[6], P[6][0:8, 0:1], lg, lg[:, :, i], ones, ones[:, 0:1], i == 0, False)
            self.mm(P[5], P[5][0:8, 0:256], lgn, lgn[:], vnb, vnb[:], False, True)
            self.mm(P[6], P[6][0:8, 0:1], lgn, lgn[:], ones, ones[:, 0:1], False, True)
            self.tt(o4, o4[:], P[5], P[5][0:8, 0:256].rearrange("p (n d) -> p n d", n=4), cs, nsel.unsqueeze(2).to_broadcast([8, 4, 64]), OP.mult)
            s.op("dve", lambda e: e.tensor_reduce(out=osel[:], in_=o4[:].rearrange("p n d -> p d n"), axis=AX.X, op=OP.add), R=[o4], W=[osel])
            s.op("dve", lambda e: e.reciprocal(out=rd[:], in_=P[6][0:8, 0:1]), R=[P[6]], W=[rd])
            self.ts(A2, A2[:, 0:64], osel, osel[:], rd[:, 0:1], even, OP.mult, OP.mult, R=[rd, cs])
            self.ts(A2, A2[:, 64:128], osel, osel[:], rd[:, 0:1], odd, OP.mult, OP.mult, R=[rd, cs])
            self.mm(P[7], P[7][:, 0:4], A2, A2[:], cs, pairsel, True, True)
            self.vcopy(mixT, mixT[:, 0:4, si], P[7], P[7][:, 0:4])

    def phase2(self, l):
        s, I, O, c = self.s, self.I, self.O, self.c
        S, NS = self.S, self.NS
        NF = DFF // 128
        WN = 256
        wu = s.sb("wu", [128, 8, 2 * DFF], BF16)
        wd = s.sb("wd", [128, NF, D], BF16)
        for k in range(8):
            for c0 in range(0, 2 * DFF, 2048):
                n = min(2048, 2 * DFF - c0)
                self.dma("pool", wu, wu[:, k, c0:c0 + n], I["w_up"], I["w_up"][l, k * 128:(k + 1) * 128, c0:c0 + n], key=None)
        for f in range(NF):
            self.dma("pool", wd, wd[:, f, :], I["w_down"], I["w_down"][l, f * 128:(f + 1) * 128, :], key=None)
        g2 = self.load_gvec("g2", I["norm2_g"], l)
        cw = s.sb("cw", [128, 44, 3], F32)
        cb = s.sb("cb", [128, 44], F32)
        for j in range(3):
            ap = bass.AP(I["conv_w"].t.tensor, (l * 3 + j) * 2 * DFF, [[1, 128], [128, 44]])
            self.s.dma("sp", lambda e, ap=ap, j=j: e.dma_start(out=cw[:, :, j], in_=ap, allow_slow_non_contiguous=True), R=[I["conv_w"]], W=[cw], key="cw%d" % j)
        apb = bass.AP(I["conv_b"].t.tensor, l * 2 * DFF, [[1, 128], [128, 44]])
        self.s.dma("sp", lambda e: e.dma_start(out=cb[:], in_=apb, allow_slow_non_contiguous=True), R=[I["conv_b"]], W=[cb], key="cb")
        P = [s.ps("Q%d" % i, [128, 512], F32) for i in range(8)]
        xw = s.sb("xw", [128, 8, WN], F32)
        sq = s.sb("sq8b", [128, 8, WN], F32)
        rs = s.sb("rsb", [128, WN], F32)
        h2 = s.sb("h2", [128, 8, WN], BF16)
        aT = s.sb("aT", [128, NF, WN], BF16)
        t1 = Ring([s.sb("t1_%d" % i, [128, WN], F32) for i in range(2)])
        t2 = Ring([s.sb("t2_%d" % i, [128, WN], F32) for i in range(2)])
        cg = Ring([s.sb("cg_%d" % i, [128, WN], F32) for i in range(2)])
        cv = Ring([s.sb("cv_%d" % i, [128, WN], F32) for i in range(2)])
        sg = Ring([s.sb("sg_%d" % i, [128, WN], F32) for i in range(2)])
        xo = Ring([s.sb("xo_%d" % i, [128, WN], F32) for i in range(2)])
        ytok = s.sb("ytok", [128, D], F32)
        ctk = Ring([s.sb("ctk_%d" % i, [NS, 512], F32) for i in range(2)])
        ident = c["ident"]

        def up_pair(f, n, Pg, Pv):
            for (Pb, ff) in ((Pg, f), (Pv, NF + f)):
                for k in range(8):
                    self.mm(Pb, Pb[:, 0:n], wu, wu[:, k, ff * 128:(ff + 1) * 128], h2, h2[:, k, 0:n], k == 0, k == 7)

        def norm2(xb, xap3, n):
            self.rms_feature_major(xb, xap3, n, g2, h2, P[6], sq, rs)

        def up_rows_out(col0, ncols, dst, dst_ap_fn):
            for ci, c0 in enumerate(range(0, 2 * DFF, 512)):
                Pb = P[4 + ci % 2]
                for k in range(8):
                    self.mm(Pb, Pb[0:ncols, 0:512], h2, h2[:, k, col0:col0 + ncols], wu, wu[:, k, c0:c0 + 512], k == 0, k == 7)
                ct = ctk.next()
                self.vcopy(ct, ct[0:ncols, :], Pb, Pb[0:ncols, 0:512])
                self.dma("sp", dst, dst_ap_fn(c0), ct, ct[0:ncols, :], key=ct.name)

        step = WN - 2
        starts = list(range(0, S, step))
        for wi, st0 in enumerate(starts):
            nnew = min(step, S - st0)
            n = nnew + 2
            if st0 == 0:
                self.memset(xw, xw[:, :, 0:2], 0.0, eng="dve")
                self.dma("sp", xw, xw[:, :, 2:n], self.XA, self.XA[:, 0:nnew].rearrange("(k p) t -> p k t", p=128))
            else:
                self.dma("sp", xw, xw[:, :, 0:n], self.XA, self.XA[:, st0 - 2:st0 + nnew].rearrange("(k p) t -> p k t", p=128))
            norm2(xw, xw[:, :, 0:n], n)
            if wi == len(starts) - 1:
                up_rows_out(n - 2, 2, O["conv_p"], lambda c0: O["conv_p"][l, :, c0:c0 + 512])
            for f in range(NF):
                Pg, Pv = P[(f % 2) * 2], P[(f % 2) * 2 + 1]
                up_pair(f, n, Pg, Pv)
                outs = []
                for (Pb, ff, ring) in ((Pg, f, cg), (Pv, NF + f, cv)):
                    a1, a2, cc_ = t1.next(), t2.next(), ring.next()
                    self.act(a1, a1[:, 0:nnew], Pb, Pb[:, 2:n], AF.Identity, bias=cb[:, ff:ff + 1], scale=cw[:, ff, 2:3], R=[cb, cw])
                    self.stt(a2, a2[:, 0:nnew], Pb, Pb[:, 1:n - 1], cw[:, ff, 1:2], a1, a1[:, 0:nnew], OP.mult, OP.add, R=[cw])
                    self.stt(cc_, cc_[:, 0:nnew], Pb, Pb[:, 0:n - 2], cw[:, ff, 0:1], a2, a2[:, 0:nnew], OP.mult, OP.add, R=[cw])
                    outs.append(cc_)
                sgt = sg.next()
                self.act(sgt, sgt[:, 0:nnew], outs[0], outs[0][:, 0:nnew], AF.Silu)
                self.tt(aT, aT[:, f, 0:nnew], sgt, sgt[:, 0:nnew], outs[1], outs[1][:, 0:nnew], OP.mult)
            for dm in range(8):
                Pb = P[4 + dm % 2]
                for f in range(NF):
                    self.mm(Pb, Pb[:, 0:nnew], wd, wd[:, f, dm * 128:(dm + 1) * 128], aT, aT[:, f, 0:nnew], f == 0, f == NF - 1)
                if l == 0:
                    xt = xo.next()
                    self.tt(xt, xt[:, 0:nnew], Pb, Pb[:, 0:nnew], xw, xw[:, dm, 2:n], OP.add)
                    self.dma("sp", self.XB, self.XB[dm * 128:(dm + 1) * 128, st0:st0 + nnew], xt, xt[:, 0:nnew], key=xt.name)
                else:
                    self.tt(xw, xw[:, dm, 2:n], Pb, Pb[:, 0:nnew], xw, xw[:, dm, 2:n], OP.add)
            if l == 1:
                for b0 in range(0, nnew, 128):
                    nb = min(128, nnew - b0)
                    for k0 in (0, 4):
                        Pb = P[6 + (k0 // 4)]
                        for k in range(k0, k0 + 4):
                            self.tr(Pb, Pb[0:nb, (k - k0) * 128:(k - k0 + 1) * 128], xw, xw[:, k, 2 + b0:2 + b0 + nb], ident, ident[:])
                        self.acopy(ytok, ytok[0:nb, k0 * 128:(k0 + 4) * 128], Pb, Pb[0:nb, 0:512])
                    self.dma("sp", O["y_p"], O["y_p"][st0 + b0:st0 + b0 + nb, :], ytok, ytok[0:nb, :], key="ytok")

        xsT = c["xsT"]
        norm2(xsT, xsT[:], NS)
        up_rows_out(0, NS, O["conv_s"], lambda c0: O["conv_s"][l, :, 1, c0:c0 + 512])
        sT = [s.sb("sT%d" % i, [128, 44, NS], F32) for i in range(2)]
        for i in range(2):
            for ci, c0 in enumerate(range(0, 2 * DFF, 512)):
                ct = ctk.next()
                self.dma("sp", ct, ct[0:NS, :], I["sconv"], I["sconv"][l, :, i, c0:c0 + 512], key=ct.name)
                if i == 1:
                    self.dma("sp", O["conv_s"], O["conv_s"][l, :, 0, c0:c0 + 512], ct, ct[0:NS, :], key=ct.name + "o")
                Pb = P[6 + ci % 2]
                for j in range(4):
                    self.tr(Pb, Pb[:, j * NS:(j + 1) * NS], ct, ct[0:NS, j * 128:(j + 1) * 128], ident, ident[0:NS, 0:NS])
                f0 = c0 // 128
                self.vcopy(sT[i], sT[i][:, f0:f0 + 4, :], Pb, Pb[:, 0:4 * NS].rearrange("p (f s) -> p f s", s=NS))
        for f in range(NF):
            Pg, Pv = P[(f % 2) * 2], P[(f % 2) * 2 + 1]
            up_pair(f, NS, Pg, Pv)
            outs = []
            for (Pb, ff, ring) in ((Pg, f, cg), (Pv, NF + f, cv)):
                a1, a2, cc_ = t1.next(), t2.next(), ring.next()
                self.act(a1, a1[:, 0:NS], Pb, Pb[:, 0:NS], AF.Identity, bias=cb[:, ff:ff + 1], scale=cw[:, ff, 2:3], R=[cb, cw])
                self.stt(a2, a2[:, 0:NS], sT[1], sT[1][:, ff, :], cw[:, ff, 1:2], a1, a1[:, 0:NS], OP.mult, OP.add, R=[cw])
                self.stt(cc_, cc_[:, 0:NS], sT[0], sT[0][:, ff, :], cw[:, ff, 0:1], a2, a2[:, 0:NS], OP.mult, OP.add, R=[cw])
                outs.append(cc_)
            sgt = sg.next()
            self.act(sgt, sgt[:, 0:NS], outs[0], outs[0][:, 0:NS], AF.Silu)
            self.tt(aT, aT[:, f, 0:NS], sgt, sgt[:, 0:NS], outs[1], outs[1][:, 0:NS], OP.mult)
        for dm in range(8):
            Pb = P[4 + dm % 2]
            for f in range(NF):
                self.mm(Pb, Pb[:, 0:NS], wd, wd[:, f, dm * 128:(dm + 1) * 128], aT, aT[:, f, 0:NS], f == 0, f == NF - 1)
            self.tt(xsT, xsT[:, dm, :], Pb, Pb[:, 0:NS], xsT, xsT[:, dm, :], OP.add)
        if l == 1:
            for k0 in (0, 4):
                for k in range(k0, k0 + 4):
                    self.tr(P[6], P[6][0:NS, (k - k0) * 128:(k - k0 + 1) * 128], xsT, xsT[:, k, :], ident, ident[:])
                self.acopy(ytok, ytok[0:NS, k0 * 128:(k0 + 4) * 128], P[6], P[6][0:NS, 0:512])
            self.dma("sp", O["y_s"], O["y_s"][:, :], ytok, ytok[0:NS, :], key="ytok")


def make_consts(S, NPG=128):
    m = np.arange(TABLEN)
    d = m - 127
    oh = np.zeros((32, TABLEN), np.float32)
    b = t5_bucket_np(np.maximum(d, 0))
    valid = d >= 0
    oh[b[valid], m[valid]] = 1.0
    w = np.array([2, 4, 8, 16], np.float32)
    cinv = np.zeros((128, 2, 128), np.float32)
    winv = np.zeros((128, 2), np.float32)
    pos = np.arange(128, dtype=np.float32)
    for g in range(4):
        cc, hh = g // 2, g % 2
        cinv[hh * 64:(hh + 1) * 64, cc, :] = 1.0 / np.minimum(pos + 1.0, w[g])[None, :]
        winv[hh * 64:(hh + 1) * 64, cc] = 1.0 / w[g]
    pow2 = np.tile((2.0 ** -np.arange(NIT + 1, dtype=np.float64)).astype(np.float32)[None, :], (128, 1))
    dd = np.arange(0, 4096)
    bb = t5_bucket_np(dd)
    thr = np.array([dd[bb >= k].min() for k in range(1, 32)], np.float32)
    thr = np.tile(thr[None, :], (128, 1))
    iota = np.tile(np.arange(128, dtype=np.float32)[None, :], (128, 1))
    smp = np.zeros((128, 167), np.float32)
    k = np.arange(128)
    smp[:, 0:128] = (k[:, None] % 32 == k[None, :] % 32)
    for g in range(3):
        for h in range(8):
            smp[:, 128 + g * 8 + h] = (k < 96) & (h // 3 == g) & (k // 32 == h % 3)
    for r in range(4):
        smp[:, 152 + r] = (k // 32 == r)
    smp[:, 156] = np.where(k < NPG, (NPG - k) * 128, 0)
    for pos in range(8):
        for n in range(4):
            smp[pos, 157 + n] = float(n == QPERM[pos] // 2)
        smp[pos, 161] = float(pos % 2 == 0)
        smp[pos, 162] = float(pos % 2 == 1)
        for ch in range(4):
            smp[pos, 163 + ch] = float(pos // 2 == ch)
    return dict(c_oh=oh, c_cinv=cinv, c_winv=winv, c_pow2=pow2, c_thr=thr, c_iota=iota, c_smp=smp)


def core_inputs(inp, core, NS, consts):
    b = core % 4
    f = lambda a: np.ascontiguousarray(a)
    sl = slice(core * NS, (core + 1) * NS)
    m = {}
    m["xp"] = f(inp["x_prompt"][b])
    m["xs"] = f(inp["x_sample"][sl, 0])
    m["memp"] = f(inp["mem_prompt"][b])
    ck = inp["cache_k"]
    m["ck"] = ck.reshape(ck.shape[0] * ck.shape[1] * ck.shape[2], 256)
    cv = inp["cache_v"]
    m["cv"] = cv.reshape(cv.shape[0] * cv.shape[1] * cv.shape[2], 256)
    ci = inp["cache_idx_k"]
    m["cik"] = ci.reshape(ci.shape[0] * ci.shape[1] * ci.shape[2], 32)
    m["cmk"] = f(inp["cache_mem_k"][:, sl].reshape(2, NS, 256, 256))
    m["cmv"] = f(inp["cache_mem_v"][:, sl].reshape(2, NS, 256, 256))
    m["spool"] = f(inp["state_pool"][:, sl])
    m["sconv"] = f(inp["state_conv"][:, sl])
    m["ptab"] = f(inp["page_table"][sl].astype(np.int32))
    for k in ("rel_bias", "norm1_g", "w_in", "q_norm_g", "k_norm_g", "mem_norm_g", "w_mem_kv", "mq_norm_g", "mk_norm_g",
              "w_pool", "pool_scale", "w_out", "norm2_g", "w_up", "conv_w", "conv_b", "w_down"):
        m[k] = f(inp[k])
    m.update(consts)
    return m


_CACHE = {}


def run(inp, n_cores=8, NS=4, with_sample_dsa=True):
    inp = {k: np.asarray(v) for k, v in inp.items()}
    S = inp["x_prompt"].shape[1]
    NPG = inp["page_table"].shape[1]
    NPOOL = inp["cache_k"].shape[1]
    key = (S, NPG, NPOOL, NS, with_sample_dsa)
    if key not in _CACHE:
        _CACHE[key] = K(S, NPG, NPOOL, NS, with_sample_dsa).build()
    nc = _CACHE[key]
    consts = make_consts(S, NPG)
    maps = [core_inputs(inp, c, NS, consts) for c in range(n_cores)]
    res = run_bass_kernel_spmd(nc, maps, core_ids=list(range(n_cores))).results
    B = inp["x_prompt"].shape[0]
    nb = min(B, n_cores)
    DB = n_cores * NS
    st = lambda name, shp: np.stack([res[c][name] for c in range(nb)], axis=0)
    y_p = st("y_p", None)
    y_s = np.concatenate([res[c]["y_s"] for c in range(n_cores)], axis=0)[:, None, :]
    k_p = st("k_p", None).transpose(1, 0, 2, 3).reshape(2, nb, S, 4, 64)
    v_p = st("v_p", None).transpose(1, 0, 2, 3).reshape(2, nb, S, 4, 64)
    ik_p = st("ik_p", None).transpose(1, 0, 2, 3)
    pool_p = st("pool_p", None).transpose(1, 0, 2, 3)
    conv_p = st("conv_p", None).transpose(1, 0, 2, 3)
    mk_p = st("mk_p", None).transpose(1, 0, 2, 3).reshape(2, nb, 256, 4, 64)
    mv_p = st("mv_p", None).transpose(1, 0, 2, 3).reshape(2, nb, 256, 4, 64)
    cat = lambda name: np.concatenate([res[c][name] for c in range(n_cores)], axis=1)
    k_s = cat("k_s").reshape(2, DB, 1, 4, 64)
    v_s = cat("v_s").reshape(2, DB, 1, 4, 64)
    ik_s = cat("ik_s").reshape(2, DB, 1, 32)
    pool_s = cat("pool_s")
    conv_s = cat("conv_s")
    outs = (y_p, y_s, k_p, v_p, ik_p, pool_p, conv_p, mk_p, mv_p, k_s, v_s, ik_s, pool_s, conv_s)
    return tuple(np.ascontiguousarray(o.astype(np.float32)) for o in outs)


def kernel(**inputs):
    return run(inputs, n_cores=8, NS=4)
```

```python
import math
import numpy as np
from contextlib import ExitStack
import concourse.bass as bass
import concourse.mybir as mybir
from concourse.bass_utils import run_bass_kernel_spmd

F32 = mybir.dt.float32
BF16 = mybir.dt.bfloat16
I32 = mybir.dt.int32
U32 = mybir.dt.uint32
AF = mybir.ActivationFunctionType
OP = mybir.AluOpType
AX = mybir.AxisListType

D = 1024
DFF = 2816
NIN = 1832
EPS = 1e-6
SHIFT = 8.0
NEG = -1.0e30
NIT = 16
ENGS = ["pe", "act", "dve", "pool", "sp"]
QPERM = [0, 2, 1, 3, 4, 6, 5, 7]
HCLAMP = 1664
HLEN = HCLAMP + 128
TABLEN = HLEN + 128
IXENG = "dve"
MKENG = "pool"


class Buf:
    def __init__(self, name, t=None):
        self.name = name
        self.t = t
        self.w = {}
        self.rs = {}

    def __getitem__(self, k):
        return self.t[k]


class Ring:
    def __init__(self, bufs):
        self.bufs = bufs
        self.i = 0

    def next(self):
        b = self.bufs[self.i % len(self.bufs)]
        self.i += 1
        return b


class Sched:
    def __init__(self, nc, es):
        self.nc = nc
        self.es_global = es
        self.es = es
        self.q = {e: [] for e in ENGS}
        self.cnt = {}
        self.sems = {}
        self.seen = {e: {} for e in ENGS}
        self.uid = 0
        for e in ENGS:
            self._sem("E_" + e)

    def _sem(self, key):
        if key not in self.sems:
            self.sems[key] = self.es_global.enter_context(self.nc.semaphore("s_" + key))
            self.cnt[key] = 0
        return self.sems[key]

    def sb(self, name, shape, dt=F32):
        self.uid += 1
        t = self.es.enter_context(self.nc.sbuf_tensor("%s_%d" % (name, self.uid), list(shape), dt))
        return Buf(name, t)

    def ps(self, name, shape, dt=F32):
        self.uid += 1
        t = self.es.enter_context(self.nc.psum_tensor("%s_%d" % (name, self.uid), list(shape), dt))
        return Buf(name, t)

    def _deps(self, eng, R, W):
        deps = {}

        def add(d):
            for k, v in d.items():
                if deps.get(k, 0) < v:
                    deps[k] = v
        for b in R:
            add(b.w)
        for b in W:
            add(b.w)
            add(b.rs)
        out = []
        for k, v in deps.items():
            if eng == "pe" and k == "E_pe":
                continue
            if self.seen[eng].get(k, 0) >= v:
                continue
            self.seen[eng][k] = v
            out.append((k, v))
        return out

    def _commit(self, tok, R, W):
        k, v = tok
        for b in R:
            if b.rs.get(k, 0) < v:
                b.rs[k] = v
        for b in W:
            if b.w.get(k, 0) < v:
                b.w[k] = v
            b.rs = {}

    def op(self, eng, fn, R=(), W=()):
        waits = self._deps(eng, R, W)
        key = "E_" + eng
        self.cnt[key] += 1
        tok = (key, self.cnt[key])
        self.q[eng].append((waits, fn, key, 1))
        self._commit(tok, R, W)

    def dma(self, eng, fn, R, W, key):
        key = "D_" + key
        self._sem(key)
        waits = self._deps(eng, R, W)
        self.cnt[key] += 16
        tok = (key, self.cnt[key])
        self.q[eng].append((waits, fn, key, 16))
        self._commit(tok, R, W)

    def barrier(self):
        for e in ENGS:
            waits = []
            for k, v in self.cnt.items():
                if v == 0:
                    continue
                if self.seen[e].get(k, 0) >= v:
                    continue
                self.seen[e][k] = v
                waits.append((k, v))
            self.q[e].append((waits, None, None, 0))

    def emit(self):
        nc = self.nc
        qs = self.q
        self.q = {e: [] for e in ENGS}
        sems = self.sems
        with nc.Block() as block:
            def run(e, items):
                for waits, fn, key, inc in items:
                    for k, v in waits:
                        e.wait_ge(sems[k], v)
                    if fn is not None:
                        ins = fn(e)
                        ins.then_inc(sems[key], inc)

            @block.tensor
            def _(e):
                run(e, qs["pe"])

            @block.scalar
            def _(e):
                run(e, qs["act"])

            @block.vector
            def _(e):
                run(e, qs["dve"])

            @block.gpsimd
            def _(e):
                run(e, qs["pool"])

            @block.sync
            def _(e):
                run(e, qs["sp"])


def t5_bucket_np(d):
    d = np.asarray(d, dtype=np.int64)
    dd = np.maximum(d, 1).astype(np.float32)
    large = 16 + (np.log(dd / np.float32(16)) / np.float32(math.log(2048 / 16)) * np.float32(16)).astype(np.int32)
    large = np.minimum(large, 31)
    return np.where(d < 16, d, large)


class K:
    def __init__(self, S, NPG, NPOOL, NS=4, with_sample_dsa=True):
        self.S, self.NPG, self.NPOOL, self.NS = S, NPG, NPOOL, NS
        self.NT = S // 128
        self.KTOP = min(256, S // 4)
        self.L_s = NPG * 128 + 1
        self.KTOP_S = min(256, self.L_s // 4)
        self.with_sample_dsa = with_sample_dsa
        import os
        self.dbg = int(os.environ.get('KDBG', '99'))

    def mm(self, ob, oap, lb, lap, rb, rap, st, sp):
        self.s.op("pe", lambda e: e.matmul(oap, lhsT=lap, rhs=rap, start=st, stop=sp), R=[lb, rb], W=[ob])

    def tr(self, ob, oap, ib, iap, idb, idap):
        self.s.op("pe", lambda e: e.transpose(out=oap, in_=iap, identity=idap), R=[ib, idb], W=[ob])

    def act(self, ob, oap, ib, iap, func, bias=None, scale=1.0, R=(), accum=None, W=()):
        def fn(e):
            kw = {}
            if bias is not None:
                kw["bias"] = bias
            if accum is not None:
                kw["accum_out"] = accum
            return e.activation(out=oap, in_=iap, func=func, scale=scale, **kw)
        self.s.op("act", fn, R=[ib] + list(R), W=[ob] + list(W))

    def acopy(self, ob, oap, ib, iap):
        self.s.op("act", lambda e: e.copy(out=oap, in_=iap), R=[ib], W=[ob])

    def vcopy(self, ob, oap, ib, iap, eng="dve"):
        self.s.op(eng, lambda e: e.tensor_copy(out=oap, in_=iap), R=[ib], W=[ob])

    def tt(self, ob, oap, ab, aap, bb, bap, op, eng="dve"):
        self.s.op(eng, lambda e: e.tensor_tensor(out=oap, in0=aap, in1=bap, op=op), R=[ab, bb], W=[ob])

    def ts(self, ob, oap, ib, iap, s1, s2, op0, op1=None, R=(), accum=None, W=(), eng="dve"):
        def fn(e):
            kw = {}
            if op1 is not None:
                kw["op1"] = op1
            if accum is not None:
                kw["accum_out"] = accum
            return e.tensor_scalar(out=oap, in0=iap, scalar1=s1, scalar2=s2, op0=op0, **kw)
        self.s.op(eng, fn, R=[ib] + list(R), W=[ob] + list(W))

    def stt(self, ob, oap, ab, aap, sc, bb, bap, op0, op1, R=(), eng="dve"):
        self.s.op(eng, lambda e: e.scalar_tensor_tensor(out=oap, in0=aap, scalar=sc, in1=bap, op0=op0, op1=op1),
                  R=[ab, bb] + list(R), W=[ob])

    def memset(self, b, ap, val, eng="pool"):
        self.s.op(eng, lambda e: e.memset(ap, val), R=[], W=[b])

    def dma(self, q, ob, oap, ib, iap, key=None, R=(), W=()):
        self.s.dma(q, lambda e: e.dma_start(out=oap, in_=iap), R=[ib] + list(R), W=[ob] + list(W), key=key or ob.name)

    def build(self):
        S, NT, NS, NPG = self.S, self.NT, self.NS, self.NPG
        nc = bass.Bass("TRN2", target_bir_lowering=False)
        self.nc = nc
        NROWS = self.NPOOL * 128

        def din(name, shape, dt=F32):
            return Buf(name, nc.dram_tensor(name, list(shape), dt, kind="ExternalInput").ap())

        def dout(name, shape, dt=F32):
            return Buf(name, nc.dram_tensor(name, list(shape), dt, kind="ExternalOutput").ap())

        def dscr(name, shape, dt=F32):
            return Buf(name, nc.dram_tensor(name, list(shape), dt, kind="Internal").ap())

        I = {}
        I["xp"] = din("xp", [S, D])
        I["xs"] = din("xs", [NS, D])
        I["memp"] = din("memp", [256, D])
        I["ck"] = din("ck", [2 * NROWS, 256])
        I["cv"] = din("cv", [2 * NROWS, 256])
        I["cik"] = din("cik", [2 * NROWS, 32])
        I["cmk"] = din("cmk", [2, NS, 256, 256])
        I["cmv"] = din("cmv", [2, NS, 256, 256])
        I["spool"] = din("spool", [2, NS, 15, 256])
        I["sconv"] = din("sconv", [2, NS, 2, 2 * DFF])
        I["ptab"] = din("ptab", [NS, NPG], I32)
        I["rel_bias"] = din("rel_bias", [32, 8])
        I["norm1_g"] = din("norm1_g", [2, D])
        I["w_in"] = din("w_in", [2, D, NIN])
        I["q_norm_g"] = din("q_norm_g", [2, 64])
        I["k_norm_g"] = din("k_norm_g", [2, 64])
        I["mem_norm_g"] = din("mem_norm_g", [2, D])
        I["w_mem_kv"] = din("w_mem_kv", [2, D, 512])
        I["mq_norm_g"] = din("mq_norm_g", [2, 64])
        I["mk_norm_g"] = din("mk_norm_g", [2, 64])
        I["w_pool"] = din("w_pool", [2, 4, 64, 64])
        I["pool_scale"] = din("pool_scale", [2, 256])
        I["w_out"] = din("w_out", [2, D, D])
        I["norm2_g"] = din("norm2_g", [2, D])
        I["w_up"] = din("w_up", [2, D, 2 * DFF])
        I["conv_w"] = din("conv_w", [2, 3, 2 * DFF])
        I["conv_b"] = din("conv_b", [2, 2 * DFF])
        I["w_down"] = din("w_down", [2, DFF, D])
        I["c_oh"] = din("c_oh", [32, TABLEN])
        I["c_cinv"] = din("c_cinv", [128, 2, 128])
        I["c_winv"] = din("c_winv", [128, 2])
        I["c_pow2"] = din("c_pow2", [128, NIT + 1])
        I["c_thr"] = din("c_thr", [128, 31])
        I["c_iota"] = din("c_iota", [128, 128])
        I["c_smp"] = din("c_smp", [128, 167])
        self.I = I
        O = {}
        O["y_p"] = dout("y_p", [S, D])
        O["y_s"] = dout("y_s", [NS, D])
        O["k_p"] = dout("k_p", [2, S, 256])
        O["v_p"] = dout("v_p", [2, S, 256])
        O["ik_p"] = dout("ik_p", [2, S, 32])
        O["pool_p"] = dout("pool_p", [2, 15, 256])
        O["conv_p"] = dout("conv_p", [2, 2, 2 * DFF])
        O["mk_p"] = dout("mk_p", [2, 256, 256])
        O["mv_p"] = dout("mv_p", [2, 256, 256])
        O["k_s"] = dout("k_s", [2, NS, 256])
        O["v_s"] = dout("v_s", [2, NS, 256])
        O["ik_s"] = dout("ik_s", [2, NS, 32])
        O["pool_s"] = dout("pool_s", [2, NS, 15, 256])
        O["conv_s"] = dout("conv_s", [2, NS, 2, 2 * DFF])
        self.O = O
        self.XA = dscr("XA", [D, S])
        self.XB = dscr("XB", [D, S])
        self.TAB = dscr("TAB", [8, TABLEN])

        with ExitStack() as esg:
            s = Sched(nc, esg)
            self.s = s
            self.setup_consts()
            for l in range(2):
                if self.dbg <= 1:
                    break
                with ExitStack() as es1:
                    s.es = es1
                    self.phase1(l)
                    s.barrier()
                    s.emit()
                if self.dbg <= 8:
                    break
                with ExitStack() as es2:
                    s.es = es2
                    self.phase2(l)
                    s.barrier()
                    s.emit()
                s.es = esg
        return nc

    def setup_consts(self):
        s, I = self.s, self.I
        c = {}
        self.c = c
        c["ident"] = s.sb("ident", [128, 128], F32)
        c["identb"] = s.sb("identb", [128, 128], BF16)
        c["J"] = s.sb("J", [128, 128], BF16)
        c["ones"] = s.sb("ones", [128, 128], F32)
        c["onesb"] = s.sb("onesb", [128, 128], BF16)
        c["bones"] = s.sb("bones", [128, 128], F32)
        c["cm"] = s.sb("cm", [128, 128], F32)
        c["eps"] = s.sb("eps", [128, 1], F32)
        c["nshift"] = s.sb("nshift", [128, 1], F32)
        c["pow2"] = s.sb("pow2", [128, NIT + 1], F32)
        c["cinv"] = s.sb("cinv", [128, 2, 128], F32)
        c["winv"] = s.sb("winv", [128, 2], F32)
        c["xsT"] = s.sb("xsT", [128, 8, self.NS], F32)
        est = ExitStack()
        s.es = est
        ident, J = c["ident"], c["J"]
        self.memset(ident, ident[:], 0.0)
        s.op("pool", lambda e: e.affine_select(out=ident[:], in_=ident[:], pattern=[[-1, 128]], compare_op=OP.not_equal,
                                               fill=1.0, base=0, channel_multiplier=1), R=[ident], W=[ident])
        self.vcopy(c["identb"], c["identb"][:], ident, ident[:], eng="pool")
        jf = s.sb("jf", [128, 128], F32)
        self.memset(jf, jf[:], 0.0)
        s.op("pool", lambda e: e.affine_select(out=jf[:], in_=jf[:], pattern=[[1, 128]], compare_op=OP.not_equal,
                                               fill=1.0, base=-127, channel_multiplier=1), R=[jf], W=[jf])
        self.vcopy(J, J[:], jf, jf[:], eng="pool")
        self.memset(c["ones"], c["ones"][:], 1.0)
        self.memset(c["onesb"], c["onesb"][:], 1.0)
        bo = c["bones"]
        self.memset(bo, bo[:], 0.0)
        self.memset(bo, bo[0:64, 0:64], 1.0)
        self.memset(bo, bo[64:128, 64:128], 1.0)
        cm = c["cm"]
        self.memset(cm, cm[:], 0.0)
        s.op("pool", lambda e: e.affine_select(out=cm[:], in_=cm[:], pattern=[[-1, 128]], compare_op=OP.is_ge,
                                               fill=NEG, base=0, channel_multiplier=1), R=[cm], W=[cm])
        self.memset(c["eps"], c["eps"][:], EPS)
        self.memset(c["nshift"], c["nshift"][:], -SHIFT)
        self.dma("sp", c["pow2"], c["pow2"][:], I["c_pow2"], I["c_pow2"][:, :])
        self.dma("sp", c["cinv"], c["cinv"][:], I["c_cinv"], I["c_cinv"][:, :, :])
        self.dma("sp", c["winv"], c["winv"][:], I["c_winv"], I["c_winv"][:, :])
        rb = s.sb("rb", [32, 8], F32)
        oh = s.sb("oh", [32, TABLEN], F32)
        tabs = s.sb("tabs", [8, TABLEN], F32)
        self.dma("sp", rb, rb[:], I["rel_bias"], I["rel_bias"][:, :])
        self.dma("sp", oh, oh[:], I["c_oh"], I["c_oh"][:, :])
        pt = s.ps("ptab", [128, 512], F32)
        for c0 in range(0, TABLEN, 512):
            n = min(512, TABLEN - c0)
            self.mm(pt, pt[0:8, 0:n], rb, rb[:], oh, oh[:, c0:c0 + n], True, True)
            self.vcopy(tabs, tabs[:, c0:c0 + n], pt, pt[0:8, 0:n])
        self.dma("sp", self.TAB, self.TAB[:, :], tabs, tabs[:], key="tabst")
        xs_tok = s.sb("xs_tok", [self.NS, D], F32)
        self.dma("sp", xs_tok, xs_tok[:], I["xs"], I["xs"][:, :])
        for k in range(8):
            self.tr(pt, pt[:, k * 4:k * 4 + self.NS], xs_tok, xs_tok[:, k * 128:(k + 1) * 128], ident, ident[0:self.NS, 0:self.NS])
        self.vcopy(c["xsT"], c["xsT"][:].rearrange("p k s -> p (k s)"), pt, pt[:, 0:8 * self.NS])
        s.barrier()
        s.emit()
        est.close()
        s.es = s.es_global

    def rms_feature_major(self, xb, xap3, n, gvec, hT, P_N, tmp_sq, tmp_rs):
        c = self.c
        sq, rs = tmp_sq, tmp_rs
        self.act(sq, sq[:, 0:8, 0:n], xb, xap3, AF.Square)
        for k in range(8):
            self.mm(P_N, P_N[:, 0:n], c["ones"], c["ones"][:], sq, sq[:, k, 0:n], k == 0, k == 7)
        self.act(rs, rs[:, 0:n], P_N, P_N[:, 0:n], AF.Sqrt, bias=c["eps"][:, 0:1], scale=1.0 / D, R=[c["eps"]])
        self.s.op("dve", lambda e: e.reciprocal(out=rs[:, 0:n], in_=rs[:, 0:n]), R=[rs], W=[rs])
        for k in range(8):
            self.stt(hT, hT[:, k, 0:n], xb, xap3[:, k, :], gvec[:, k:k + 1], rs, rs[:, 0:n], OP.mult, OP.mult, R=[gvec])

    def headnorm(self, P_Z, zap, ncols, P_N, sq, rs):
        c = self.c
        self.act(sq, sq[:, 0:ncols], P_Z, zap, AF.Square)
        self.mm(P_N, P_N[:, 0:ncols], c["bones"], c["bones"][:], sq, sq[:, 0:ncols], True, True)
        self.act(rs, rs[:, 0:ncols], P_N, P_N[:, 0:ncols], AF.Sqrt, bias=c["eps"][:, 0:1], scale=1.0 / 64, R=[c["eps"]])
        self.s.op("dve", lambda e: e.reciprocal(out=rs[:, 0:ncols], in_=rs[:, 0:ncols]), R=[rs], W=[rs])
        return rs[:, 0:ncols]

    def load_gvec(self, name, src, l, scale=None):
        g = self.s.sb(name, [128, 8], F32)
        ap = bass.AP(src.t.tensor, l * D, [[1, 128], [128, 8]])
        self.s.dma("sp", lambda e: e.dma_start(out=g[:], in_=ap, allow_slow_non_contiguous=True), R=[src], W=[g], key=name)
        return g

    def load_hvec(self, name, src, l, scale=1.0):
        g = self.s.sb(name, [128, 1], F32)
        for h in range(2):
            ap = bass.AP(src.t.tensor, l * 64, [[1, 64], [1, 1]])
            self.s.dma("sp", lambda e, ap=ap, h=h: e.dma_start(out=g[h * 64:(h + 1) * 64, :], in_=ap), R=[src], W=[g], key=name + str(h))
        if scale != 1.0:
            self.ts(g, g[:], g, g[:], scale, None, OP.mult)
        return g

    def phase1(self, l):
        s, I, O, c = self.s, self.I, self.O, self.c
        S, NT, NS = self.S, self.NT, self.NS
        W = {}
        wsrc = I["w_in"]
        win3 = wsrc[l].rearrange("(k p) c -> p k c", p=128)
        wf = s.sb("wf", [128, 8, 14 * 128], BF16)
        W["wf"] = wf
        for pos, h in enumerate(QPERM):
            self.dma("pool", wf, wf[:, :, pos * 64:(pos + 1) * 64], wsrc, win3[:, :, h * 64:(h + 1) * 64], key=None)
        self.dma("pool", wf, wf[:, :, 512:768], wsrc, win3[:, :, 512:768], key=None)
        self.dma("pool", wf, wf[:, :, 768:1024], wsrc, win3[:, :, 1576:1832], key=None)
        self.memset(wf, wf[:, :, 1024:1408], 0.0, eng="dve")
        for h in range(8):
            d0 = 1024 + (h // 3) * 128 + (h % 3) * 32
            self.dma("pool", wf, wf[:, :, d0:d0 + 32], wsrc, win3[:, :, 1024 + h * 32:1024 + (h + 1) * 32], key=None)
        for r in range(4):
            self.dma("pool", wf, wf[:, :, 1408 + r * 32:1408 + (r + 1) * 32], wsrc, win3[:, :, 1280:1312], key=None)
        self.dma("pool", wf, wf[:, :, 1536:1792], wsrc, win3[:, :, 1320:1576], key=None)
        wt = s.sb("wt", [128, 8, 296], BF16)
        W["wt"] = wt
        self.dma("pool", wt, wt[:, :, 0:256], wsrc, win3[:, :, 768:1024], key=None)
        self.dma("pool", wt, wt[:, :, 256:296], wsrc, win3[:, :, 1280:1320], key=None)
        wo = s.sb("wo", [128, 8, D], BF16)
        wm3 = I["w_mem_kv"][l].rearrange("(k p) c -> p k c", p=128)
        self.dma("pool", wo, wo[:, :, 0:512], I["w_mem_kv"], wm3, key=None)
        g1 = self.load_gvec("g1", I["norm1_g"], l)
        gm = self.load_gvec("gm", I["mem_norm_g"], l)
        gq = self.load_hvec("gq", I["q_norm_g"], l, scale=0.125)
        gk = self.load_hvec("gk", I["k_norm_g"], l)
        gmq = self.load_hvec("gmq", I["mq_norm_g"], l, scale=0.125)
        gmk = self.load_hvec("gmk", I["mk_norm_g"], l)
        psc = s.sb("psc", [128, 2], F32)
        self.s.dma("sp", lambda e: e.dma_start(out=psc[:], in_=bass.AP(I["pool_scale"].t.tensor, l * 256, [[1, 128], [128, 2]]),
                                               allow_slow_non_contiguous=True), R=[I["pool_scale"]], W=[psc], key="psc")
        wpb = s.sb("wpb", [128, 2, 128], F32)
        self.memset(wpb, wpb[:], 0.0)
        for g in range(4):
            cc, hh = g // 2, g % 2
            self.dma("sp", wpb, wpb[hh * 64:(hh + 1) * 64, cc, hh * 64:(hh + 1) * 64], I["w_pool"], I["w_pool"][l, g, :, :], key="wpb%d" % g)
        P = [s.ps("P%d" % i, [128, 512], F32) for i in range(8)]
        self.P = P
        mkT = s.sb("mkT", [128, 2, 256], BF16)
        mvb = s.sb("mvb", [128, 2, 256], BF16)
        T = {}
        T["xT2"] = [s.sb("xT%d" % i, [128, 8, 128], F32) for i in range(2)]
        T["xT"] = T["xT2"][0]
        T["rs1"] = s.sb("rs1", [128, 128], F32)
        T["hT"] = s.sb("hT", [128, 8, 128], BF16)
        T["sq"] = s.sb("sq", [128, 512], F32)
        T["rs"] = s.sb("rs", [128, 512], F32)
        T["qT"] = s.sb("qT", [128, 4, 128], BF16)
        T["qTm2"] = [s.sb("qTm%d" % i, [128, 2, 4, 128], BF16) for i in range(2)]
        for q_ in T["qTm2"]:
            self.memset(q_, q_[:], 0.0)
        T["qTm"] = T["qTm2"][0]
        T["mqT"] = s.sb("mqT", [128, 2, 128], BF16)
        T["mqm2"] = [s.sb("mqm%d" % i, [128, 2, 2, 128], BF16) for i in range(2)]
        for mq_ in T["mqm2"]:
            self.memset(mq_, mq_[:], 0.0)
        T["mqm"] = T["mqm2"][0]
        T["iqT"] = s.sb("iqT", [128, 3, 128], BF16)
        T["kn32"] = s.sb("kn32", [128, 2, 128], F32)
        T["ktok"] = s.sb("ktok", [128, 256], F32)
        T["tm"] = s.sb("tm", [128, 296], F32)
        T["iw16"] = s.sb("iw16", [128, 8], F32)
        T["mixT"] = s.sb("mixT", [128, 8, 128], BF16)
        T["x1"] = s.sb("x1", [128, 8, 128], F32)
        T["r"] = Ring([s.sb("r%d" % i, [128, 512], F32) for i in range(2)])
        T["pb"] = Ring([s.sb("pb%d" % i, [128, 512], BF16) for i in range(2)])
        T["pm"] = Ring([s.sb("pm%d" % i, [128, 512], BF16) for i in range(2)])
        T["rden"] = s.sb("rden", [128, 512], F32)
        T["bs"] = s.sb("bs", [128, 8], F32)
        T["dd"] = s.sb("dd", [128, NIT + 1], F32)
        T["pl"] = [s.sb("pl%d" % i, [128, 2, 143], F32) for i in range(2)]
        T["pooled"] = s.sb("pooled", [128, 2, 128], F32)
        self.T = T
        self.Wt = W
        esp = ExitStack()
        es_phase = s.es
        s.es = esp
        H = s.sb("H", [128, 8, HLEN], BF16)
        c["H"] = H
        for h in range(8):
            hsrc = bass.AP(self.TAB.t.tensor, h * TABLEN, [[1, 128], [1, HLEN]])
            self.dma("pool", H, H[:, h, :], self.TAB, hsrc, key="Hld")
        kT = s.sb("kT", [128, 2, S], BF16)
        vc = s.sb("vc", [128, NT, 256], BF16)
        ikT = s.sb("ikT", [128, S], BF16)
        Sc = s.sb("Sc", [128, S], F32)
        mb = s.sb("mb", [128, S], BF16)
        maskT = s.sb("maskT", [128, NT, 128], BF16)
        ubuf2 = [s.sb("ubuf%d" % i, [128, 2, 143], F32) for i in range(2)]
        for ub_ in ubuf2:
            self.memset(ub_, ub_[:], 0.0)
        kTt = [Buf("kTt%d" % i, kT.t) for i in range(NT)]
        vct = [Buf("vct%d" % i, vc.t) for i in range(NT)]
        ikTt = [Buf("ikTt%d" % i, ikT.t) for i in range(NT)]
        T["xtok"] = s.sb("xtok", [128, D], F32)
        self.lay = dict(l=l, g1=g1, gq=gq, gk=gk, gmq=gmq, gmk=gmk, psc=psc, wpb=wpb, wo=wo, kT=kT, vc=vc, ikT=ikT, Sc=Sc,
                        mb=mb, maskT=maskT, mkT=mkT, mvb=mvb, ubuf2=ubuf2, kTt=kTt, vct=vct, ikTt=ikTt)

        self.mem_kv(l, gm, gmk)
        wo_src = I["w_out"]
        for j in range(4):
            for hh in range(2):
                h = QPERM[2 * j + hh]
                self.dma("pool", wo, wo[hh * 64:(hh + 1) * 64, j, :], wo_src, wo_src[l, h * 64:(h + 1) * 64, :], key=None)
        self.dma("pool", wo, wo[:, 4:8, :], wo_src, wo_src[l, 512:1024, :].rearrange("(k p) c -> p k c", p=128), key=None)

        for (_c, fn, _t) in self.prompt_A(l, 0):
            fn()
        for t in range(NT):
            UB = self.prompt_B(l, t)
            UA = self.prompt_A(l, t + 1) if t + 1 < NT else []
            self.merge_run(UB, UA)
        s.barrier()
        s.emit()
        esp.close()
        s.es = es_phase
        for kdead in ("kT", "vc", "ikT", "Sc", "mb", "maskT", "ubuf2", "kTt", "vct", "ikTt"):
            self.lay.pop(kdead)
        T.pop("xtok")
        T["xT"], T["mqm"] = T["xT2"][0], T["mqm2"][0]
        if self.dbg <= 8:
            return
        self.sample_tile(l)

    def mem_kv(self, l, gm, gmk):
        s, I, O, c, T, P = self.s, self.I, self.O, self.c, self.T, self.P
        lay = self.lay
        wo, mkT, mvb = lay["wo"], lay["mkT"], lay["mvb"]
        xtok, xT = T["xtok"], T["xT"]
        for i in range(2):
            self.dma("sp", xtok, xtok[:], I["memp"], I["memp"][i * 128:(i + 1) * 128, :])
            for k0 in (0, 4):
                for k in range(k0, k0 + 4):
                    self.tr(P[7], P[7][:, (k - k0) * 128:(k - k0 + 1) * 128], xtok, xtok[:, k * 128:(k + 1) * 128], c["ident"], c["ident"][:])
                self.acopy(xT, xT[:, k0:k0 + 4, :], P[7], P[7][:, 0:512].rearrange("p (k t) -> p k t", k=4))
            self.rms_feature_major(xT, xT[:], 128, gm, T["hT"], P[2], T["x1"], T["rs1"])
            hT = T["hT"]
            for g in range(4):
                for k in range(8):
                    self.mm(P[0], P[0][:, g * 128:(g + 1) * 128], wo, wo[:, k, g * 128:(g + 1) * 128], hT, hT[:, k, :], k == 0, k == 7)
            rs = self.headnorm(P[0], P[0][:, 0:256], 256, P[2], T["sq"], T["rs"])
            kn = T["kn32"]
            self.stt(kn, kn[:].rearrange("p c t -> p (c t)"), P[0], P[0][:, 0:256], gmk[:, 0:1], T["rs"], rs, OP.mult, OP.mult, R=[gmk])
            self.acopy(mkT, mkT[:, :, i * 128:(i + 1) * 128], kn, kn[:])
            for cc in range(2):
                self.tr(P[7], P[7][:, cc * 128:(cc + 1) * 128], kn, kn[:, cc, :], c["ident"], c["ident"][:])
            self.vcopy(T["ktok"], T["ktok"][:], P[7], P[7][:, 0:256])
            self.dma("sp", O["mk_p"], O["mk_p"][l, i * 128:(i + 1) * 128, :], T["ktok"], T["ktok"][:], key="ktok")
            vn = T["x1"]
            self.acopy(vn, vn[:, 0:2, :].rearrange("p c t -> p (c t)"), P[0], P[0][:, 256:512])
            for cc in range(2):
                self.tr(P[7], P[7][:, 256 + cc * 128:256 + (cc + 1) * 128], vn, vn[:, cc, :], c["ident"], c["ident"][:])
            self.vcopy(T["tm"], T["tm"][:, 0:256], P[7], P[7][:, 256:512])
            self.dma("sp", O["mv_p"], O["mv_p"][l, i * 128:(i + 1) * 128, :], T["tm"], T["tm"][:, 0:256], key="tm")
            self.acopy(mvb, mvb[:, i, :], T["tm"], T["tm"][:, 0:256])

    def project(self, l, xb, xap3, n, P):
        s, c, T, lay, Wt = self.s, self.c, self.T, self.lay, self.Wt
        wf, wt = Wt["wf"], Wt["wt"]
        hT = T["hT"]
        self.rms_feature_major(xb, xap3, n, lay["g1"], hT, P[2], T["x1"], T["rs1"])

        def fm(Pb, slot, grp):
            for k in range(8):
                self.mm(Pb, Pb[:, slot * 128:slot * 128 + n], wf, wf[:, k, grp * 128:(grp + 1) * 128], hT, hT[:, k, 0:n], k == 0, k == 7)
        for j in range(4):
            fm(P[0], j, j)
        for j in range(4):
            fm(P[1], j, 4 + j)
        qT, mqT, kn = T["qT"], T["mqT"], T["kn32"]
        if n == 128:
            rs = self.headnorm(P[0], P[0][:, 0:512], 512, P[2], T["sq"], T["rs"])
            qTm = T["qTm"]
            for hh in range(2):
                pr = slice(hh * 64, (hh + 1) * 64)
                self.stt(qTm, qTm[pr, hh, :, :].rearrange("p c t -> p (c t)"), P[0], P[0][pr, 0:512], lay["gq"][pr, 0:1], T["rs"], rs[pr, :],
                         OP.mult, OP.mult, R=[lay["gq"]])
            rs = self.headnorm(P[1], P[1][:, 0:512], 512, P[2], T["sq"], T["rs"])
            self.stt(kn, kn[:].rearrange("p c t -> p (c t)"), P[1], P[1][:, 0:256], lay["gk"][:, 0:1], T["rs"], rs[:, 0:256], OP.mult, OP.mult, R=[lay["gk"]])
            self.stt(mqT, mqT[:].rearrange("p c t -> p (c t)"), P[1], P[1][:, 256:512], lay["gmq"][:, 0:1], T["rs"], rs[:, 256:512], OP.mult, OP.mult, R=[lay["gmq"]])
        else:
            for (Pb, nslot) in ((P[0], 4), (P[1], 4)):
                pass
            sq, rsb = T["sq"], T["rs"]
            for Pb in (P[0], P[1]):
                for j in range(4):
                    self.act(sq, sq[:, j * 128:j * 128 + n], Pb, Pb[:, j * 128:j * 128 + n], AF.Square)
                    self.mm(P[2], P[2][:, j * 128:j * 128 + n], c["bones"], c["bones"][:], sq, sq[:, j * 128:j * 128 + n], True, True)
                for j in range(4):
                    js = slice(j * 128, j * 128 + n)
                    self.act(rsb, rsb[:, js], P[2], P[2][:, js], AF.Sqrt, bias=c["eps"][:, 0:1], scale=1.0 / 64, R=[c["eps"]])
                    self.s.op("dve", lambda e, js=js: e.reciprocal(out=rsb[:, js], in_=rsb[:, js]), R=[rsb], W=[rsb])
                if Pb is P[0]:
                    for j in range(4):
                        self.stt(qT, qT[:, j, 0:n], Pb, Pb[:, j * 128:j * 128 + n], lay["gq"][:, 0:1], rsb, rsb[:, j * 128:j * 128 + n], OP.mult, OP.mult, R=[lay["gq"]])
                else:
                    for j in range(2):
                        self.stt(kn, kn[:, j, 0:n], Pb, Pb[:, j * 128:j * 128 + n], lay["gk"][:, 0:1], rsb, rsb[:, j * 128:j * 128 + n], OP.mult, OP.mult, R=[lay["gk"]])
                    for j in range(2):
                        self.stt(mqT, mqT[:, j, 0:n], Pb, Pb[:, (2 + j) * 128:(2 + j) * 128 + n], lay["gmq"][:, 0:1], rsb, rsb[:, (2 + j) * 128:(2 + j) * 128 + n], OP.mult, OP.mult, R=[lay["gmq"]])
        mqm = T["mqm"]
        self.vcopy(mqm, mqm[0:64, 0, :, 0:n], mqT, mqT[0:64, :, 0:n], eng="pool")
        self.vcopy(mqm, mqm[64:128, 1, :, 0:n], mqT, mqT[64:128, :, 0:n], eng="pool")
        for j in range(4):
            fm(P[0], j, 8 + j)
        for j in range(2):
            fm(P[1], j, 12 + j)
        iqT = T["iqT"]
        for j in range(3):
            self.acopy(iqT, iqT[:, j, 0:n], P[0], P[0][:, j * 128:j * 128 + n])
        for k in range(8):
            self.mm(P[7], P[7][0:n, 0:296], hT, hT[:, k, 0:n], wt, wt[:, k, :], k == 0, k == 7)
        tm = T["tm"]
        self.vcopy(tm, tm[0:n, :], P[7], P[7][0:n, 0:296])
        self.ts(T["iw16"], T["iw16"][0:n, :], tm, tm[0:n, 288:296], 1.0 / 16.0, None, OP.mult)

    def merge_run(self, UB, UA):
        tb = sum(u[0] for u in UB) or 1.0
        ta = sum(u[0] for u in UA) or 1.0
        ia = ib = 0
        ca = cb = 0.0
        while ia < len(UA) or ib < len(UB):
            pick_a = ib >= len(UB) or (ia < len(UA) and ca / ta < cb / tb)
            if pick_a and UA[ia][2] == "mask":
                while any(u[2] == "att" for u in UB[ib:]):
                    cb += UB[ib][0]
                    UB[ib][1]()
                    ib += 1
            if pick_a:
                ca += UA[ia][0]
                UA[ia][1]()
                ia += 1
            else:
                cb += UB[ib][0]
                UB[ib][1]()
                ib += 1

    def prompt_A(self, l, t):
        s, I, O, c, T, P, lay = self.s, self.I, self.O, self.c, self.T, self.P, self.lay
        S, NT = self.S, self.NT
        U = []
        add = lambda cost, fn, tag=None: U.append((cost, fn, tag))
        p = t % 2
        cols = slice(t * 128, (t + 1) * 128)
        xT = T["xT2"][p]
        ub = lay["ubuf2"][p]
        kT, vc, ikT = lay["kT"], lay["vc"], lay["ikT"]
        kTt, vct, ikTt = lay["kTt"], lay["vct"], lay["ikTt"]
        Sc, mb, maskT = lay["Sc"], lay["mb"], lay["maskT"]

        def load():
            if l == 0:
                xtok = T["xtok"]
                self.dma("sp", xtok, xtok[:], I["xp"], I["xp"][cols, :])
                for k0 in (0, 4):
                    for k in range(k0, k0 + 4):
                        self.tr(P[7], P[7][:, (k - k0) * 128:(k - k0 + 1) * 128], xtok, xtok[:, k * 128:(k + 1) * 128], c["ident"], c["ident"][:])
                    self.acopy(xT, xT[:, k0:k0 + 4, :], P[7], P[7][:, 0:512].rearrange("p (k t) -> p k t", k=4))
            else:
                self.dma("sp", xT, xT[:], self.XB, self.XB[:, cols].rearrange("(k p) t -> p k t", p=128))
        add(6.0, load)

        def proj():
            T["qTm"], T["mqm"] = T["qTm2"][p], T["mqm2"][p]
            self.project(l, xT, xT[:], 128, P)
        add(40.0, proj)

        def caches():
            kn, tm = T["kn32"], T["tm"]
            s.op("act", lambda e: e.copy(out=kT[:, :, cols], in_=kn[:]), R=[kn], W=[kTt[t]])
            s.op("act", lambda e: e.copy(out=ikT[:, cols], in_=P[0][:, 384:512]), R=[P[0]], W=[ikTt[t]])
            s.op("act", lambda e: e.copy(out=vc[:, t, :], in_=tm[:, 0:256]), R=[tm], W=[vct[t]])
            for cc in range(2):
                self.tr(P[7], P[7][:, cc * 128:(cc + 1) * 128], kn, kn[:, cc, :], c["ident"], c["ident"][:])
            self.vcopy(T["ktok"], T["ktok"][:], P[7], P[7][:, 0:256])
            self.dma("sp", O["k_p"], O["k_p"][l, cols, :], T["ktok"], T["ktok"][:], key="ktok")
            self.dma("sp", O["v_p"], O["v_p"][l, cols, :], tm, tm[:, 0:256], key="tm")
            self.dma("sp", O["ik_p"], O["ik_p"][l, cols, :], tm, tm[:, 256:288], key="tm2")
            self.acopy(ub, ub[:, :, 15:143], P[1], P[1][:, 0:256].rearrange("p (c t) -> p c t", c=2))
            if t > 0:
                ubp = lay["ubuf2"][1 - p]
                self.vcopy(ub, ub[:, :, 0:15], ubp, ubp[:, :, 128:143])
            if t == NT - 1:
                for cc in range(2):
                    self.tr(P[7], P[7][:, 256 + cc * 128:256 + (cc + 1) * 128], ub, ub[:, cc, 15:143], c["ident"], c["ident"][:])
                self.vcopy(T["xtok"], T["xtok"][:, 0:256], P[7], P[7][:, 256:512])
                self.dma("sp", O["pool_p"], O["pool_p"][l, :, :], T["xtok"], T["xtok"][113:128, 0:256], key="xtok_o")
        add(8.0, caches)

        iqT, iw, bs, dd = T["iqT"], T["iw16"], T["bs"], T["dd"]
        Wd = (t + 1) * 128
        chunks = [(c0, min(512, Wd - c0)) for c0 in range(0, Wd, 512)]
        xi = 0
        for h in range(8):
            for (c0, n) in chunks:
                def ix(h=h, c0=c0, n=n, xi=xi):
                    pbs = (h % 3) * 32
                    Px = P[xi % 2]
                    tl = [ikTt[j] for j in range(c0 // 128, (c0 + n) // 128)]
                    s.op("pe", lambda e: e.matmul(Px[:, 0:n], lhsT=iqT[pbs:pbs + 32, h // 3, :], rhs=ikT[pbs:pbs + 32, c0:c0 + n],
                                                  start=True, stop=True), R=[iqT] + tl, W=[Px])
                    r = T["r"].next()
                    self.act(r, r[:, 0:n], Px, Px[:, 0:n], AF.Relu)
                    if h == 0:
                        self.ts(Sc, Sc[:, c0:c0 + n], r, r[:, 0:n], iw[:, 0:1], None, OP.mult, R=[iw], eng=IXENG)
                    else:
                        self.stt(Sc, Sc[:, c0:c0 + n], r, r[:, 0:n], iw[:, h:h + 1], Sc, Sc[:, c0:c0 + n], OP.mult, OP.add, R=[iw], eng=IXENG)
                add(0.15 + 0.8 * n / 512.0, ix)
                xi += 1

        def rng():
            s.op("dve", lambda e: e.tensor_reduce(out=bs[:, 0:1], in_=Sc[:, 0:Wd], axis=AX.X, op=OP.min), R=[Sc], W=[bs])
            s.op("dve", lambda e: e.tensor_reduce(out=bs[:, 1:2], in_=Sc[:, 0:Wd], axis=AX.X, op=OP.max), R=[Sc], W=[bs])
            self.tt(Sc, Sc[:, t * 128:Wd], Sc, Sc[:, t * 128:Wd], c["cm"], c["cm"][:], OP.add)
            self.tt(bs, bs[:, 2:3], bs, bs[:, 1:2], bs, bs[:, 0:1], OP.subtract)
            self.ts(dd, dd[:], c["pow2"], c["pow2"][:], bs[:, 2:3], None, OP.mult, R=[bs])
            self.tt(bs, bs[:, 3:4], bs, bs[:, 0:1], dd, dd[:, 1:2], OP.add)
        add(3.0 + 2.8 * Wd / 1000.0, rng)
        kk = float(self.KTOP) - 0.5
        for i in range(NIT):
            def bis(i=i):
                self.ts(mb, mb[:, 0:Wd], Sc, Sc[:, 0:Wd], bs[:, 3:4], None, OP.is_ge, OP.add, R=[bs], accum=bs[:, 4:5], W=[bs])
                self.ts(bs, bs[:, 5:6], bs, bs[:, 4:5], kk, -0.5, OP.is_ge, OP.add)
                self.stt(bs, bs[:, 3:4], bs, bs[:, 5:6], dd[:, i + 1:i + 2], bs, bs[:, 3:4], OP.mult, OP.add, R=[dd])
            add(1.5 + 1.4 * Wd / 1000.0, bis)

        def fin():
            self.stt(bs, bs[:, 6:7], dd, dd[:, NIT:NIT + 1], -1.0, bs, bs[:, 3:4], OP.mult, OP.add)
            self.ts(mb, mb[:, 0:Wd], Sc, Sc[:, 0:Wd], bs[:, 6:7], None, OP.is_ge, R=[bs])
        add(1.0 + 1.4 * Wd / 1000.0, fin)
        Pm = P[2]
        for j0 in range(0, t + 1, 8):
            def mtr(j0=j0):
                nj = min(8, t + 1 - j0)
                pmv = Pm[:].bitcast(BF16)
                for j in range(j0, j0 + nj):
                    self.tr(Pm, pmv[:, (j - j0) * 128:(j - j0 + 1) * 128], mb, mb[:, j * 128:(j + 1) * 128], c["identb"], c["identb"][:])
                self.acopy(maskT, maskT[:, j0:j0 + nj, :], Pm, pmv[:, 0:nj * 128].rearrange("p (j q) -> p j q", j=nj))
            add(1.5, mtr, "mask")
        return U

    def prompt_B(self, l, t):
        s, I, O, c, T, P, lay = self.s, self.I, self.O, self.c, self.T, self.P, self.lay
        U = []
        add = lambda cost, fn, tag=None: U.append((cost, fn, tag))
        p = t % 2
        cols = slice(t * 128, (t + 1) * 128)
        xT, qTm, mqm, ub = T["xT2"][p], T["qTm2"][p], T["mqm2"][p], lay["ubuf2"][p]
        kT, vc, maskT = lay["kT"], lay["vc"], lay["maskT"]
        kTt, vct = lay["kTt"], lay["vct"]
        mixT, H, J = T["mixT"], c["H"], c["J"]
        PO, PD = P[5], P[6]
        items = [(half, j) for half in range(2) for j in range(t + 1)]

        def qk(i):
            half, j = items[i]
            Pst = P[3 + (i % 2)]
            kc = slice(j * 128, (j + 1) * 128)
            d0 = min((t - j) * 128, HCLAMP)
            for jj in range(2):
                ch = 2 * half + jj
                for hh in range(2):
                    hq = QPERM[2 * ch + hh]
                    n = hq // 2
                    col = (jj * 2 + hh) * 128
                    self.mm(Pst, Pst[:, col:col + 128], kTt[j], kT[:, n // 2, kc], qTm, qTm[:, hh, ch, :], True, False)
                    self.mm(Pst, Pst[:, col:col + 128], J, J[:], H, H[:, hq, d0:d0 + 128], False, True)

        pms = {}

        def em(i):
            half, j = items[i]
            Pst = P[3 + (i % 2)]
            pb = T["pb"].next()
            self.act(pb, pb[:], Pst, Pst[:], AF.Exp, bias=c["nshift"][:, 0:1], scale=1.0, R=[c["nshift"]])
            pm = T["pm"].next()
            pms[i] = pm
            self.tt(pm, pm[:].rearrange("p (h q) -> p h q", h=4), pb, pb[:].rearrange("p (h q) -> p h q", h=4),
                    maskT, maskT[:, j, :].unsqueeze(1).to_broadcast([128, 4, 128]), OP.mult, eng=MKENG)

        def pv(i):
            half, j = items[i]
            pm = pms.pop(i)
            for jj in range(2):
                ch = 2 * half + jj
                nb = 2 * (ch // 2) * 64
                for hh in range(2):
                    col = (jj * 2 + hh) * 128
                    first = (j == 0 and jj == 0 and hh == 0)
                    lastm = (j == t and jj == 1 and hh == 1)
                    self.mm(PO, PO[:, col:col + 128], vct[j], vc[:, j, nb:nb + 128], pm, pm[:, col:col + 128], first, lastm)
            self.mm(PD, PD[:], c["onesb"], c["onesb"][:], pm, pm[:], j == 0, j == t)
            if j == t:
                rden = T["rden"]
                s.op("dve", lambda e: e.reciprocal(out=rden[:], in_=PD[:]), R=[PD], W=[rden])
                for jj in range(2):
                    ch = 2 * half + jj
                    for hh in range(2):
                        col = (jj * 2 + hh) * 128
                        pr = slice(hh * 64, (hh + 1) * 64)
                        self.tt(mixT, mixT[pr, ch, :], PO, PO[pr, col:col + 128], rden, rden[pr, col:col + 128], OP.mult)

        def pro():
            qk(0)
            em(0)
        add(2.5, pro, "att")
        for i in range(len(items)):
            def au(i=i):
                if i + 1 < len(items):
                    qk(i + 1)
                    em(i + 1)
                pv(i)
            add(2.9, au, "att")
        add(10.0, lambda: self.pool_mix(t == 0, 128, ub, bank=3))
        add(15.0, lambda: self.mem_attend(128, mqm=mqm, pst=(3, 4)))

        def outp():
            self.out_proj(xT, xT[:], 128, banks=(3, 4), dst=xT)
            self.dma("sp", self.XA, self.XA[:, cols].rearrange("(k p) t -> p k t", p=128), xT, xT[:], key="x1st")
        add(14.0, outp)
        return U


    def out_proj(self, xb, xap3, n, banks=(7, 2), dst=None):
        T, P, lay = self.T, self.P, self.lay
        wo, mixT, x1 = lay["wo"], T["mixT"], (dst or T["x1"])
        for half in range(2):
            Pb = P[banks[half]]
            for j in range(4):
                dm = half * 4 + j
                for k in range(8):
                    self.mm(Pb, Pb[:, j * 128:j * 128 + n], wo, wo[:, k, dm * 128:(dm + 1) * 128], mixT, mixT[:, k, 0:n], k == 0, k == 7)
            if n == 128:
                self.tt(x1, x1[:, half * 4:half * 4 + 4, :].rearrange("p c t -> p (c t)"), Pb, Pb[:, 0:512],
                        xb, xap3[:, half * 4:half * 4 + 4, :].rearrange("p c t -> p (c t)"), OP.add)
            else:
                for j in range(4):
                    dm = half * 4 + j
                    self.tt(x1, x1[:, dm, 0:n], Pb, Pb[:, j * 128:j * 128 + n], xb, xap3[:, dm, :], OP.add)

    def pool_mix(self, first, n, ubuf, oc=0, bank=7):
        s, c, T, P, lay = self.s, self.c, self.T, self.P, self.lay
        A, B = T["pl"]
        Wn = 15 + n
        self.tt(A, A[:, :, 1:Wn], ubuf, ubuf[:, :, 1:Wn], ubuf, ubuf[:, :, 0:Wn - 1], OP.add)
        self.tt(B, B[:, :, 3:Wn], A, A[:, :, 3:Wn], A, A[:, :, 1:Wn - 2], OP.add)
        pooled = T["pooled"]
        self.vcopy(pooled, pooled[0:64, 0, 0:n], A, A[0:64, 0, 15:Wn])
        self.vcopy(pooled, pooled[64:128, 0, 0:n], B, B[64:128, 0, 15:Wn])
        self.tt(A, A[:, 1, 7:Wn], B, B[:, 1, 7:Wn], B, B[:, 1, 3:Wn - 4], OP.add)
        self.vcopy(pooled, pooled[0:64, 1, 0:n], A, A[0:64, 1, 15:Wn])
        self.tt(B, B[64:128, 1, 15:Wn], A, A[64:128, 1, 15:Wn], A, A[64:128, 1, 7:Wn - 8], OP.add)
        self.vcopy(pooled, pooled[64:128, 1, 0:n], B, B[64:128, 1, 15:Wn])
        if first:
            self.tt(pooled, pooled[:, :, 0:n], pooled, pooled[:, :, 0:n], c["cinv"], c["cinv"][:, :, 0:n], OP.mult)
            self.tt(pooled, pooled[:, :, 0:n], pooled, pooled[:, :, 0:n], ubuf, ubuf[:, :, 15:Wn], OP.subtract)
        else:
            for cc in range(2):
                self.stt(pooled, pooled[:, cc, 0:n], pooled, pooled[:, cc, 0:n], c["winv"][:, cc:cc + 1], ubuf, ubuf[:, cc, 15:Wn],
                         OP.mult, OP.subtract, R=[c["winv"]])
        wpb, psc, mixT = lay["wpb"], lay["psc"], T["mixT"]
        for cc in range(2):
            self.mm(P[bank], P[bank][:, cc * 128:cc * 128 + n], wpb, wpb[:, cc, :], pooled, pooled[:, cc, 0:n], True, True)
            self.ts(mixT, mixT[:, 4 + cc, oc:oc + n], P[bank], P[bank][:, cc * 128:cc * 128 + n], psc[:, cc:cc + 1], None, OP.mult, R=[psc])

    def mem_attend(self, n, mkT=None, mvb=None, qc=0, mqm=None, pst=(0, 1)):
        s, c, T, P, lay = self.s, self.c, self.T, self.P, self.lay
        mkT = mkT or lay["mkT"]
        mvb = mvb or lay["mvb"]
        mqm, mixT = (mqm or T["mqm"]), T["mixT"]
        PO, PD = P[5], P[6]
        for h in range(4):
            for i in range(2):
                Pst = P[pst[i]]
                cc, hh = h // 2, h % 2
                self.mm(Pst, Pst[:, h * 128:h * 128 + n], mkT, mkT[:, cc, i * 128:(i + 1) * 128],
                        mqm, mqm[:, hh, cc, qc:qc + n], True, True)
        km = 9
        for i in range(2):
            Pst = P[pst[i]]
            pb = T["pb"].next()
            if n == 128:
                self.act(pb, pb[:], Pst, Pst[:], AF.Exp, bias=c["nshift"][:, 0:1], scale=1.0, R=[c["nshift"]])
            else:
                for h in range(4):
                    self.act(pb, pb[:, h * 128:h * 128 + n], Pst, Pst[:, h * 128:h * 128 + n], AF.Exp, bias=c["nshift"][:, 0:1], scale=1.0, R=[c["nshift"]])
            for h in range(4):
                cc = h // 2
                first = (i == 0 and h == 0)
                lastm = (i == 1 and h == 3)
                self.mm(PO, PO[:, h * 128:h * 128 + n], mvb, mvb[:, i, cc * 128:(cc + 1) * 128], pb, pb[:, h * 128:h * 128 + n], first, lastm)
                self.mm(PD, PD[:, h * 128:h * 128 + n], c["onesb"], c["onesb"][:], pb, pb[:, h * 128:h * 128 + n], first, lastm)
        rden = T["rden"]
        for h in range(4):
            cc, hh = h // 2, h % 2
            pr = slice(hh * 64, (hh + 1) * 64)
            cs = slice(h * 128, h * 128 + n)
            s.op("dve", lambda e, pr=pr, cs=cs: e.reciprocal(out=rden[pr, cs], in_=PD[pr, cs]), R=[PD], W=[rden])
            self.tt(mixT, mixT[pr, 6 + cc, qc:qc + n], PO, PO[pr, cs], rden, rden[pr, cs], OP.mult)

    def sample_tile(self, l):
        s, I, O, c, T, P, lay = self.s, self.I, self.O, self.c, self.T, self.P, self.lay
        NS = self.NS
        xsT = c["xsT"]
        hT, wf = T["hT"], self.Wt["wf"]
        self.project(l, xsT, xsT[:], NS, P)
        kn, tm = T["kn32"], T["tm"]
        usT = s.sb("usT", [128, 2, NS], F32)
        self.vcopy(usT, usT[:], P[1], P[1][:, 0:256].rearrange("p (c t) -> p c t", c=2)[:, :, 0:NS])
        ikn = s.sb("ikn", [128, NS], BF16)
        self.vcopy(ikn, ikn[:], P[0], P[0][:, 384:384 + NS])
        self.smp = dict(usT=usT, ikn=ikn)
        ks = s.sb("ks_tok", [NS, 256], F32)
        for cc in range(2):
            self.tr(P[3], P[3][0:NS, cc * 128:(cc + 1) * 128], kn, kn[:, cc, 0:NS], c["ident"], c["ident"][:])
        self.vcopy(ks, ks[:], P[3], P[3][0:NS, 0:256])
        self.smp["ks"] = ks
        self.dma("sp", O["k_s"], O["k_s"][l, :, :], ks, ks[:], key="ks")
        self.dma("sp", O["v_s"], O["v_s"][l, :, :], tm, tm[0:NS, 0:256], key="tm")
        self.dma("sp", O["ik_s"], O["ik_s"][l, :, :], tm, tm[0:NS, 256:288], key="tm2")
        for k in range(8):
            self.mm(P[4], P[4][0:NS, 0:256], hT, hT[:, k, 0:NS], wf, wf[:, k, 1536:1792], k == 0, k == 7)
        us = s.sb("us_tok", [NS, 256], F32)
        self.vcopy(us, us[:], P[4], P[4][0:NS, 0:256])
        mixT = T["mixT"]
        self.memset(mixT, mixT[:, 0:4, 0:NS], 0.0, eng="dve")
        st = s.sb("st_tok", [15, 256], F32)
        ubs = s.sb("ubs", [128, 2, 16], F32)
        mkt = s.sb("mkt", [128, 2, 256], BF16)
        mvs = s.sb("mvs", [128, 2, 256], BF16)
        mks = s.sb("mks", [128, 2, 256], BF16)
        for si in range(NS):
            self.dma("sp", st, st[:], I["spool"], I["spool"][l, si, :, :], key="st")
            self.dma("sp", O["pool_s"], O["pool_s"][l, si, 0:14, :], st, st[1:15, :], key="st_o")
            self.dma("sp", O["pool_s"], O["pool_s"][l, si, 14:15, :], us, us[si:si + 1, :], key="us_o")
            for cc in range(2):
                self.tr(P[7], P[7][:, cc * 16:cc * 16 + 15], st, st[:, cc * 128:(cc + 1) * 128], c["ident"], c["ident"][0:15, 0:15])
            self.vcopy(ubs, ubs[:, :, 0:15], P[7], P[7][:, 0:32].rearrange("p (c t) -> p c t", c=2)[:, :, 0:15])
            self.vcopy(ubs, ubs[:, :, 15:16], usT, usT[:, :, si:si + 1])
            self.pool_mix(False, 1, ubs, oc=si)
            self.dma("pool", mkt, mkt[:], I["cmk"], I["cmk"][l, si].rearrange("(i p) c -> p i c", p=128), key="mkt")
            self.dma("pool", mvs, mvs[:], I["cmv"], I["cmv"][l, si].rearrange("(i p) c -> p i c", p=128), key="mvs")
            pmv = P[2][:].bitcast(BF16)
            for i in range(2):
                for cc in range(2):
                    self.tr(P[2], pmv[:, (i * 2 + cc) * 128:(i * 2 + cc + 1) * 128], mkt, mkt[:, i, cc * 128:(cc + 1) * 128], c["identb"], c["identb"][:])
            for cc in range(2):
                for i in range(2):
                    self.acopy(mks, mks[:, cc, i * 128:(i + 1) * 128], P[2], pmv[:, (i * 2 + cc) * 128:(i * 2 + cc + 1) * 128])
            self.mem_attend(1, mks, mvs, qc=si)
        if self.with_sample_dsa:
            self.dsa_sample(l)
        self.out_proj(xsT, xsT[:], NS)
        self.vcopy(xsT, xsT[:], T["x1"], T["x1"][:, :, 0:NS])

    def getbuf(self, alias, dt, shape, name):
        n = int(np.prod(shape))
        nbytes = 0
        if alias is not None:
            ap = alias.t[:]
            nd = len(ap.shape)
            if nd == 3:
                ap = ap.rearrange("p a b -> p (a b)")
            elif nd == 4:
                ap = ap.rearrange("p a b c -> p (a b c)")
            nbytes = ap.shape[1] * mybir.dt.size(ap.dtype)
        if nbytes >= n * mybir.dt.size(dt):
            ap = ap.bitcast(dt)[:, 0:n]
            buf = alias
        else:
            buf = self.s.sb(name, [128, n], dt)
            ap = buf.t[:]
        if len(shape) == 2:
            ap = ap.rearrange("p (a b) -> p a b", a=shape[0])
        elif len(shape) == 3:
            ap = ap.rearrange("p (a b c) -> p a b c", a=shape[0], b=shape[1])
        return buf, ap

    def dsa_sample(self, l):
        s, I, O, c, T, P, lay, smp = self.s, self.I, self.O, self.c, self.T, self.P, self.lay, self.smp
        NS, NPG, NPOOL = self.NS, self.NPG, self.NPOOL
        KT = float(self.KTOP_S) - 0.5
        ident, identb, ones, onesb = c["ident"], c["identb"], c["ones"], c["onesb"]
        cs = s.sb("c_smp", [128, 167], F32)
        self.dma("sp", cs, cs[:], I["c_smp"], I["c_smp"][:, :])
        fold = cs[:, 0:128]
        bmask = cs[:, 128:152].rearrange("p (g h) -> p g h", g=3)
        rmask = cs[:, 152:156]
        dbase = cs[:, 156:157]
        nsel = cs[0:8, 157:161]
        even, odd = cs[0:8, 161:162], cs[0:8, 162:163]
        pairsel = cs[0:8, 163:167]
        thr = s.sb("thr", [128, 31], F32)
        self.dma("sp", thr, thr[:], I["c_thr"], I["c_thr"][:, :])
        rbb = s.sb("rb_bc", [128, 32, 8], F32)
        self.dma("sp", rbb, rbb[:].rearrange("p b h -> p (b h)"), I["rel_bias"], bass.AP(I["rel_bias"].t.tensor, 0, [[0, 128], [1, 256]]))
        drel = s.sb("drel", [128, 8, 31], F32)
        rb0 = s.sb("rb0", [128, 8], F32)
        for pos in range(8):
            hq = QPERM[pos]
            self.tt(drel, drel[:, pos, :], rbb, rbb[:, 1:32, hq], rbb, rbb[:, 0:31, hq], OP.subtract)
            self.vcopy(rb0, rb0[:, pos:pos + 1], rbb, rbb[:, 0, hq:hq + 1])
        sel = s.sb("sel", [NS, NS, 128], F32)
        for si in range(NS):
            self.vcopy(sel, sel[0:NS, si, :], ident, ident[0:NS, si:si + 1].to_broadcast([NS, 128]))
        ptT = s.sb("ptT", [128, NS], I32)
        self.memset(ptT, ptT[:], 0, eng="dve")
        self.s.dma("sp", lambda e: e.dma_start(out=ptT[0:NPG, :], in_=bass.AP(I["ptab"].t.tensor, 0, [[1, NPG], [NPG, NS]]),
                                               allow_slow_non_contiguous=True), R=[I["ptab"]], W=[ptT], key="ptT")
        gidx = s.sb("gidx", [128, NS], I32)
        self.ts(gidx, gidx[:], ptT, ptT[:], float(l * NPOOL), None, OP.add)
        pb128 = s.sb("pb128", [128, NS], F32)
        self.ts(pb128, pb128[:], gidx, gidx[:], 128.0, None, OP.mult)
        Scs = s.sb("Scs", [128, NS, 129], F32)
        self.memset(Scs, Scs[:], NEG, eng="dve")
        PGb, PG = self.getbuf(None, F32, [4096], "PGd")
        IKb, IKT = self.getbuf(None, BF16, [32, NPG], "IKTd")
        Rb, R = self.getbuf(T["x1"], F32, [128, 8], "Rd")
        cikp = I["cik"][:, :].rearrange("(g t) d -> g (t d)", t=128)
        rhs3 = s.sb("rhs3", [128, 3, 8], F32)
        IQm = s.sb("IQm", [128, 4, 8], BF16)
        iwb = s.sb("iwb", [128, 8], F32)
        t8 = s.sb("t8", [1, 8], F32)
        iqT, iw16, ikn = T["iqT"], T["iw16"], smp["ikn"]
        for si in range(NS):
            self.s.dma("pool", lambda e, si=si: e.indirect_dma_start(
                out=PG[0:NPG, :], out_offset=None, in_=cikp,
                in_offset=bass.IndirectOffsetOnAxis(ap=gidx[0:NPG, si:si + 1], axis=0)),
                R=[I["cik"], gidx], W=[PGb], key="PG")
            for g0 in range(0, 32, 4):
                Pb = P[(g0 // 4) % 2]
                for g in range(g0, g0 + 4):
                    self.tr(Pb, Pb[:, (g - g0) * NPG:(g - g0 + 1) * NPG], PGb, PG[0:NPG, g * 128:(g + 1) * 128], ident, ident[0:NPG, 0:NPG])
                self.acopy(IKb, IKT[:, g0:g0 + 4, :], Pb, Pb[:, 0:4 * NPG].rearrange("p (g n) -> p g n", g=4))
            for g in range(3):
                self.ts(rhs3, rhs3[:, g, :], cs, bmask[:, g, :], iqT[:, g, si:si + 1], None, OP.mult, R=[iqT])
            for g in range(3):
                self.mm(P[2], P[2][:, 0:8], cs, fold, rhs3, rhs3[:, g, :], g == 0, g == 2)
            for r in range(4):
                self.ts(IQm, IQm[:, r, :], P[2], P[2][:, 0:8], rmask[:, r:r + 1], None, OP.mult, R=[cs])
            for t in range(128):
                g, r = t // 4, t % 4
                Pb = P[3 + t // 64]
                col = (t % 64) * 8
                self.mm(Pb, Pb[0:NPG, col:col + 8], IKb, IKT[:, g, :], IQm, IQm[:, r, :], True, True)
            for hf in range(2):
                self.act(Rb, R[0:NPG, hf * 64:(hf + 1) * 64, :], P[3 + hf], P[3 + hf][0:NPG, 0:512].rearrange("p (t h) -> p t h", h=8), AF.Relu)
            self.mm(P[2], P[2][:, 8:16], sel, sel[0:NS, si, :], iw16, iw16[0:NS, 0:8], True, True)
            self.vcopy(iwb, iwb[:], P[2], P[2][:, 8:16])
            self.tt(Rb, R[0:NPG, :, :], Rb, R[0:NPG, :, :], iwb, iwb[0:NPG, :].unsqueeze(1).to_broadcast([NPG, 128, 8]), OP.mult)
            s.op("dve", lambda e, si=si: e.tensor_reduce(out=Scs[0:NPG, si, 0:128], in_=R[0:NPG, :, :], axis=AX.X, op=OP.add), R=[Rb], W=[Scs])
            self.mm(P[2], P[2][0:1, 16:24], ikn, ikn[:, si:si + 1], IQm, IQm[:, 0, :], True, True)
            self.act(t8, t8[:], P[2], P[2][0:1, 16:24], AF.Relu)
            self.tt(t8, t8[:], t8, t8[:], iwb, iwb[0:1, :], OP.mult)
            s.op("dve", lambda e, si=si: e.tensor_reduce(out=Scs[0:1, si, 128:129], in_=t8[:], axis=AX.X, op=OP.add), R=[t8], W=[Scs])
        mnp = s.sb("mnp", [128, NS], F32)
        mxp = s.sb("mxp", [128, NS], F32)
        s.op("dve", lambda e: e.tensor_reduce(out=mnp[0:NPG, :], in_=Scs[0:NPG, :, 0:128], axis=AX.X, op=OP.min), R=[Scs], W=[mnp])
        s.op("dve", lambda e: e.tensor_reduce(out=mxp[0:NPG, :], in_=Scs[0:NPG, :, 0:128], axis=AX.X, op=OP.max), R=[Scs], W=[mxp])
        self.tr(P[2], P[2][0:NS, 0:NPG], mnp, mnp[0:NPG, :], ident, ident[0:NPG, 0:NPG])
        self.tr(P[2], P[2][0:NS, 128:128 + NPG], mxp, mxp[0:NPG, :], ident, ident[0:NPG, 0:NPG])
        v3 = s.sb("v3", [NS, 4], F32)
        s.op("dve", lambda e: e.tensor_reduce(out=v3[:, 0:1], in_=P[2][0:NS, 0:NPG], axis=AX.X, op=OP.min), R=[P[2]], W=[v3])
        s.op("dve", lambda e: e.tensor_reduce(out=v3[:, 1:2], in_=P[2][0:NS, 128:128 + NPG], axis=AX.X, op=OP.max), R=[P[2]], W=[v3])
        self.tt(v3, v3[:, 2:3], v3, v3[:, 1:2], v3, v3[:, 0:1], OP.subtract)
        dg = s.sb("dg", [NS, 2, NS], F32)
        self.ts(dg, dg[:, 0, :], ident, ident[0:NS, 0:NS], v3[:, 0:1], None, OP.mult, R=[v3])
        self.ts(dg, dg[:, 1, :], ident, ident[0:NS, 0:NS], v3[:, 2:3], None, OP.mult, R=[v3])
        self.mm(P[2], P[2][:, 256:256 + 2 * NS], ones, ones[0:NS, :], dg, dg[:].rearrange("p a b -> p (a b)"), True, True)
        mr = s.sb("mr", [128, 2, NS], F32)
        self.vcopy(mr, mr[:].rearrange("p a b -> p (a b)"), P[2], P[2][:, 256:256 + 2 * NS])
        ddS = s.sb("ddS", [128, NS, NIT + 1], F32)
        self.tt(ddS, ddS[:], c["pow2"], c["pow2"][:].unsqueeze(1).to_broadcast([128, NS, NIT + 1]),
                mr, mr[:, 1, :].unsqueeze(2).to_broadcast([128, NS, NIT + 1]), OP.mult)
        mid = s.sb("mid", [128, NS], F32)
        self.tt(mid, mid[:], mr, mr[:, 0, :], ddS, ddS[:, :, 1], OP.add)
        jb, junk = self.getbuf(None, F32, [NS, 129], "junkd")
        cntp = s.sb("cntp", [128, NS], F32)
        sgn = s.sb("sgn", [128, NS], F32)
        for i in range(NIT):
            self.tt(jb, junk, Scs, Scs[:], mid, mid[:].unsqueeze(2).to_broadcast([128, NS, 129]), OP.is_ge)
            s.op("dve", lambda e: e.tensor_reduce(out=cntp[:], in_=junk, axis=AX.X, op=OP.add), R=[jb], W=[cntp])
            self.mm(P[2], P[2][:, 320:320 + NS], ones, ones[:], cntp, cntp[:], True, True)
            self.ts(sgn, sgn[:], P[2], P[2][:, 320:320 + NS], KT, -0.5, OP.is_ge, OP.add)
            self.tt(sgn, sgn[:], sgn, sgn[:], ddS, ddS[:, :, i + 1], OP.mult)
            self.tt(mid, mid[:], mid, mid[:], sgn, sgn[:], OP.add)
        lob = s.sb("lob", [128, NS], F32)
        self.tt(lob, lob[:], mid, mid[:], ddS, ddS[:, :, NIT], OP.subtract)
        Kcb, Kc = self.getbuf(None, F32, [16, 256], "Kcd")
        Vcb, Vc = self.getbuf(None, F32, [16, 256], "Vcd")
        tmb, tmp = self.getbuf(None, F32, [16, 64], "tmpd")
        geb, ge = self.getbuf(T["sq"], F32, [16, 31], "ged")
        tbb, tb = self.getbuf(T["rs"], F32, [16, 31], "tbd")
        qbb, q_bc = self.getbuf(T["rden"], F32, [512], "qbd")
        Wk = s.sb("Wk", [128, 129], F32)
        Wk2 = s.sb("Wk2", [128, 129], F32)
        m8 = s.sb("m8", [128, 16], F32)
        i8 = s.sb("i8", [128, 16], U32)
        cf = s.sb("cf", [128, 6, 16], F32)
        rowf = s.sb("rowf", [128, 16], F32)
        rowi = s.sb("rowi", [128, 16], I32)
        vnew = s.sb("vnew", [128, 1], F32)
        dgq = s.sb("dgq", [128, 128], BF16)
        dgk = s.sb("dgk", [128, 128], F32)
        knb = s.sb("knb", [128, 256], F32)
        vnb = s.sb("vnb", [128, 256], F32)
        lgr = s.sb("lgr", [128, 8, 16], F32)
        lg = s.sb("lg", [128, 8, 16], F32)
        lgn = s.sb("lgn", [128, 8], F32)
        tmn = s.sb("tmn", [128, 8, 64], F32)
        bp = s.sb("bp", [128, 16], F32)
        o4 = s.sb("o4", [8, 4, 64], F32)
        osel = s.sb("osel", [8, 64], F32)
        rd = s.sb("rd", [8, 1], F32)
        A2 = s.sb("A2", [8, 128], F32)
        ckv, cvv = I["ck"], I["cv"]
        qT, kn, tm, mixT = T["qT"], T["kn32"], T["tm"], T["mixT"]
        for si in range(NS):
            lo_s = lob[:, si:si + 1]
            self.vcopy(Wk, Wk[:], Scs, Scs[:, si, :])
            s.op("dve", lambda e: e.max(out=m8[:, 0:8], in_=Wk[:]), R=[Wk], W=[m8])
            s.op("dve", lambda e: e.max_index(out=i8[:, 0:8], in_max=m8[:, 0:8], in_values=Wk[:]), R=[Wk, m8], W=[i8])
            s.op("dve", lambda e: e.match_replace(out=Wk2[:], in_to_replace=m8[:, 0:8], in_values=Wk[:], imm_value=NEG), R=[Wk, m8], W=[Wk2])
            s.op("dve", lambda e: e.max(out=m8[:, 8:16], in_=Wk2[:]), R=[Wk2], W=[m8])
            s.op("dve", lambda e: e.max_index(out=i8[:, 8:16], in_max=m8[:, 8:16], in_values=Wk2[:]), R=[Wk2, m8], W=[i8])
            self.vcopy(cf, cf[:, 0, :], i8, i8[:])
            self.ts(cf, cf[:, 1, :], m8, m8[:], lo_s, None, OP.is_ge, R=[lob])
            self.ts(cf, cf[:, 2, :], cf, cf[:, 0, :], 127.5, None, OP.is_le)
            self.tt(cf, cf[:, 3, :], cf, cf[:, 1, :], cf, cf[:, 2, :], OP.mult)
            self.ts(vnew, vnew[:], Scs, Scs[:, si, 128:129], lo_s, None, OP.is_ge, R=[lob])
            self.ts(cf, cf[:, 4, :], cf, cf[:, 0, :], 127.0, None, OP.min)
            self.ts(rowf, rowf[:], cf, cf[:, 4, :], pb128[:, si:si + 1], None, OP.add, R=[pb128])
            self.vcopy(rowi, rowi[:], rowf, rowf[:])
            self.ts(cf, cf[:, 5, :], cf, cf[:, 4, :], -1.0, dbase, OP.mult, OP.add, R=[cs])
            for i in range(16):
                self.s.dma("pool", lambda e, i=i: e.indirect_dma_start(
                    out=Kc[:, i, :], out_offset=None, in_=ckv[:, :],
                    in_offset=bass.IndirectOffsetOnAxis(ap=rowi[:, i:i + 1], axis=0)), R=[ckv, rowi], W=[Kcb], key="Kc")
                self.s.dma("pool", lambda e, i=i: e.indirect_dma_start(
                    out=Vc[:, i, :], out_offset=None, in_=cvv[:, :],
                    in_offset=bass.IndirectOffsetOnAxis(ap=rowi[:, i:i + 1], axis=0)), R=[cvv, rowi], W=[Vcb], key="Vc")
            for ch in range(4):
                self.ts(dgq, dgq[:], identb, identb[:], qT[:, ch, si:si + 1], None, OP.mult, R=[qT])
                self.mm(P[3], P[3][:, ch * 128:(ch + 1) * 128], onesb, onesb[:], dgq, dgq[:], True, True)
            self.vcopy(qbb, q_bc, P[3], P[3][:, 0:512])
            for cc in range(2):
                self.ts(dgk, dgk[:], ident, ident[:], kn[:, cc, si:si + 1], None, OP.mult, R=[kn])
                self.mm(P[4], P[4][:, cc * 128:(cc + 1) * 128], ones, ones[:], dgk, dgk[:], True, True)
            self.mm(P[4], P[4][:, 256:512], sel, sel[0:NS, si, :], tm, tm[0:NS, 0:256], True, True)
            self.vcopy(knb, knb[:], P[4], P[4][:, 0:256])
            self.vcopy(vnb, vnb[:], P[4], P[4][:, 256:512])
            for pos in range(8):
                n_ = QPERM[pos] // 2
                self.tt(tmb, tmp, Kcb, Kc[:, :, n_ * 64:(n_ + 1) * 64], qbb,
                        q_bc[:, pos * 64:(pos + 1) * 64].unsqueeze(1).to_broadcast([128, 16, 64]), OP.mult)
                s.op("dve", lambda e, pos=pos: e.tensor_reduce(out=lgr[:, pos, :], in_=tmp, axis=AX.X, op=OP.add), R=[tmb], W=[lgr])
            for A in range(2):
                q4 = q_bc[:, A * 256:(A + 1) * 256].rearrange("p (b c d) -> p b c d", b=2, c=2)
                k2 = knb[:, A * 128:(A + 1) * 128].rearrange("p (c d) -> p c d", c=2).unsqueeze(1).to_broadcast([128, 2, 2, 64])
                self.tt(tmn, tmn[:, A * 4:(A + 1) * 4, :].rearrange("p (b c) d -> p b c d", b=2), qbb, q4, knb, k2, OP.mult)
            s.op("dve", lambda e: e.tensor_reduce(out=lgn[:], in_=tmn[:], axis=AX.X, op=OP.add), R=[tmn], W=[lgn])
            self.tt(geb, ge, cf, cf[:, 5, :].unsqueeze(2).to_broadcast([128, 16, 31]), thr, thr[:].unsqueeze(1).to_broadcast([128, 16, 31]), OP.is_ge)
            for pos in range(8):
                self.tt(tbb, tb, geb, ge, drel, drel[:, pos, :].unsqueeze(1).to_broadcast([128, 16, 31]), OP.mult)
                s.op("dve", lambda e: e.tensor_reduce(out=bp[:], in_=tb, axis=AX.X, op=OP.add), R=[tbb], W=[bp])
                self.stt(lg, lg[:, pos, :], bp, bp[:], rb0[:, pos:pos + 1], lgr, lgr[:, pos, :], OP.add, OP.add, R=[rb0])
            self.tt(lgn, lgn[:], lgn, lgn[:], rb0, rb0[:], OP.add)
            self.act(lg, lg[:], lg, lg[:], AF.Exp, bias=c["nshift"][:, 0:1], scale=1.0, R=[c["nshift"]])
            self.tt(lg, lg[:], lg, lg[:], cf, cf[:, 3, :].unsqueeze(1).to_broadcast([128, 8, 16]), OP.mult)
            self.act(lgn, lgn[:], lgn, lgn[:], AF.Exp, bias=c["nshift"][:, 0:1], scale=1.0, R=[c["nshift"]])
            self.ts(lgn, lgn[:], lgn, lgn[:], vnew[:, 0:1], None, OP.mult, R=[vnew])
            for i in range(16):
                self.mm(P[5], P[5][0:8, 0:256], lg, lg[:, :, i], Vcb, Vc[:, i, :], i == 0, False)
                self.mm(P[6], P[6][0:8, 0:1], lg, lg[:, :, i], ones, ones[:, 0:1], i == 0, False)
            self.mm(P[5], P[5][0:8, 0:256], lgn, lgn[:], vnb, vnb[:], False, True)
            self.mm(P[6], P[6][0:8, 0:1], lgn, lgn[:], ones, ones[:, 0:1], False, True)
            self.tt(o4, o4[:], P[5], P[5][0:8, 0:256].rearrange("p (n d) -> p n d", n=4), cs, nsel.unsqueeze(2).to_broadcast([8, 4, 64]), OP.mult)
            s.op("dve", lambda e: e.tensor_reduce(out=osel[:], in_=o4[:].rearrange("p n d -> p d n"), axis=AX.X, op=OP.add), R=[o4], W=[osel])
            s.op("dve", lambda e: e.reciprocal(out=rd[:], in_=P[6][0:8, 0:1]), R=[P[6]], W=[rd])
            self.ts(A2, A2[:, 0:64], osel, osel[:], rd[:, 0:1], even, OP.mult, OP.mult, R=[rd, cs])
            self.ts(A2, A2[:, 64:128], osel, osel[:], rd[:, 0:1], odd, OP.mult, OP.mult, R=[rd, cs])
            self.mm(P[7], P[7][:, 0:4], A2, A2[:], cs, pairsel, True, True)
            self.vcopy(mixT, mixT[:, 0:4, si], P[7], P[7][:, 0:4])

    def phase2(self, l):
        s, I, O, c = self.s, self.I, self.O, self.c
        S, NS = self.S, self.NS
        NF = DFF // 128
        WN = 256
        wu = s.sb("wu", [128, 8, 2 * DFF], BF16)
        wd = s.sb("wd", [128, NF, D], BF16)
        for k in range(8):
            for c0 in range(0, 2 * DFF, 2048):
                n = min(2048, 2 * DFF - c0)
                self.dma("pool", wu, wu[:, k, c0:c0 + n], I["w_up"], I["w_up"][l, k * 128:(k + 1) * 128, c0:c0 + n], key=None)
        for f in range(NF):
            self.dma("pool", wd, wd[:, f, :], I["w_down"], I["w_down"][l, f * 128:(f + 1) * 128, :], key=None)
        g2 = self.load_gvec("g2", I["norm2_g"], l)
        cw = s.sb("cw", [128, 44, 3], F32)
        cb = s.sb("cb", [128, 44], F32)
        for j in range(3):
            ap = bass.AP(I["conv_w"].t.tensor, (l * 3 + j) * 2 * DFF, [[1, 128], [128, 44]])
            self.s.dma("sp", lambda e, ap=ap, j=j: e.dma_start(out=cw[:, :, j], in_=ap, allow_slow_non_contiguous=True), R=[I["conv_w"]], W=[cw], key="cw%d" % j)
        apb = bass.AP(I["conv_b"].t.tensor, l * 2 * DFF, [[1, 128], [128, 44]])
        self.s.dma("sp", lambda e: e.dma_start(out=cb[:], in_=apb, allow_slow_non_contiguous=True), R=[I["conv_b"]], W=[cb], key="cb")
        P = [s.ps("Q%d" % i, [128, 512], F32) for i in range(8)]
        xw2 = [s.sb("xw%d" % i, [128, 8, WN], F32) for i in range(2)]
        sq = s.sb("sq8b", [128, 8, WN], F32)
        rs = s.sb("rsb", [128, WN], F32)
        h2 = s.sb("h2", [128, 8, WN], BF16)
        aT = s.sb("aT", [128, NF, WN], BF16)
        t1 = Ring([s.sb("t1_%d" % i, [128, WN], F32) for i in range(2)])
        t2 = Ring([s.sb("t2_%d" % i, [128, WN], F32) for i in range(2)])
        cg = Ring([s.sb("cg_%d" % i, [128, WN], F32) for i in range(2)])
        cv = Ring([s.sb("cv_%d" % i, [128, WN], F32) for i in range(2)])
        sg = Ring([s.sb("sg_%d" % i, [128, WN], F32) for i in range(2)])
        xo = Ring([s.sb("xo_%d" % i, [128, WN], F32) for i in range(2)])
        ytok = s.sb("ytok", [128, D], F32)
        ctk = Ring([s.sb("ctk_%d" % i, [NS, 512], F32) for i in range(2)])
        ident = c["ident"]

        def up_pair(f, n, Pg, Pv):
            for (Pb, ff) in ((Pg, f), (Pv, NF + f)):
                for k in range(8):
                    self.mm(Pb, Pb[:, 0:n], wu, wu[:, k, ff * 128:(ff + 1) * 128], h2, h2[:, k, 0:n], k == 0, k == 7)

        def norm2(xb, xap3, n):
            self.rms_feature_major(xb, xap3, n, g2, h2, P[6], sq, rs)

        def up_rows_out(col0, ncols, dst, dst_ap_fn):
            for ci, c0 in enumerate(range(0, 2 * DFF, 512)):
                Pb = P[4 + ci % 2]
                for k in range(8):
                    self.mm(Pb, Pb[0:ncols, 0:512], h2, h2[:, k, col0:col0 + ncols], wu, wu[:, k, c0:c0 + 512], k == 0, k == 7)
                ct = ctk.next()
                self.vcopy(ct, ct[0:ncols, :], Pb, Pb[0:ncols, 0:512])
                self.dma("sp", dst, dst_ap_fn(c0), ct, ct[0:ncols, :], key=ct.name)

        step = WN - 2
        starts = list(range(0, S, step))
        def load_norm(wi):
            xw = xw2[wi % 2]
            st0 = starts[wi]
            nnew = min(step, S - st0)
            n = nnew + 2
            if st0 == 0:
                self.memset(xw, xw[:, :, 0:2], 0.0, eng="dve")
                self.dma("sp", xw, xw[:, :, 2:n], self.XA, self.XA[:, 0:nnew].rearrange("(k p) t -> p k t", p=128))
            else:
                self.dma("sp", xw, xw[:, :, 0:n], self.XA, self.XA[:, st0 - 2:st0 + nnew].rearrange("(k p) t -> p k t", p=128))
            norm2(xw, xw[:, :, 0:n], n)

        load_norm(0)
        for wi, st0 in enumerate(starts):
            xw = xw2[wi % 2]
            nnew = min(step, S - st0)
            n = nnew + 2
            if wi == len(starts) - 1:
                up_rows_out(n - 2, 2, O["conv_p"], lambda c0: O["conv_p"][l, :, c0:c0 + 512])
            for f in range(NF):
                Pg, Pv = P[(f % 2) * 2], P[(f % 2) * 2 + 1]
                up_pair(f, n, Pg, Pv)
                outs = []
                for (Pb, ff, ring) in ((Pg, f, cg), (Pv, NF + f, cv)):
                    a1, a2, cc_ = t1.next(), t2.next(), ring.next()
                    self.act(a1, a1[:, 0:nnew], Pb, Pb[:, 2:n], AF.Identity, bias=cb[:, ff:ff + 1], scale=cw[:, ff, 2:3], R=[cb, cw])
                    self.stt(a2, a2[:, 0:nnew], Pb, Pb[:, 1:n - 1], cw[:, ff, 1:2], a1, a1[:, 0:nnew], OP.mult, OP.add, R=[cw])
                    self.stt(cc_, cc_[:, 0:nnew], Pb, Pb[:, 0:n - 2], cw[:, ff, 0:1], a2, a2[:, 0:nnew], OP.mult, OP.add, R=[cw])
                    outs.append(cc_)
                sgt = sg.next()
                self.act(sgt, sgt[:, 0:nnew], outs[0], outs[0][:, 0:nnew], AF.Silu)
                self.tt(aT, aT[:, f, 0:nnew], sgt, sgt[:, 0:nnew], outs[1], outs[1][:, 0:nnew], OP.mult)
            if wi + 1 < len(starts):
                load_norm(wi + 1)
            for dm in range(8):
                Pb = P[4 + dm % 2]
                for f in range(NF):
                    self.mm(Pb, Pb[:, 0:nnew], wd, wd[:, f, dm * 128:(dm + 1) * 128], aT, aT[:, f, 0:nnew], f == 0, f == NF - 1)
                if l == 0:
                    xt = xo.next()
                    self.tt(xt, xt[:, 0:nnew], Pb, Pb[:, 0:nnew], xw, xw[:, dm, 2:n], OP.add)
                    self.dma("sp", self.XB, self.XB[dm * 128:(dm + 1) * 128, st0:st0 + nnew], xt, xt[:, 0:nnew], key=xt.name)
                else:
                    self.tt(xw, xw[:, dm, 2:n], Pb, Pb[:, 0:nnew], xw, xw[:, dm, 2:n], OP.add)
            if l == 1:
                for b0 in range(0, nnew, 128):
                    nb = min(128, nnew - b0)
                    for k0 in (0, 4):
                        Pb = P[6 + (k0 // 4)]
                        for k in range(k0, k0 + 4):
                            self.tr(Pb, Pb[0:nb, (k - k0) * 128:(k - k0 + 1) * 128], xw, xw[:, k, 2 + b0:2 + b0 + nb], ident, ident[:])
                        self.acopy(ytok, ytok[0:nb, k0 * 128:(k0 + 4) * 128], Pb, Pb[0:nb, 0:512])
                    self.dma("sp", O["y_p"], O["y_p"][st0 + b0:st0 + b0 + nb, :], ytok, ytok[0:nb, :], key="ytok")

        xsT = c["xsT"]
        norm2(xsT, xsT[:], NS)
        up_rows_out(0, NS, O["conv_s"], lambda c0: O["conv_s"][l, :, 1, c0:c0 + 512])
        sT = [s.sb("sT%d" % i, [128, 44, NS], F32) for i in range(2)]
        for i in range(2):
            for ci, c0 in enumerate(range(0, 2 * DFF, 512)):
                ct = ctk.next()
                self.dma("sp", ct, ct[0:NS, :], I["sconv"], I["sconv"][l, :, i, c0:c0 + 512], key=ct.name)
                if i == 1:
                    self.dma("sp", O["conv_s"], O["conv_s"][l, :, 0, c0:c0 + 512], ct, ct[0:NS, :], key=ct.name + "o")
                Pb = P[6 + ci % 2]
                for j in range(4):
                    self.tr(Pb, Pb[:, j * NS:(j + 1) * NS], ct, ct[0:NS, j * 128:(j + 1) * 128], ident, ident[0:NS, 0:NS])
                f0 = c0 // 128
                self.vcopy(sT[i], sT[i][:, f0:f0 + 4, :], Pb, Pb[:, 0:4 * NS].rearrange("p (f s) -> p f s", s=NS))
        for f in range(NF):
            Pg, Pv = P[(f % 2) * 2], P[(f % 2) * 2 + 1]
            up_pair(f, NS, Pg, Pv)
            outs = []
            for (Pb, ff, ring) in ((Pg, f, cg), (Pv, NF + f, cv)):
                a1, a2, cc_ = t1.next(), t2.next(), ring.next()
                self.act(a1, a1[:, 0:NS], Pb, Pb[:, 0:NS], AF.Identity, bias=cb[:, ff:ff + 1], scale=cw[:, ff, 2:3], R=[cb, cw])
                self.stt(a2, a2[:, 0:NS], sT[1], sT[1][:, ff, :], cw[:, ff, 1:2], a1, a1[:, 0:NS], OP.mult, OP.add, R=[cw])
                self.stt(cc_, cc_[:, 0:NS], sT[0], sT[0][:, ff, :], cw[:, ff, 0:1], a2, a2[:, 0:NS], OP.mult, OP.add, R=[cw])
                outs.append(cc_)
            sgt = sg.next()
            self.act(sgt, sgt[:, 0:NS], outs[0], outs[0][:, 0:NS], AF.Silu)
            self.tt(aT, aT[:, f, 0:NS], sgt, sgt[:, 0:NS], outs[1], outs[1][:, 0:NS], OP.mult)
        for dm in range(8):
            Pb = P[4 + dm % 2]
            for f in range(NF):
                self.mm(Pb, Pb[:, 0:NS], wd, wd[:, f, dm * 128:(dm + 1) * 128], aT, aT[:, f, 0:NS], f == 0, f == NF - 1)
            self.tt(xsT, xsT[:, dm, :], Pb, Pb[:, 0:NS], xsT, xsT[:, dm, :], OP.add)
        if l == 1:
            for k0 in (0, 4):
                for k in range(k0, k0 + 4):
                    self.tr(P[6], P[6][0:NS, (k - k0) * 128:(k - k0 + 1) * 128], xsT, xsT[:, k, :], ident, ident[:])
                self.acopy(ytok, ytok[0:NS, k0 * 128:(k0 + 4) * 128], P[6], P[6][0:NS, 0:512])
            self.dma("sp", O["y_s"], O["y_s"][:, :], ytok, ytok[0:NS, :], key="ytok")


def make_consts(S, NPG=128):
    m = np.arange(TABLEN)
    d = m - 127
    oh = np.zeros((32, TABLEN), np.float32)
    b = t5_bucket_np(np.maximum(d, 0))
    valid = d >= 0
    oh[b[valid], m[valid]] = 1.0
    w = np.array([2, 4, 8, 16], np.float32)
    cinv = np.zeros((128, 2, 128), np.float32)
    winv = np.zeros((128, 2), np.float32)
    pos = np.arange(128, dtype=np.float32)
    for g in range(4):
        cc, hh = g // 2, g % 2
        cinv[hh * 64:(hh + 1) * 64, cc, :] = 1.0 / np.minimum(pos + 1.0, w[g])[None, :]
        winv[hh * 64:(hh + 1) * 64, cc] = 1.0 / w[g]
    pow2 = np.tile((2.0 ** -np.arange(NIT + 1, dtype=np.float64)).astype(np.float32)[None, :], (128, 1))
    dd = np.arange(0, 4096)
    bb = t5_bucket_np(dd)
    thr = np.array([dd[bb >= k].min() for k in range(1, 32)], np.float32)
    thr = np.tile(thr[None, :], (128, 1))
    iota = np.tile(np.arange(128, dtype=np.float32)[None, :], (128, 1))
    smp = np.zeros((128, 167), np.float32)
    k = np.arange(128)
    smp[:, 0:128] = (k[:, None] % 32 == k[None, :] % 32)
    for g in range(3):
        for h in range(8):
            smp[:, 128 + g * 8 + h] = (k < 96) & (h // 3 == g) & (k // 32 == h % 3)
    for r in range(4):
        smp[:, 152 + r] = (k // 32 == r)
    smp[:, 156] = np.where(k < NPG, (NPG - k) * 128, 0)
    for pos in range(8):
        for n in range(4):
            smp[pos, 157 + n] = float(n == QPERM[pos] // 2)
        smp[pos, 161] = float(pos % 2 == 0)
        smp[pos, 162] = float(pos % 2 == 1)
        for ch in range(4):
            smp[pos, 163 + ch] = float(pos // 2 == ch)
    return dict(c_oh=oh, c_cinv=cinv, c_winv=winv, c_pow2=pow2, c_thr=thr, c_iota=iota, c_smp=smp)


def core_inputs(inp, core, NS, consts):
    b = core % 4
    f = lambda a: np.ascontiguousarray(a)
    sl = slice(core * NS, (core + 1) * NS)
    m = {}
    m["xp"] = f(inp["x_prompt"][b])
    m["xs"] = f(inp["x_sample"][sl, 0])
    m["memp"] = f(inp["mem_prompt"][b])
    ck = inp["cache_k"]
    m["ck"] = ck.reshape(ck.shape[0] * ck.shape[1] * ck.shape[2], 256)
    cv = inp["cache_v"]
    m["cv"] = cv.reshape(cv.shape[0] * cv.shape[1] * cv.shape[2], 256)
    ci = inp["cache_idx_k"]
    m["cik"] = ci.reshape(ci.shape[0] * ci.shape[1] * ci.shape[2], 32)
    m["cmk"] = f(inp["cache_mem_k"][:, sl].reshape(2, NS, 256, 256))
    m["cmv"] = f(inp["cache_mem_v"][:, sl].reshape(2, NS, 256, 256))
    m["spool"] = f(inp["state_pool"][:, sl])
    m["sconv"] = f(inp["state_conv"][:, sl])
    m["ptab"] = f(inp["page_table"][sl].astype(np.int32))
    for k in ("rel_bias", "norm1_g", "w_in", "q_norm_g", "k_norm_g", "mem_norm_g", "w_mem_kv", "mq_norm_g", "mk_norm_g",
              "w_pool", "pool_scale", "w_out", "norm2_g", "w_up", "conv_w", "conv_b", "w_down"):
        m[k] = f(inp[k])
    m.update(consts)
    return m


_CACHE = {}


def run(inp, n_cores=8, NS=4, with_sample_dsa=True):
    inp = {k: np.asarray(v) for k, v in inp.items()}
    S = inp["x_prompt"].shape[1]
    NPG = inp["page_table"].shape[1]
    NPOOL = inp["cache_k"].shape[1]
    key = (S, NPG, NPOOL, NS, with_sample_dsa)
    if key not in _CACHE:
        _CACHE[key] = K(S, NPG, NPOOL, NS, with_sample_dsa).build()
    nc = _CACHE[key]
    consts = make_consts(S, NPG)
    maps = [core_inputs(inp, c, NS, consts) for c in range(n_cores)]
    res = run_bass_kernel_spmd(nc, maps, core_ids=list(range(n_cores))).results
    B = inp["x_prompt"].shape[0]
    nb = min(B, n_cores)
    DB = n_cores * NS
    st = lambda name, shp: np.stack([res[c][name] for c in range(nb)], axis=0)
    y_p = st("y_p", None)
    y_s = np.concatenate([res[c]["y_s"] for c in range(n_cores)], axis=0)[:, None, :]
    k_p = st("k_p", None).transpose(1, 0, 2, 3).reshape(2, nb, S, 4, 64)
    v_p = st("v_p", None).transpose(1, 0, 2, 3).reshape(2, nb, S, 4, 64)
    ik_p = st("ik_p", None).transpose(1, 0, 2, 3)
    pool_p = st("pool_p", None).transpose(1, 0, 2, 3)
    conv_p = st("conv_p", None).transpose(1, 0, 2, 3)
    mk_p = st("mk_p", None).transpose(1, 0, 2, 3).reshape(2, nb, 256, 4, 64)
    mv_p = st("mv_p", None).transpose(1, 0, 2, 3).reshape(2, nb, 256, 4, 64)
    cat = lambda name: np.concatenate([res[c][name] for c in range(n_cores)], axis=1)
    k_s = cat("k_s").reshape(2, DB, 1, 4, 64)
    v_s = cat("v_s").reshape(2, DB, 1, 4, 64)
    ik_s = cat("ik_s").reshape(2, DB, 1, 32)
    pool_s = cat("pool_s")
    conv_s = cat("conv_s")
    outs = (y_p, y_s, k_p, v_p, ik_p, pool_p, conv_p, mk_p, mv_p, k_s, v_s, ik_s, pool_s, conv_s)
    return tuple(np.ascontiguousarray(o.astype(np.float32)) for o in outs)


def kernel(**inputs):
    return run(inputs, n_cores=8, NS=4)
```

```python
import math
import numpy as np
from contextlib import ExitStack
import concourse.bass as bass
import concourse.mybir as mybir
from concourse.bass_utils import run_bass_kernel_spmd

F32 = mybir.dt.float32
BF16 = mybir.dt.bfloat16
I32 = mybir.dt.int32
U32 = mybir.dt.uint32
AF = mybir.ActivationFunctionType
OP = mybir.AluOpType
AX = mybir.AxisListType

D = 1024
DFF = 2816
NIN = 1832
EPS = 1e-6
SHIFT = 8.0
NEG = -1.0e30
NIT = 16
ENGS = ["pe", "act", "dve", "pool", "sp"]
QPERM = [0, 2, 1, 3, 4, 6, 5, 7]
HCLAMP = 1664
HLEN = HCLAMP + 128
TABLEN = HLEN + 128
IXENG = "dve"
MKENG = "pool"


class Buf:
    def __init__(self, name, t=None):
        self.name = name
        self.t = t
        self.w = {}
        self.rs = {}

    def __getitem__(self, k):
        return self.t[k]


class Ring:
    def __init__(self, bufs):
        self.bufs = bufs
        self.i = 0

    def next(self):
        b = self.bufs[self.i % len(self.bufs)]
        self.i += 1
        return b


class Sched:
    def __init__(self, nc, es):
        self.nc = nc
        self.es_global = es
        self.es = es
        self.q = {e: [] for e in ENGS}
        self.cnt = {}
        self.sems = {}
        self.seen = {e: {} for e in ENGS}
        self.uid = 0
        for e in ENGS:
            self._sem("E_" + e)

    def _sem(self, key):
        if key not in self.sems:
            self.sems[key] = self.es_global.enter_context(self.nc.semaphore("s_" + key))
            self.cnt[key] = 0
        return self.sems[key]

    def sb(self, name, shape, dt=F32):
        self.uid += 1
        t = self.es.enter_context(self.nc.sbuf_tensor("%s_%d" % (name, self.uid), list(shape), dt))
        return Buf(name, t)

    def ps(self, name, shape, dt=F32):
        self.uid += 1
        t = self.es.enter_context(self.nc.psum_tensor("%s_%d" % (name, self.uid), list(shape), dt))
        return Buf(name, t)

    def _deps(self, eng, R, W):
        deps = {}

        def add(d):
            for k, v in d.items():
                if deps.get(k, 0) < v:
                    deps[k] = v
        for b in R:
            add(b.w)
        for b in W:
            add(b.w)
            add(b.rs)
        out = []
        for k, v in deps.items():
            if eng == "pe" and k == "E_pe":
                continue
            if self.seen[eng].get(k, 0) >= v:
                continue
            self.seen[eng][k] = v
            out.append((k, v))
        return out

    def _commit(self, tok, R, W):
        k, v = tok
        for b in R:
            if b.rs.get(k, 0) < v:
                b.rs[k] = v
        for b in W:
            if b.w.get(k, 0) < v:
                b.w[k] = v
            b.rs = {}

    def op(self, eng, fn, R=(), W=()):
        waits = self._deps(eng, R, W)
        key = "E_" + eng
        self.cnt[key] += 1
        tok = (key, self.cnt[key])
        self.q[eng].append((waits, fn, key, 1))
        self._commit(tok, R, W)

    def dma(self, eng, fn, R, W, key):
        key = "D_" + key
        self._sem(key)
        waits = self._deps(eng, R, W)
        self.cnt[key] += 16
        tok = (key, self.cnt[key])
        self.q[eng].append((waits, fn, key, 16))
        self._commit(tok, R, W)

    def barrier(self):
        for e in ENGS:
            waits = []
            for k, v in self.cnt.items():
                if v == 0:
                    continue
                if self.seen[e].get(k, 0) >= v:
                    continue
                self.seen[e][k] = v
                waits.append((k, v))
            self.q[e].append((waits, None, None, 0))

    def emit(self):
        nc = self.nc
        qs = self.q
        self.q = {e: [] for e in ENGS}
        sems = self.sems
        with nc.Block() as block:
            def run(e, items):
                for waits, fn, key, inc in items:
                    for k, v in waits:
                        e.wait_ge(sems[k], v)
                    if fn is not None:
                        ins = fn(e)
                        ins.then_inc(sems[key], inc)

            @block.tensor
            def _(e):
                run(e, qs["pe"])

            @block.scalar
            def _(e):
                run(e, qs["act"])

            @block.vector
            def _(e):
                run(e, qs["dve"])

            @block.gpsimd
            def _(e):
                run(e, qs["pool"])

            @block.sync
            def _(e):
                run(e, qs["sp"])


def t5_bucket_np(d):
    d = np.asarray(d, dtype=np.int64)
    dd = np.maximum(d, 1).astype(np.float32)
    large = 16 + (np.log(dd / np.float32(16)) / np.float32(math.log(2048 / 16)) * np.float32(16)).astype(np.int32)
    large = np.minimum(large, 31)
    return np.where(d < 16, d, large)


class K:
    def __init__(self, S, NPG, NPOOL, NS=4, with_sample_dsa=True):
        self.S, self.NPG, self.NPOOL, self.NS = S, NPG, NPOOL, NS
        self.NT = S // 128
        self.KTOP = min(256, S // 4)
        self.L_s = NPG * 128 + 1
        self.KTOP_S = min(256, self.L_s // 4)
        self.with_sample_dsa = with_sample_dsa
        import os
        self.dbg = int(os.environ.get('KDBG', '99'))

    def mm(self, ob, oap, lb, lap, rb, rap, st, sp):
        self.s.op("pe", lambda e: e.matmul(oap, lhsT=lap, rhs=rap, start=st, stop=sp), R=[lb, rb], W=[ob])

    def tr(self, ob, oap, ib, iap, idb, idap):
        self.s.op("pe", lambda e: e.transpose(out=oap, in_=iap, identity=idap), R=[ib, idb], W=[ob])

    def act(self, ob, oap, ib, iap, func, bias=None, scale=1.0, R=(), accum=None, W=()):
        def fn(e):
            kw = {}
            if bias is not None:
                kw["bias"] = bias
            if accum is not None:
                kw["accum_out"] = accum
            return e.activation(out=oap, in_=iap, func=func, scale=scale, **kw)
        self.s.op("act", fn, R=[ib] + list(R), W=[ob] + list(W))

    def acopy(self, ob, oap, ib, iap):
        self.s.op("act", lambda e: e.copy(out=oap, in_=iap), R=[ib], W=[ob])

    def vcopy(self, ob, oap, ib, iap, eng="dve"):
        self.s.op(eng, lambda e: e.tensor_copy(out=oap, in_=iap), R=[ib], W=[ob])

    def tt(self, ob, oap, ab, aap, bb, bap, op, eng="dve"):
        self.s.op(eng, lambda e: e.tensor_tensor(out=oap, in0=aap, in1=bap, op=op), R=[ab, bb], W=[ob])

    def ts(self, ob, oap, ib, iap, s1, s2, op0, op1=None, R=(), accum=None, W=(), eng="dve"):
        def fn(e):
            kw = {}
            if op1 is not None:
                kw["op1"] = op1
            if accum is not None:
                kw["accum_out"] = accum
            return e.tensor_scalar(out=oap, in0=iap, scalar1=s1, scalar2=s2, op0=op0, **kw)
        self.s.op(eng, fn, R=[ib] + list(R), W=[ob] + list(W))

    def stt(self, ob, oap, ab, aap, sc, bb, bap, op0, op1, R=(), eng="dve"):
        self.s.op(eng, lambda e: e.scalar_tensor_tensor(out=oap, in0=aap, scalar=sc, in1=bap, op0=op0, op1=op1),
                  R=[ab, bb] + list(R), W=[ob])

    def memset(self, b, ap, val, eng="pool"):
        self.s.op(eng, lambda e: e.memset(ap, val), R=[], W=[b])

    def dma(self, q, ob, oap, ib, iap, key=None, R=(), W=()):
        self.s.dma(q, lambda e: e.dma_start(out=oap, in_=iap), R=[ib] + list(R), W=[ob] + list(W), key=key or ob.name)

    def build(self):
        S, NT, NS, NPG = self.S, self.NT, self.NS, self.NPG
        nc = bass.Bass("TRN2", target_bir_lowering=False)
        self.nc = nc
        NROWS = self.NPOOL * 128

        def din(name, shape, dt=F32):
            return Buf(name, nc.dram_tensor(name, list(shape), dt, kind="ExternalInput").ap())

        def dout(name, shape, dt=F32):
            return Buf(name, nc.dram_tensor(name, list(shape), dt, kind="ExternalOutput").ap())

        def dscr(name, shape, dt=F32):
            return Buf(name, nc.dram_tensor(name, list(shape), dt, kind="Internal").ap())

        I = {}
        I["xp"] = din("xp", [S, D])
        I["xs"] = din("xs", [NS, D])
        I["memp"] = din("memp", [256, D])
        I["ck"] = din("ck", [2 * NROWS, 256])
        I["cv"] = din("cv", [2 * NROWS, 256])
        I["cik"] = din("cik", [2 * NROWS, 32])
        I["cmk"] = din("cmk", [2, NS, 256, 256])
        I["cmv"] = din("cmv", [2, NS, 256, 256])
        I["spool"] = din("spool", [2, NS, 15, 256])
        I["sconv"] = din("sconv", [2, NS, 2, 2 * DFF])
        I["ptab"] = din("ptab", [NS, NPG], I32)
        I["rel_bias"] = din("rel_bias", [32, 8])
        I["norm1_g"] = din("norm1_g", [2, D])
        I["w_in"] = din("w_in", [2, D, NIN])
        I["q_norm_g"] = din("q_norm_g", [2, 64])
        I["k_norm_g"] = din("k_norm_g", [2, 64])
        I["mem_norm_g"] = din("mem_norm_g", [2, D])
        I["w_mem_kv"] = din("w_mem_kv", [2, D, 512])
        I["mq_norm_g"] = din("mq_norm_g", [2, 64])
        I["mk_norm_g"] = din("mk_norm_g", [2, 64])
        I["w_pool"] = din("w_pool", [2, 4, 64, 64])
        I["pool_scale"] = din("pool_scale", [2, 256])
        I["w_out"] = din("w_out", [2, D, D])
        I["norm2_g"] = din("norm2_g", [2, D])
        I["w_up"] = din("w_up", [2, D, 2 * DFF])
        I["conv_w"] = din("conv_w", [2, 3, 2 * DFF])
        I["conv_b"] = din("conv_b", [2, 2 * DFF])
        I["w_down"] = din("w_down", [2, DFF, D])
        I["c_oh"] = din("c_oh", [32, TABLEN])
        I["c_cinv"] = din("c_cinv", [128, 2, 128])
        I["c_winv"] = din("c_winv", [128, 2])
        I["c_pow2"] = din("c_pow2", [128, NIT + 1])
        I["c_thr"] = din("c_thr", [128, 31])
        I["c_iota"] = din("c_iota", [128, 128])
        I["c_smp"] = din("c_smp", [128, 167])
        self.I = I
        O = {}
        O["y_p"] = dout("y_p", [S, D])
        O["y_s"] = dout("y_s", [NS, D])
        O["k_p"] = dout("k_p", [2, S, 256])
        O["v_p"] = dout("v_p", [2, S, 256])
        O["ik_p"] = dout("ik_p", [2, S, 32])
        O["pool_p"] = dout("pool_p", [2, 15, 256])
        O["conv_p"] = dout("conv_p", [2, 2, 2 * DFF])
        O["mk_p"] = dout("mk_p", [2, 256, 256])
        O["mv_p"] = dout("mv_p", [2, 256, 256])
        O["k_s"] = dout("k_s", [2, NS, 256])
        O["v_s"] = dout("v_s", [2, NS, 256])
        O["ik_s"] = dout("ik_s", [2, NS, 32])
        O["pool_s"] = dout("pool_s", [2, NS, 15, 256])
        O["conv_s"] = dout("conv_s", [2, NS, 2, 2 * DFF])
        self.O = O
        self.XA = dscr("XA", [D, S])
        self.XB = dscr("XB", [D, S])
        self.TAB = dscr("TAB", [8, TABLEN])

        with ExitStack() as esg:
            s = Sched(nc, esg)
            self.s = s
            self.setup_consts()
            for l in range(2):
                if self.dbg <= 1:
                    break
                with ExitStack() as es1:
                    s.es = es1
                    self.phase1(l)
                    s.barrier()
                    s.emit()
                if self.dbg <= 8:
                    break
                with ExitStack() as es2:
                    s.es = es2
                    self.phase2(l)
                    s.barrier()
                    s.emit()
                s.es = esg
        return nc

    def setup_consts(self):
        s, I = self.s, self.I
        c = {}
        self.c = c
        c["ident"] = s.sb("ident", [128, 128], F32)
        c["identb"] = s.sb("identb", [128, 128], BF16)
        c["J"] = s.sb("J", [128, 128], BF16)
        c["ones"] = s.sb("ones", [128, 128], F32)
        c["onesb"] = s.sb("onesb", [128, 128], BF16)
        c["bones"] = s.sb("bones", [128, 128], F32)
        c["cm"] = s.sb("cm", [128, 128], F32)
        c["eps"] = s.sb("eps", [128, 1], F32)
        c["nshift"] = s.sb("nshift", [128, 1], F32)
        c["pow2"] = s.sb("pow2", [128, NIT + 1], F32)
        c["cinv"] = s.sb("cinv", [128, 2, 128], F32)
        c["winv"] = s.sb("winv", [128, 2], F32)
        c["xsT"] = s.sb("xsT", [128, 8, self.NS], F32)
        est = ExitStack()
        s.es = est
        ident, J = c["ident"], c["J"]
        self.memset(ident, ident[:], 0.0)
        s.op("pool", lambda e: e.affine_select(out=ident[:], in_=ident[:], pattern=[[-1, 128]], compare_op=OP.not_equal,
                                               fill=1.0, base=0, channel_multiplier=1), R=[ident], W=[ident])
        self.vcopy(c["identb"], c["identb"][:], ident, ident[:], eng="pool")
        jf = s.sb("jf", [128, 128], F32)
        self.memset(jf, jf[:], 0.0)
        s.op("pool", lambda e: e.affine_select(out=jf[:], in_=jf[:], pattern=[[1, 128]], compare_op=OP.not_equal,
                                               fill=1.0, base=-127, channel_multiplier=1), R=[jf], W=[jf])
        self.vcopy(J, J[:], jf, jf[:], eng="pool")
        self.memset(c["ones"], c["ones"][:], 1.0)
        self.memset(c["onesb"], c["onesb"][:], 1.0)
        bo = c["bones"]
        self.memset(bo, bo[:], 0.0)
        self.memset(bo, bo[0:64, 0:64], 1.0)
        self.memset(bo, bo[64:128, 64:128], 1.0)
        cm = c["cm"]
        self.memset(cm, cm[:], 0.0)
        s.op("pool", lambda e: e.affine_select(out=cm[:], in_=cm[:], pattern=[[-1, 128]], compare_op=OP.is_ge,
                                               fill=NEG, base=0, channel_multiplier=1), R=[cm], W=[cm])
        self.memset(c["eps"], c["eps"][:], EPS)
        self.memset(c["nshift"], c["nshift"][:], -SHIFT)
        self.dma("sp", c["pow2"], c["pow2"][:], I["c_pow2"], I["c_pow2"][:, :])
        self.dma("sp", c["cinv"], c["cinv"][:], I["c_cinv"], I["c_cinv"][:, :, :])
        self.dma("sp", c["winv"], c["winv"][:], I["c_winv"], I["c_winv"][:, :])
        rb = s.sb("rb", [32, 8], F32)
        oh = s.sb("oh", [32, TABLEN], F32)
        tabs = s.sb("tabs", [8, TABLEN], F32)
        self.dma("sp", rb, rb[:], I["rel_bias"], I["rel_bias"][:, :])
        self.dma("sp", oh, oh[:], I["c_oh"], I["c_oh"][:, :])
        pt = s.ps("ptab", [128, 512], F32)
        for c0 in range(0, TABLEN, 512):
            n = min(512, TABLEN - c0)
            self.mm(pt, pt[0:8, 0:n], rb, rb[:], oh, oh[:, c0:c0 + n], True, True)
            self.vcopy(tabs, tabs[:, c0:c0 + n], pt, pt[0:8, 0:n])
        self.dma("sp", self.TAB, self.TAB[:, :], tabs, tabs[:], key="tabst")
        xs_tok = s.sb("xs_tok", [self.NS, D], F32)
        self.dma("sp", xs_tok, xs_tok[:], I["xs"], I["xs"][:, :])
        for k in range(8):
            self.tr(pt, pt[:, k * 4:k * 4 + self.NS], xs_tok, xs_tok[:, k * 128:(k + 1) * 128], ident, ident[0:self.NS, 0:self.NS])
        self.vcopy(c["xsT"], c["xsT"][:].rearrange("p k s -> p (k s)"), pt, pt[:, 0:8 * self.NS])
        s.barrier()
        s.emit()
        est.close()
        s.es = s.es_global

    def rms_feature_major(self, xb, xap3, n, gvec, hT, P_N, tmp_sq, tmp_rs):
        c = self.c
        sq, rs = tmp_sq, tmp_rs
        self.act(sq, sq[:, 0:8, 0:n], xb, xap3, AF.Square)
        for k in range(8):
            self.mm(P_N, P_N[:, 0:n], c["ones"], c["ones"][:], sq, sq[:, k, 0:n], k == 0, k == 7)
        self.act(rs, rs[:, 0:n], P_N, P_N[:, 0:n], AF.Sqrt, bias=c["eps"][:, 0:1], scale=1.0 / D, R=[c["eps"]])
        self.s.op("dve", lambda e: e.reciprocal(out=rs[:, 0:n], in_=rs[:, 0:n]), R=[rs], W=[rs])
        for k in range(8):
            self.stt(hT, hT[:, k, 0:n], xb, xap3[:, k, :], gvec[:, k:k + 1], rs, rs[:, 0:n], OP.mult, OP.mult, R=[gvec])

    def headnorm(self, P_Z, zap, ncols, P_N, sq, rs):
        c = self.c
        self.act(sq, sq[:, 0:ncols], P_Z, zap, AF.Square)
        self.mm(P_N, P_N[:, 0:ncols], c["bones"], c["bones"][:], sq, sq[:, 0:ncols], True, True)
        self.act(rs, rs[:, 0:ncols], P_N, P_N[:, 0:ncols], AF.Sqrt, bias=c["eps"][:, 0:1], scale=1.0 / 64, R=[c["eps"]])
        self.s.op("dve", lambda e: e.reciprocal(out=rs[:, 0:ncols], in_=rs[:, 0:ncols]), R=[rs], W=[rs])
        return rs[:, 0:ncols]

    def load_gvec(self, name, src, l, scale=None):
        g = self.s.sb(name, [128, 8], F32)
        ap = bass.AP(src.t.tensor, l * D, [[1, 128], [128, 8]])
        self.s.dma("sp", lambda e: e.dma_start(out=g[:], in_=ap, allow_slow_non_contiguous=True), R=[src], W=[g], key=name)
        return g

    def load_hvec(self, name, src, l, scale=1.0):
        g = self.s.sb(name, [128, 1], F32)
        for h in range(2):
            ap = bass.AP(src.t.tensor, l * 64, [[1, 64], [1, 1]])
            self.s.dma("sp", lambda e, ap=ap, h=h: e.dma_start(out=g[h * 64:(h + 1) * 64, :], in_=ap), R=[src], W=[g], key=name + str(h))
        if scale != 1.0:
            self.ts(g, g[:], g, g[:], scale, None, OP.mult)
        return g

    def phase1(self, l):
        s, I, O, c = self.s, self.I, self.O, self.c
        S, NT, NS = self.S, self.NT, self.NS
        W = {}
        wsrc = I["w_in"]
        win3 = wsrc[l].rearrange("(k p) c -> p k c", p=128)
        wf = s.sb("wf", [128, 8, 14 * 128], BF16)
        W["wf"] = wf
        for pos, h in enumerate(QPERM):
            self.dma("pool", wf, wf[:, :, pos * 64:(pos + 1) * 64], wsrc, win3[:, :, h * 64:(h + 1) * 64], key=None)
        self.dma("pool", wf, wf[:, :, 512:768], wsrc, win3[:, :, 512:768], key=None)
        self.dma("pool", wf, wf[:, :, 768:1024], wsrc, win3[:, :, 1576:1832], key=None)
        self.memset(wf, wf[:, :, 1024:1408], 0.0, eng="dve")
        for h in range(8):
            d0 = 1024 + (h // 3) * 128 + (h % 3) * 32
            self.dma("pool", wf, wf[:, :, d0:d0 + 32], wsrc, win3[:, :, 1024 + h * 32:1024 + (h + 1) * 32], key=None)
        for r in range(4):
            self.dma("pool", wf, wf[:, :, 1408 + r * 32:1408 + (r + 1) * 32], wsrc, win3[:, :, 1280:1312], key=None)
        self.dma("pool", wf, wf[:, :, 1536:1792], wsrc, win3[:, :, 1320:1576], key=None)
        wt = s.sb("wt", [128, 8, 296], BF16)
        W["wt"] = wt
        self.dma("pool", wt, wt[:, :, 0:256], wsrc, win3[:, :, 768:1024], key=None)
        self.dma("pool", wt, wt[:, :, 256:296], wsrc, win3[:, :, 1280:1320], key=None)
        wo = s.sb("wo", [128, 8, D], BF16)
        wm3 = I["w_mem_kv"][l].rearrange("(k p) c -> p k c", p=128)
        self.dma("pool", wo, wo[:, :, 0:512], I["w_mem_kv"], wm3, key=None)
        g1 = self.load_gvec("g1", I["norm1_g"], l)
        gm = self.load_gvec("gm", I["mem_norm_g"], l)
        gq = self.load_hvec("gq", I["q_norm_g"], l, scale=0.125)
        gk = self.load_hvec("gk", I["k_norm_g"], l)
        gmq = self.load_hvec("gmq", I["mq_norm_g"], l, scale=0.125)
        gmk = self.load_hvec("gmk", I["mk_norm_g"], l)
        psc = s.sb("psc", [128, 2], F32)
        self.s.dma("sp", lambda e: e.dma_start(out=psc[:], in_=bass.AP(I["pool_scale"].t.tensor, l * 256, [[1, 128], [128, 2]]),
                                               allow_slow_non_contiguous=True), R=[I["pool_scale"]], W=[psc], key="psc")
        wpb = s.sb("wpb", [128, 2, 128], F32)
        self.memset(wpb, wpb[:], 0.0)
        for g in range(4):
            cc, hh = g // 2, g % 2
            self.dma("sp", wpb, wpb[hh * 64:(hh + 1) * 64, cc, hh * 64:(hh + 1) * 64], I["w_pool"], I["w_pool"][l, g, :, :], key="wpb%d" % g)
        P = [s.ps("P%d" % i, [128, 512], F32) for i in range(8)]
        self.P = P
        mkT = s.sb("mkT", [128, 2, 256], BF16)
        mvb = s.sb("mvb", [128, 2, 256], BF16)
        T = {}
        T["xT2"] = [s.sb("xT%d" % i, [128, 8, 128], F32) for i in range(2)]
        T["xT"] = T["xT2"][0]
        T["rs1"] = s.sb("rs1", [128, 128], F32)
        T["hT"] = s.sb("hT", [128, 8, 128], BF16)
        T["sq"] = s.sb("sq", [128, 512], F32)
        T["rs"] = s.sb("rs", [128, 512], F32)
        T["qT"] = s.sb("qT", [128, 4, 128], BF16)
        T["qTm2"] = [s.sb("qTm%d" % i, [128, 2, 4, 128], BF16) for i in range(2)]
        for q_ in T["qTm2"]:
            self.memset(q_, q_[:], 0.0)
        T["qTm"] = T["qTm2"][0]
        T["mqT"] = s.sb("mqT", [128, 2, 128], BF16)
        T["mqm2"] = [s.sb("mqm%d" % i, [128, 2, 2, 128], BF16) for i in range(2)]
        for mq_ in T["mqm2"]:
            self.memset(mq_, mq_[:], 0.0)
        T["mqm"] = T["mqm2"][0]
        T["iqT"] = s.sb("iqT", [128, 3, 128], BF16)
        T["kn32"] = s.sb("kn32", [128, 2, 128], F32)
        T["ktok"] = s.sb("ktok", [128, 256], F32)
        T["tm"] = s.sb("tm", [128, 296], F32)
        T["iw16"] = s.sb("iw16", [128, 8], F32)
        T["mixT"] = s.sb("mixT", [128, 8, 128], BF16)
        T["x1"] = s.sb("x1", [128, 8, 128], F32)
        T["r"] = Ring([s.sb("r%d" % i, [128, 512], F32) for i in range(2)])
        T["pb"] = Ring([s.sb("pb%d" % i, [128, 512], BF16) for i in range(2)])
        T["pm"] = Ring([s.sb("pm%d" % i, [128, 512], BF16) for i in range(2)])
        T["rden"] = s.sb("rden", [128, 512], F32)
        T["bs"] = s.sb("bs", [128, 8], F32)
        T["dd"] = s.sb("dd", [128, NIT + 1], F32)
        for nm_ in ("md", "cntb", "sa", "tot", "sg"):
            T[nm_] = s.sb(nm_, [128, 1], F32)
        T["pl"] = [s.sb("pl%d" % i, [128, 2, 143], F32) for i in range(2)]
        T["pooled"] = s.sb("pooled", [128, 2, 128], F32)
        self.T = T
        self.Wt = W
        esp = ExitStack()
        es_phase = s.es
        s.es = esp
        H = s.sb("H", [128, 8, HLEN], BF16)
        c["H"] = H
        for h in range(8):
            hsrc = bass.AP(self.TAB.t.tensor, h * TABLEN, [[1, 128], [1, HLEN]])
            self.dma("pool", H, H[:, h, :], self.TAB, hsrc, key="Hld")
        kT = s.sb("kT", [128, 2, S], BF16)
        vc = s.sb("vc", [128, NT, 256], BF16)
        ikT = s.sb("ikT", [128, S], BF16)
        Sc = s.sb("Sc", [128, S], F32)
        mb = s.sb("mb", [128, S], BF16)
        maskT = s.sb("maskT", [128, NT, 128], BF16)
        ubuf2 = [s.sb("ubuf%d" % i, [128, 2, 143], F32) for i in range(2)]
        for ub_ in ubuf2:
            self.memset(ub_, ub_[:], 0.0)
        mbA = Buf("mbA", mb.t)
        kTt = [Buf("kTt%d" % i, kT.t) for i in range(NT)]
        vct = [Buf("vct%d" % i, vc.t) for i in range(NT)]
        ikTt = [Buf("ikTt%d" % i, ikT.t) for i in range(NT)]
        T["xtok"] = s.sb("xtok", [128, D], F32)
        self.lay = dict(l=l, g1=g1, gq=gq, gk=gk, gmq=gmq, gmk=gmk, psc=psc, wpb=wpb, wo=wo, kT=kT, vc=vc, ikT=ikT, Sc=Sc,
                        mb=mb, maskT=maskT, mkT=mkT, mvb=mvb, ubuf2=ubuf2, kTt=kTt, vct=vct, ikTt=ikTt, mbA=mbA)

        self.mem_kv(l, gm, gmk)
        wo_src = I["w_out"]
        for j in range(4):
            for hh in range(2):
                h = QPERM[2 * j + hh]
                self.dma("pool", wo, wo[hh * 64:(hh + 1) * 64, j, :], wo_src, wo_src[l, h * 64:(h + 1) * 64, :], key=None)
        self.dma("pool", wo, wo[:, 4:8, :], wo_src, wo_src[l, 512:1024, :].rearrange("(k p) c -> p k c", p=128), key=None)

        for (_c, fn, _t) in self.prompt_A(l, 0):
            fn()
        for t in range(NT):
            UB = self.prompt_B(l, t)
            UA = self.prompt_A(l, t + 1) if t + 1 < NT else []
            self.merge_run(UB, UA)
        s.barrier()
        s.emit()
        esp.close()
        s.es = es_phase
        for kdead in ("kT", "vc", "ikT", "Sc", "mb", "maskT", "ubuf2", "kTt", "vct", "ikTt", "mbA"):
            self.lay.pop(kdead)
        T.pop("xtok")
        T["xT"], T["mqm"] = T["xT2"][0], T["mqm2"][0]
        if self.dbg <= 8:
            return
        self.sample_tile(l)

    def mem_kv(self, l, gm, gmk):
        s, I, O, c, T, P = self.s, self.I, self.O, self.c, self.T, self.P
        lay = self.lay
        wo, mkT, mvb = lay["wo"], lay["mkT"], lay["mvb"]
        xtok, xT = T["xtok"], T["xT"]
        for i in range(2):
            self.dma("sp", xtok, xtok[:], I["memp"], I["memp"][i * 128:(i + 1) * 128, :])
            for k0 in (0, 4):
                for k in range(k0, k0 + 4):
                    self.tr(P[7], P[7][:, (k - k0) * 128:(k - k0 + 1) * 128], xtok, xtok[:, k * 128:(k + 1) * 128], c["ident"], c["ident"][:])
                self.acopy(xT, xT[:, k0:k0 + 4, :], P[7], P[7][:, 0:512].rearrange("p (k t) -> p k t", k=4))
            self.rms_feature_major(xT, xT[:], 128, gm, T["hT"], P[2], T["x1"], T["rs1"])
            hT = T["hT"]
            for g in range(4):
                for k in range(8):
                    self.mm(P[0], P[0][:, g * 128:(g + 1) * 128], wo, wo[:, k, g * 128:(g + 1) * 128], hT, hT[:, k, :], k == 0, k == 7)
            rs = self.headnorm(P[0], P[0][:, 0:256], 256, P[2], T["sq"], T["rs"])
            kn = T["kn32"]
            self.stt(kn, kn[:].rearrange("p c t -> p (c t)"), P[0], P[0][:, 0:256], gmk[:, 0:1], T["rs"], rs, OP.mult, OP.mult, R=[gmk])
            self.acopy(mkT, mkT[:, :, i * 128:(i + 1) * 128], kn, kn[:])
            for cc in range(2):
                self.tr(P[7], P[7][:, cc * 128:(cc + 1) * 128], kn, kn[:, cc, :], c["ident"], c["ident"][:])
            self.vcopy(T["ktok"], T["ktok"][:], P[7], P[7][:, 0:256])
            self.dma("sp", O["mk_p"], O["mk_p"][l, i * 128:(i + 1) * 128, :], T["ktok"], T["ktok"][:], key="ktok")
            vn = T["x1"]
            self.acopy(vn, vn[:, 0:2, :].rearrange("p c t -> p (c t)"), P[0], P[0][:, 256:512])
            for cc in range(2):
                self.tr(P[7], P[7][:, 256 + cc * 128:256 + (cc + 1) * 128], vn, vn[:, cc, :], c["ident"], c["ident"][:])
            self.vcopy(T["tm"], T["tm"][:, 0:256], P[7], P[7][:, 256:512])
            self.dma("sp", O["mv_p"], O["mv_p"][l, i * 128:(i + 1) * 128, :], T["tm"], T["tm"][:, 0:256], key="tm")
            self.acopy(mvb, mvb[:, i, :], T["tm"], T["tm"][:, 0:256])

    def project(self, l, xb, xap3, n, P):
        for _ in self.project_gen(l, xb, xap3, n, P):
            pass

    def project_gen(self, l, xb, xap3, n, P):
        s, c, T, lay, Wt = self.s, self.c, self.T, self.lay, self.Wt
        wf, wt = Wt["wf"], Wt["wt"]
        hT = T["hT"]
        self.rms_feature_major(xb, xap3, n, lay["g1"], hT, P[2], T["x1"], T["rs1"])
        yield

        def fm(Pb, slot, grp):
            for k in range(8):
                self.mm(Pb, Pb[:, slot * 128:slot * 128 + n], wf, wf[:, k, grp * 128:(grp + 1) * 128], hT, hT[:, k, 0:n], k == 0, k == 7)
        for j in range(4):
            fm(P[0], j, j)
        for j in range(4):
            fm(P[1], j, 4 + j)
        yield
        qT, mqT, kn = T["qT"], T["mqT"], T["kn32"]
        if n == 128:
            rs = self.headnorm(P[0], P[0][:, 0:512], 512, P[2], T["sq"], T["rs"])
            qTm = T["qTm"]
            for hh in range(2):
                pr = slice(hh * 64, (hh + 1) * 64)
                self.stt(qTm, qTm[pr, hh, :, :].rearrange("p c t -> p (c t)"), P[0], P[0][pr, 0:512], lay["gq"][pr, 0:1], T["rs"], rs[pr, :],
                         OP.mult, OP.mult, R=[lay["gq"]])
            yield
            rs = self.headnorm(P[1], P[1][:, 0:512], 512, P[2], T["sq"], T["rs"])
            self.stt(kn, kn[:].rearrange("p c t -> p (c t)"), P[1], P[1][:, 0:256], lay["gk"][:, 0:1], T["rs"], rs[:, 0:256], OP.mult, OP.mult, R=[lay["gk"]])
            self.stt(mqT, mqT[:].rearrange("p c t -> p (c t)"), P[1], P[1][:, 256:512], lay["gmq"][:, 0:1], T["rs"], rs[:, 256:512], OP.mult, OP.mult, R=[lay["gmq"]])
        else:
            for (Pb, nslot) in ((P[0], 4), (P[1], 4)):
                pass
            sq, rsb = T["sq"], T["rs"]
            for Pb in (P[0], P[1]):
                for j in range(4):
                    self.act(sq, sq[:, j * 128:j * 128 + n], Pb, Pb[:, j * 128:j * 128 + n], AF.Square)
                    self.mm(P[2], P[2][:, j * 128:j * 128 + n], c["bones"], c["bones"][:], sq, sq[:, j * 128:j * 128 + n], True, True)
                for j in range(4):
                    js = slice(j * 128, j * 128 + n)
                    self.act(rsb, rsb[:, js], P[2], P[2][:, js], AF.Sqrt, bias=c["eps"][:, 0:1], scale=1.0 / 64, R=[c["eps"]])
                    self.s.op("dve", lambda e, js=js: e.reciprocal(out=rsb[:, js], in_=rsb[:, js]), R=[rsb], W=[rsb])
                if Pb is P[0]:
                    for j in range(4):
                        self.stt(qT, qT[:, j, 0:n], Pb, Pb[:, j * 128:j * 128 + n], lay["gq"][:, 0:1], rsb, rsb[:, j * 128:j * 128 + n], OP.mult, OP.mult, R=[lay["gq"]])
                else:
                    for j in range(2):
                        self.stt(kn, kn[:, j, 0:n], Pb, Pb[:, j * 128:j * 128 + n], lay["gk"][:, 0:1], rsb, rsb[:, j * 128:j * 128 + n], OP.mult, OP.mult, R=[lay["gk"]])
                    for j in range(2):
                        self.stt(mqT, mqT[:, j, 0:n], Pb, Pb[:, (2 + j) * 128:(2 + j) * 128 + n], lay["gmq"][:, 0:1], rsb, rsb[:, (2 + j) * 128:(2 + j) * 128 + n], OP.mult, OP.mult, R=[lay["gmq"]])
        mqm = T["mqm"]
        self.vcopy(mqm, mqm[0:64, 0, :, 0:n], mqT, mqT[0:64, :, 0:n], eng="pool")
        self.vcopy(mqm, mqm[64:128, 1, :, 0:n], mqT, mqT[64:128, :, 0:n], eng="pool")
        yield
        for j in range(4):
            fm(P[0], j, 8 + j)
        for j in range(2):
            fm(P[1], j, 12 + j)
        iqT = T["iqT"]
        for j in range(3):
            self.acopy(iqT, iqT[:, j, 0:n], P[0], P[0][:, j * 128:j * 128 + n])
        yield
        for k in range(8):
            self.mm(P[7], P[7][0:n, 0:296], hT, hT[:, k, 0:n], wt, wt[:, k, :], k == 0, k == 7)
        tm = T["tm"]
        self.vcopy(tm, tm[0:n, :], P[7], P[7][0:n, 0:296])
        self.ts(T["iw16"], T["iw16"][0:n, :], tm, tm[0:n, 288:296], 1.0 / 16.0, None, OP.mult)

    def merge_run(self, UB, UA):
        tb = sum(u[0] for u in UB) or 1.0
        ta = sum(u[0] for u in UA) or 1.0
        ia = ib = 0
        ca = cb = 0.0
        while ia < len(UA) or ib < len(UB):
            pick_a = ib >= len(UB) or (ia < len(UA) and ca / ta < cb / tb)
            if pick_a and UA[ia][2] == "mask":
                while any(u[2] == "att" for u in UB[ib:]):
                    cb += UB[ib][0]
                    UB[ib][1]()
                    ib += 1
            if pick_a:
                ca += UA[ia][0]
                UA[ia][1]()
                ia += 1
            else:
                cb += UB[ib][0]
                UB[ib][1]()
                ib += 1

    def prompt_A(self, l, t):
        s, I, O, c, T, P, lay = self.s, self.I, self.O, self.c, self.T, self.P, self.lay
        S, NT = self.S, self.NT
        U = []
        add = lambda cost, fn, tag=None: U.append((cost, fn, tag))
        p = t % 2
        cols = slice(t * 128, (t + 1) * 128)
        xT = T["xT2"][p]
        ub = lay["ubuf2"][p]
        kT, vc, ikT = lay["kT"], lay["vc"], lay["ikT"]
        kTt, vct, ikTt = lay["kTt"], lay["vct"], lay["ikTt"]
        Sc, mb, maskT = lay["Sc"], lay["mb"], lay["maskT"]
        mbA = lay["mbA"]
        md, cntb, sa, tot, sg = T["md"], T["cntb"], T["sa"], T["tot"], T["sg"]

        def load():
            if l == 0:
                xtok = T["xtok"]
                self.dma("sp", xtok, xtok[:], I["xp"], I["xp"][cols, :])
                for k0 in (0, 4):
                    for k in range(k0, k0 + 4):
                        self.tr(P[7], P[7][:, (k - k0) * 128:(k - k0 + 1) * 128], xtok, xtok[:, k * 128:(k + 1) * 128], c["ident"], c["ident"][:])
                    self.acopy(xT, xT[:, k0:k0 + 4, :], P[7], P[7][:, 0:512].rearrange("p (k t) -> p k t", k=4))
            else:
                self.dma("sp", xT, xT[:], self.XB, self.XB[:, cols].rearrange("(k p) t -> p k t", p=128))
        add(6.0, load)

        gbox = {}

        def proj0():
            T["qTm"], T["mqm"] = T["qTm2"][p], T["mqm2"][p]
            gbox["g"] = self.project_gen(l, xT, xT[:], 128, P)
            next(gbox["g"])

        def projn():
            next(gbox["g"], None)

        def projz():
            for _ in gbox["g"]:
                pass
        add(8.0, proj0)
        for _i in range(4):
            add(7.0, projn)
        add(6.0, projz)

        def caches():
            kn, tm = T["kn32"], T["tm"]
            s.op("act", lambda e: e.copy(out=kT[:, :, cols], in_=kn[:]), R=[kn], W=[kTt[t]])
            s.op("act", lambda e: e.copy(out=ikT[:, cols], in_=P[0][:, 384:512]), R=[P[0]], W=[ikTt[t]])
            s.op("act", lambda e: e.copy(out=vc[:, t, :], in_=tm[:, 0:256]), R=[tm], W=[vct[t]])
            for cc in range(2):
                self.tr(P[7], P[7][:, cc * 128:(cc + 1) * 128], kn, kn[:, cc, :], c["ident"], c["ident"][:])
            self.vcopy(T["ktok"], T["ktok"][:], P[7], P[7][:, 0:256])
            self.dma("sp", O["k_p"], O["k_p"][l, cols, :], T["ktok"], T["ktok"][:], key="ktok")
            self.dma("sp", O["v_p"], O["v_p"][l, cols, :], tm, tm[:, 0:256], key="tm")
            self.dma("sp", O["ik_p"], O["ik_p"][l, cols, :], tm, tm[:, 256:288], key="tm2")
            self.acopy(ub, ub[:, :, 15:143], P[1], P[1][:, 0:256].rearrange("p (c t) -> p c t", c=2))
            if t > 0:
                ubp = lay["ubuf2"][1 - p]
                self.vcopy(ub, ub[:, :, 0:15], ubp, ubp[:, :, 128:143])
            if t == NT - 1:
                for cc in range(2):
                    self.tr(P[7], P[7][:, 256 + cc * 128:256 + (cc + 1) * 128], ub, ub[:, cc, 15:143], c["ident"], c["ident"][:])
                self.vcopy(T["xtok"], T["xtok"][:, 0:256], P[7], P[7][:, 256:512])
                self.dma("sp", O["pool_p"], O["pool_p"][l, :, :], T["xtok"], T["xtok"][113:128, 0:256], key="xtok_o")
        add(8.0, caches)

        iqT, iw, bs, dd = T["iqT"], T["iw16"], T["bs"], T["dd"]
        Wd = (t + 1) * 128
        chunks = [(c0, min(512, Wd - c0)) for c0 in range(0, Wd, 512)]
        xi = 0
        for h in range(8):
            for (c0, n) in chunks:
                def ix(h=h, c0=c0, n=n, xi=xi):
                    pbs = (h % 3) * 32
                    Px = P[xi % 2]
                    tl = [ikTt[j] for j in range(c0 // 128, (c0 + n) // 128)]
                    s.op("pe", lambda e: e.matmul(Px[:, 0:n], lhsT=iqT[pbs:pbs + 32, h // 3, :], rhs=ikT[pbs:pbs + 32, c0:c0 + n],
                                                  start=True, stop=True), R=[iqT] + tl, W=[Px])
                    r = T["r"].next()
                    self.act(r, r[:, 0:n], Px, Px[:, 0:n], AF.Relu)
                    if h == 0:
                        self.ts(Sc, Sc[:, c0:c0 + n], r, r[:, 0:n], iw[:, 0:1], None, OP.mult, R=[iw], eng=IXENG)
                    else:
                        self.stt(Sc, Sc[:, c0:c0 + n], r, r[:, 0:n], iw[:, h:h + 1], Sc, Sc[:, c0:c0 + n], OP.mult, OP.add, R=[iw], eng=IXENG)
                add(0.15 + 0.8 * n / 512.0, ix)
                xi += 1

        def rng():
            s.op("dve", lambda e: e.tensor_reduce(out=bs[:, 0:1], in_=Sc[:, 0:Wd], axis=AX.X, op=OP.min), R=[Sc], W=[bs])
            s.op("dve", lambda e: e.tensor_reduce(out=bs[:, 1:2], in_=Sc[:, 0:Wd], axis=AX.X, op=OP.max), R=[Sc], W=[bs])
            self.tt(Sc, Sc[:, t * 128:Wd], Sc, Sc[:, t * 128:Wd], c["cm"], c["cm"][:], OP.add)
            self.tt(bs, bs[:, 2:3], bs, bs[:, 1:2], bs, bs[:, 0:1], OP.subtract)
            self.ts(dd, dd[:], c["pow2"], c["pow2"][:], bs[:, 2:3], None, OP.mult, R=[bs])
            self.tt(md, md[:], bs, bs[:, 0:1], dd, dd[:, 1:2], OP.add)
        add(3.0 + 2.8 * Wd / 1000.0, rng)
        kk = float(self.KTOP) - 0.5
        ca = Wd if Wd < 768 else (int(Wd * 0.36) // 64) * 64
        na = Wd - ca
        for i in range(NIT):
            def bis(i=i):
                self.ts(mb, mb[:, 0:ca], Sc, Sc[:, 0:ca], md[:, 0:1], None, OP.is_ge, OP.add, R=[md], accum=cntb[:, 0:1], W=[cntb])
                if na > 0:
                    self.act(mbA, mb[:, ca:Wd], Sc, Sc[:, ca:Wd], AF.Sign, bias=md[:, 0:1], scale=-1.0, R=[md], accum=sa[:, 0:1], W=[sa])
                    self.stt(tot, tot[:], sa, sa[:], -0.5, cntb, cntb[:], OP.mult, OP.add)
                    self.ts(sg, sg[:], tot, tot[:], kk - 0.5 * na, -0.5, OP.is_ge, OP.add)
                else:
                    self.ts(sg, sg[:], cntb, cntb[:], kk, -0.5, OP.is_ge, OP.add)
                self.stt(md, md[:], sg, sg[:], dd[:, i + 1:i + 2], md, md[:], OP.mult, OP.add, R=[dd])
            add(1.5 + (1.4 if na == 0 else 0.5) * Wd / 1000.0, bis)

        def fin():
            self.stt(bs, bs[:, 6:7], dd, dd[:, NIT:NIT + 1], -1.0, md, md[:], OP.mult, OP.add)
            self.ts(mb, mb[:, 0:Wd], Sc, Sc[:, 0:Wd], bs[:, 6:7], None, OP.is_ge, R=[bs], W=[mbA])
        add(1.0 + 1.4 * Wd / 1000.0, fin)
        Pm = P[2]
        for j0 in range(0, t + 1, 8):
            def mtr(j0=j0):
                nj = min(8, t + 1 - j0)
                pmv = Pm[:].bitcast(BF16)
                for j in range(j0, j0 + nj):
                    s.op("pe", lambda e, j=j: e.transpose(out=pmv[:, (j - j0) * 128:(j - j0 + 1) * 128], in_=mb[:, j * 128:(j + 1) * 128],
                                                          identity=c["identb"][:]), R=[mb, mbA, c["identb"]], W=[Pm])
                self.acopy(maskT, maskT[:, j0:j0 + nj, :], Pm, pmv[:, 0:nj * 128].rearrange("p (j q) -> p j q", j=nj))
            add(1.5, mtr, "mask")
        return U

    def prompt_B(self, l, t):
        s, I, O, c, T, P, lay = self.s, self.I, self.O, self.c, self.T, self.P, self.lay
        U = []
        add = lambda cost, fn, tag=None: U.append((cost, fn, tag))
        p = t % 2
        cols = slice(t * 128, (t + 1) * 128)
        xT, qTm, mqm, ub = T["xT2"][p], T["qTm2"][p], T["mqm2"][p], lay["ubuf2"][p]
        kT, vc, maskT = lay["kT"], lay["vc"], lay["maskT"]
        kTt, vct = lay["kTt"], lay["vct"]
        mixT, H, J = T["mixT"], c["H"], c["J"]
        PO, PD = P[5], P[6]
        items = [(half, j) for half in range(2) for j in range(t + 1)]

        def qk(i):
            half, j = items[i]
            Pst = P[3 + (i % 2)]
            kc = slice(j * 128, (j + 1) * 128)
            d0 = min((t - j) * 128, HCLAMP)
            for jj in range(2):
                ch = 2 * half + jj
                for hh in range(2):
                    hq = QPERM[2 * ch + hh]
                    n = hq // 2
                    col = (jj * 2 + hh) * 128
                    self.mm(Pst, Pst[:, col:col + 128], kTt[j], kT[:, n // 2, kc], qTm, qTm[:, hh, ch, :], True, False)
                    self.mm(Pst, Pst[:, col:col + 128], J, J[:], H, H[:, hq, d0:d0 + 128], False, True)

        pms = {}

        def em(i):
            half, j = items[i]
            Pst = P[3 + (i % 2)]
            pb = T["pb"].next()
            self.act(pb, pb[:], Pst, Pst[:], AF.Exp, bias=c["nshift"][:, 0:1], scale=1.0, R=[c["nshift"]])
            pm = T["pm"].next()
            pms[i] = pm
            self.tt(pm, pm[:].rearrange("p (h q) -> p h q", h=4), pb, pb[:].rearrange("p (h q) -> p h q", h=4),
                    maskT, maskT[:, j, :].unsqueeze(1).to_broadcast([128, 4, 128]), OP.mult, eng=MKENG)

        def pv(i):
            half, j = items[i]
            pm = pms.pop(i)
            for jj in range(2):
                ch = 2 * half + jj
                nb = 2 * (ch // 2) * 64
                for hh in range(2):
                    col = (jj * 2 + hh) * 128
                    first = (j == 0 and jj == 0 and hh == 0)
                    lastm = (j == t and jj == 1 and hh == 1)
                    self.mm(PO, PO[:, col:col + 128], vct[j], vc[:, j, nb:nb + 128], pm, pm[:, col:col + 128], first, lastm)
            self.mm(PD, PD[:], c["onesb"], c["onesb"][:], pm, pm[:], j == 0, j == t)
            if j == t:
                rden = T["rden"]
                s.op("dve", lambda e: e.reciprocal(out=rden[:], in_=PD[:]), R=[PD], W=[rden])
                for jj in range(2):
                    ch = 2 * half + jj
                    for hh in range(2):
                        col = (jj * 2 + hh) * 128
                        pr = slice(hh * 64, (hh + 1) * 64)
                        self.tt(mixT, mixT[pr, ch, :], PO, PO[pr, col:col + 128], rden, rden[pr, col:col + 128], OP.mult)

        def pro():
            qk(0)
            em(0)
        add(2.5, pro, "att")
        for i in range(len(items)):
            def au(i=i):
                if i + 1 < len(items):
                    qk(i + 1)
                    em(i + 1)
                pv(i)
            add(2.9, au, "att")
        add(10.0, lambda: self.pool_mix(t == 0, 128, ub, bank=3))
        gb = {}

        def ma0():
            gb["m"] = self.mem_attend_gen(128, mqm=mqm, pst=(3, 4))
            next(gb["m"])

        def ma1():
            next(gb["m"], None)

        def ma2():
            for _ in gb["m"]:
                pass
        add(4.0, ma0)
        add(7.0, ma1)
        add(4.0, ma2)

        def op0():
            gb["o"] = self.out_proj_gen(xT, xT[:], 128, banks=(3, 4), dst=xT)
            next(gb["o"])

        def op1():
            for _ in gb["o"]:
                pass
            self.dma("sp", self.XA, self.XA[:, cols].rearrange("(k p) t -> p k t", p=128), xT, xT[:], key="x1st")
        add(7.0, op0)
        add(7.0, op1)
        return U


    def out_proj(self, xb, xap3, n, banks=(7, 2), dst=None):
        for _ in self.out_proj_gen(xb, xap3, n, banks, dst):
            pass

    def out_proj_gen(self, xb, xap3, n, banks=(7, 2), dst=None):
        T, P, lay = self.T, self.P, self.lay
        wo, mixT, x1 = lay["wo"], T["mixT"], (dst or T["x1"])
        for half in range(2):
            Pb = P[banks[half]]
            for j in range(4):
                dm = half * 4 + j
                for k in range(8):
                    self.mm(Pb, Pb[:, j * 128:j * 128 + n], wo, wo[:, k, dm * 128:(dm + 1) * 128], mixT, mixT[:, k, 0:n], k == 0, k == 7)
            if n == 128:
                self.tt(x1, x1[:, half * 4:half * 4 + 4, :].rearrange("p c t -> p (c t)"), Pb, Pb[:, 0:512],
                        xb, xap3[:, half * 4:half * 4 + 4, :].rearrange("p c t -> p (c t)"), OP.add)
            else:
                for j in range(4):
                    dm = half * 4 + j
                    self.tt(x1, x1[:, dm, 0:n], Pb, Pb[:, j * 128:j * 128 + n], xb, xap3[:, dm, :], OP.add)
            yield

    def pool_mix(self, first, n, ubuf, oc=0, bank=7):
        s, c, T, P, lay = self.s, self.c, self.T, self.P, self.lay
        A, B = T["pl"]
        Wn = 15 + n
        self.tt(A, A[:, :, 1:Wn], ubuf, ubuf[:, :, 1:Wn], ubuf, ubuf[:, :, 0:Wn - 1], OP.add)
        self.tt(B, B[:, :, 3:Wn], A, A[:, :, 3:Wn], A, A[:, :, 1:Wn - 2], OP.add)
        pooled = T["pooled"]
        self.vcopy(pooled, pooled[0:64, 0, 0:n], A, A[0:64, 0, 15:Wn])
        self.vcopy(pooled, pooled[64:128, 0, 0:n], B, B[64:128, 0, 15:Wn])
        self.tt(A, A[:, 1, 7:Wn], B, B[:, 1, 7:Wn], B, B[:, 1, 3:Wn - 4], OP.add)
        self.vcopy(pooled, pooled[0:64, 1, 0:n], A, A[0:64, 1, 15:Wn])
        self.tt(B, B[64:128, 1, 15:Wn], A, A[64:128, 1, 15:Wn], A, A[64:128, 1, 7:Wn - 8], OP.add)
        self.vcopy(pooled, pooled[64:128, 1, 0:n], B, B[64:128, 1, 15:Wn])
        if first:
            self.tt(pooled, pooled[:, :, 0:n], pooled, pooled[:, :, 0:n], c["cinv"], c["cinv"][:, :, 0:n], OP.mult)
            self.tt(pooled, pooled[:, :, 0:n], pooled, pooled[:, :, 0:n], ubuf, ubuf[:, :, 15:Wn], OP.subtract)
        else:
            for cc in range(2):
                self.stt(pooled, pooled[:, cc, 0:n], pooled, pooled[:, cc, 0:n], c["winv"][:, cc:cc + 1], ubuf, ubuf[:, cc, 15:Wn],
                         OP.mult, OP.subtract, R=[c["winv"]])
        wpb, psc, mixT = lay["wpb"], lay["psc"], T["mixT"]
        for cc in range(2):
            self.mm(P[bank], P[bank][:, cc * 128:cc * 128 + n], wpb, wpb[:, cc, :], pooled, pooled[:, cc, 0:n], True, True)
            self.ts(mixT, mixT[:, 4 + cc, oc:oc + n], P[bank], P[bank][:, cc * 128:cc * 128 + n], psc[:, cc:cc + 1], None, OP.mult, R=[psc])

    def mem_attend(self, n, mkT=None, mvb=None, qc=0, mqm=None, pst=(0, 1)):
        for _ in self.mem_attend_gen(n, mkT, mvb, qc, mqm, pst):
            pass

    def mem_attend_gen(self, n, mkT=None, mvb=None, qc=0, mqm=None, pst=(0, 1)):
        s, c, T, P, lay = self.s, self.c, self.T, self.P, self.lay
        mkT = mkT or lay["mkT"]
        mvb = mvb or lay["mvb"]
        mqm, mixT = (mqm or T["mqm"]), T["mixT"]
        PO, PD = P[5], P[6]
        for h in range(4):
            for i in range(2):
                Pst = P[pst[i]]
                cc, hh = h // 2, h % 2
                self.mm(Pst, Pst[:, h * 128:h * 128 + n], mkT, mkT[:, cc, i * 128:(i + 1) * 128],
                        mqm, mqm[:, hh, cc, qc:qc + n], True, True)
        yield
        for i in range(2):
            Pst = P[pst[i]]
            pb = T["pb"].next()
            if n == 128:
                self.act(pb, pb[:], Pst, Pst[:], AF.Exp, bias=c["nshift"][:, 0:1], scale=1.0, R=[c["nshift"]])
            else:
                for h in range(4):
                    self.act(pb, pb[:, h * 128:h * 128 + n], Pst, Pst[:, h * 128:h * 128 + n], AF.Exp, bias=c["nshift"][:, 0:1], scale=1.0, R=[c["nshift"]])
            for h in range(4):
                cc = h // 2
                first = (i == 0 and h == 0)
                lastm = (i == 1 and h == 3)
                self.mm(PO, PO[:, h * 128:h * 128 + n], mvb, mvb[:, i, cc * 128:(cc + 1) * 128], pb, pb[:, h * 128:h * 128 + n], first, lastm)
                self.mm(PD, PD[:, h * 128:h * 128 + n], c["onesb"], c["onesb"][:], pb, pb[:, h * 128:h * 128 + n], first, lastm)
        yield
        rden = T["rden"]
        for h in range(4):
            cc, hh = h // 2, h % 2
            pr = slice(hh * 64, (hh + 1) * 64)
            cs = slice(h * 128, h * 128 + n)
            s.op("dve", lambda e, pr=pr, cs=cs: e.reciprocal(out=rden[pr, cs], in_=PD[pr, cs]), R=[PD], W=[rden])
            self.tt(mixT, mixT[pr, 6 + cc, qc:qc + n], PO, PO[pr, cs], rden, rden[pr, cs], OP.mult)

    def sample_tile(self, l):
        s, I, O, c, T, P, lay = self.s, self.I, self.O, self.c, self.T, self.P, self.lay
        NS = self.NS
        xsT = c["xsT"]
        hT, wf = T["hT"], self.Wt["wf"]
        self.project(l, xsT, xsT[:], NS, P)
        kn, tm = T["kn32"], T["tm"]
        usT = s.sb("usT", [128, 2, NS], F32)
        self.vcopy(usT, usT[:], P[1], P[1][:, 0:256].rearrange("p (c t) -> p c t", c=2)[:, :, 0:NS])
        ikn = s.sb("ikn", [128, NS], BF16)
        self.vcopy(ikn, ikn[:], P[0], P[0][:, 384:384 + NS])
        self.smp = dict(usT=usT, ikn=ikn)
        ks = s.sb("ks_tok", [NS, 256], F32)
        for cc in range(2):
            self.tr(P[3], P[3][0:NS, cc * 128:(cc + 1) * 128], kn, kn[:, cc, 0:NS], c["ident"], c["ident"][:])
        self.vcopy(ks, ks[:], P[3], P[3][0:NS, 0:256])
        self.smp["ks"] = ks
        self.dma("sp", O["k_s"], O["k_s"][l, :, :], ks, ks[:], key="ks")
        self.dma("sp", O["v_s"], O["v_s"][l, :, :], tm, tm[0:NS, 0:256], key="tm")
        self.dma("sp", O["ik_s"], O["ik_s"][l, :, :], tm, tm[0:NS, 256:288], key="tm2")
        for k in range(8):
            self.mm(P[4], P[4][0:NS, 0:256], hT, hT[:, k, 0:NS], wf, wf[:, k, 1536:1792], k == 0, k == 7)
        us = s.sb("us_tok", [NS, 256], F32)
        self.vcopy(us, us[:], P[4], P[4][0:NS, 0:256])
        mixT = T["mixT"]
        self.memset(mixT, mixT[:, 0:4, 0:NS], 0.0, eng="dve")
        st = s.sb("st_tok", [15, 256], F32)
        ubs = s.sb("ubs", [128, 2, 16], F32)
        mkt = s.sb("mkt", [128, 2, 256], BF16)
        mvs = s.sb("mvs", [128, 2, 256], BF16)
        mks = s.sb("mks", [128, 2, 256], BF16)
        for si in range(NS):
            self.dma("sp", st, st[:], I["spool"], I["spool"][l, si, :, :], key="st")
            self.dma("sp", O["pool_s"], O["pool_s"][l, si, 0:14, :], st, st[1:15, :], key="st_o")
            self.dma("sp", O["pool_s"], O["pool_s"][l, si, 14:15, :], us, us[si:si + 1, :], key="us_o")
            for cc in range(2):
                self.tr(P[7], P[7][:, cc * 16:cc * 16 + 15], st, st[:, cc * 128:(cc + 1) * 128], c["ident"], c["ident"][0:15, 0:15])
            self.vcopy(ubs, ubs[:, :, 0:15], P[7], P[7][:, 0:32].rearrange("p (c t) -> p c t", c=2)[:, :, 0:15])
            self.vcopy(ubs, ubs[:, :, 15:16], usT, usT[:, :, si:si + 1])
            self.pool_mix(False, 1, ubs, oc=si)
            self.dma("pool", mkt, mkt[:], I["cmk"], I["cmk"][l, si].rearrange("(i p) c -> p i c", p=128), key="mkt")
            self.dma("pool", mvs, mvs[:], I["cmv"], I["cmv"][l, si].rearrange("(i p) c -> p i c", p=128), key="mvs")
            pmv = P[2][:].bitcast(BF16)
            for i in range(2):
                for cc in range(2):
                    self.tr(P[2], pmv[:, (i * 2 + cc) * 128:(i * 2 + cc + 1) * 128], mkt, mkt[:, i, cc * 128:(cc + 1) * 128], c["identb"], c["identb"][:])
            for cc in range(2):
                for i in range(2):
                    self.acopy(mks, mks[:, cc, i * 128:(i + 1) * 128], P[2], pmv[:, (i * 2 + cc) * 128:(i * 2 + cc + 1) * 128])
            self.mem_attend(1, mks, mvs, qc=si)
        if self.with_sample_dsa:
            self.dsa_sample(l)
        self.out_proj(xsT, xsT[:], NS)
        self.vcopy(xsT, xsT[:], T["x1"], T["x1"][:, :, 0:NS])

    def getbuf(self, alias, dt, shape, name):
        n = int(np.prod(shape))
        nbytes = 0
        if alias is not None:
            ap = alias.t[:]
            nd = len(ap.shape)
            if nd == 3:
                ap = ap.rearrange("p a b -> p (a b)")
            elif nd == 4:
                ap = ap.rearrange("p a b c -> p (a b c)")
            nbytes = ap.shape[1] * mybir.dt.size(ap.dtype)
        if nbytes >= n * mybir.dt.size(dt):
            ap = ap.bitcast(dt)[:, 0:n]
            buf = alias
        else:
            buf = self.s.sb(name, [128, n], dt)
            ap = buf.t[:]
        if len(shape) == 2:
            ap = ap.rearrange("p (a b) -> p a b", a=shape[0])
        elif len(shape) == 3:
            ap = ap.rearrange("p (a b c) -> p a b c", a=shape[0], b=shape[1])
        return buf, ap

    def dsa_sample(self, l):
        s, I, O, c, T, P, lay, smp = self.s, self.I, self.O, self.c, self.T, self.P, self.lay, self.smp
        NS, NPG, NPOOL = self.NS, self.NPG, self.NPOOL
        KT = float(self.KTOP_S) - 0.5
        ident, identb, ones, onesb = c["ident"], c["identb"], c["ones"], c["onesb"]
        cs = s.sb("c_smp", [128, 167], F32)
        self.dma("sp", cs, cs[:], I["c_smp"], I["c_smp"][:, :])
        fold = cs[:, 0:128]
        bmask = cs[:, 128:152].rearrange("p (g h) -> p g h", g=3)
        rmask = cs[:, 152:156]
        dbase = cs[:, 156:157]
        nsel = cs[0:8, 157:161]
        even, odd = cs[0:8, 161:162], cs[0:8, 162:163]
        pairsel = cs[0:8, 163:167]
        thr = s.sb("thr", [128, 31], F32)
        self.dma("sp", thr, thr[:], I["c_thr"], I["c_thr"][:, :])
        rbb = s.sb("rb_bc", [128, 32, 8], F32)
        self.dma("sp", rbb, rbb[:].rearrange("p b h -> p (b h)"), I["rel_bias"], bass.AP(I["rel_bias"].t.tensor, 0, [[0, 128], [1, 256]]))
        drel = s.sb("drel", [128, 8, 31], F32)
        rb0 = s.sb("rb0", [128, 8], F32)
        for pos in range(8):
            hq = QPERM[pos]
            self.tt(drel, drel[:, pos, :], rbb, rbb[:, 1:32, hq], rbb, rbb[:, 0:31, hq], OP.subtract)
            self.vcopy(rb0, rb0[:, pos:pos + 1], rbb, rbb[:, 0, hq:hq + 1])
        sel = s.sb("sel", [NS, NS, 128], F32)
        for si in range(NS):
            self.vcopy(sel, sel[0:NS, si, :], ident, ident[0:NS, si:si + 1].to_broadcast([NS, 128]))
        ptT = s.sb("ptT", [128, NS], I32)
        self.memset(ptT, ptT[:], 0, eng="dve")
        self.s.dma("sp", lambda e: e.dma_start(out=ptT[0:NPG, :], in_=bass.AP(I["ptab"].t.tensor, 0, [[1, NPG], [NPG, NS]]),
                                               allow_slow_non_contiguous=True), R=[I["ptab"]], W=[ptT], key="ptT")
        gidx = s.sb("gidx", [128, NS], I32)
        self.ts(gidx, gidx[:], ptT, ptT[:], float(l * NPOOL), None, OP.add)
        pb128 = s.sb("pb128", [128, NS], F32)
        self.ts(pb128, pb128[:], gidx, gidx[:], 128.0, None, OP.mult)
        Scs = s.sb("Scs", [128, NS, 129], F32)
        self.memset(Scs, Scs[:], NEG, eng="dve")
        PGb, PG = self.getbuf(None, F32, [4096], "PGd")
        IKb, IKT = self.getbuf(None, BF16, [32, NPG], "IKTd")
        Rb, R = self.getbuf(T["x1"], F32, [128, 8], "Rd")
        cikp = I["cik"][:, :].rearrange("(g t) d -> g (t d)", t=128)
        rhs3 = s.sb("rhs3", [128, 3, 8], F32)
        IQm = s.sb("IQm", [128, 4, 8], BF16)
        iwb = s.sb("iwb", [128, 8], F32)
        t8 = s.sb("t8", [1, 8], F32)
        iqT, iw16, ikn = T["iqT"], T["iw16"], smp["ikn"]
        for si in range(NS):
            self.s.dma("pool", lambda e, si=si: e.indirect_dma_start(
                out=PG[0:NPG, :], out_offset=None, in_=cikp,
                in_offset=bass.IndirectOffsetOnAxis(ap=gidx[0:NPG, si:si + 1], axis=0)),
                R=[I["cik"], gidx], W=[PGb], key="PG")
            for g0 in range(0, 32, 4):
                Pb = P[(g0 // 4) % 2]
                for g in range(g0, g0 + 4):
                    self.tr(Pb, Pb[:, (g - g0) * NPG:(g - g0 + 1) * NPG], PGb, PG[0:NPG, g * 128:(g + 1) * 128], ident, ident[0:NPG, 0:NPG])
                self.acopy(IKb, IKT[:, g0:g0 + 4, :], Pb, Pb[:, 0:4 * NPG].rearrange("p (g n) -> p g n", g=4))
            for g in range(3):
                self.ts(rhs3, rhs3[:, g, :], cs, bmask[:, g, :], iqT[:, g, si:si + 1], None, OP.mult, R=[iqT])
            for g in range(3):
                self.mm(P[2], P[2][:, 0:8], cs, fold, rhs3, rhs3[:, g, :], g == 0, g == 2)
            for r in range(4):
                self.ts(IQm, IQm[:, r, :], P[2], P[2][:, 0:8], rmask[:, r:r + 1], None, OP.mult, R=[cs])
            for t in range(128):
                g, r = t // 4, t % 4
                Pb = P[3 + t // 64]
                col = (t % 64) * 8
                self.mm(Pb, Pb[0:NPG, col:col + 8], IKb, IKT[:, g, :], IQm, IQm[:, r, :], True, True)
            for hf in range(2):
                self.act(Rb, R[0:NPG, hf * 64:(hf + 1) * 64, :], P[3 + hf], P[3 + hf][0:NPG, 0:512].rearrange("p (t h) -> p t h", h=8), AF.Relu)
            self.mm(P[2], P[2][:, 8:16], sel, sel[0:NS, si, :], iw16, iw16[0:NS, 0:8], True, True)
            self.vcopy(iwb, iwb[:], P[2], P[2][:, 8:16])
            self.tt(Rb, R[0:NPG, :, :], Rb, R[0:NPG, :, :], iwb, iwb[0:NPG, :].unsqueeze(1).to_broadcast([NPG, 128, 8]), OP.mult)
            s.op("dve", lambda e, si=si: e.tensor_reduce(out=Scs[0:NPG, si, 0:128], in_=R[0:NPG, :, :], axis=AX.X, op=OP.add), R=[Rb], W=[Scs])
            self.mm(P[2], P[2][0:1, 16:24], ikn, ikn[:, si:si + 1], IQm, IQm[:, 0, :], True, True)
            self.act(t8, t8[:], P[2], P[2][0:1, 16:24], AF.Relu)
            self.tt(t8, t8[:], t8, t8[:], iwb, iwb[0:1, :], OP.mult)
            s.op("dve", lambda e, si=si: e.tensor_reduce(out=Scs[0:1, si, 128:129], in_=t8[:], axis=AX.X, op=OP.add), R=[t8], W=[Scs])
        mnp = s.sb("mnp", [128, NS], F32)
        mxp = s.sb("mxp", [128, NS], F32)
        s.op("dve", lambda e: e.tensor_reduce(out=mnp[0:NPG, :], in_=Scs[0:NPG, :, 0:128], axis=AX.X, op=OP.min), R=[Scs], W=[mnp])
        s.op("dve", lambda e: e.tensor_reduce(out=mxp[0:NPG, :], in_=Scs[0:NPG, :, 0:128], axis=AX.X, op=OP.max), R=[Scs], W=[mxp])
        self.tr(P[2], P[2][0:NS, 0:NPG], mnp, mnp[0:NPG, :], ident, ident[0:NPG, 0:NPG])
        self.tr(P[2], P[2][0:NS, 128:128 + NPG], mxp, mxp[0:NPG, :], ident, ident[0:NPG, 0:NPG])
        v3 = s.sb("v3", [NS, 4], F32)
        s.op("dve", lambda e: e.tensor_reduce(out=v3[:, 0:1], in_=P[2][0:NS, 0:NPG], axis=AX.X, op=OP.min), R=[P[2]], W=[v3])
        s.op("dve", lambda e: e.tensor_reduce(out=v3[:, 1:2], in_=P[2][0:NS, 128:128 + NPG], axis=AX.X, op=OP.max), R=[P[2]], W=[v3])
        self.tt(v3, v3[:, 2:3], v3, v3[:, 1:2], v3, v3[:, 0:1], OP.subtract)
        dg = s.sb("dg", [NS, 2, NS], F32)
        self.ts(dg, dg[:, 0, :], ident, ident[0:NS, 0:NS], v3[:, 0:1], None, OP.mult, R=[v3])
        self.ts(dg, dg[:, 1, :], ident, ident[0:NS, 0:NS], v3[:, 2:3], None, OP.mult, R=[v3])
        self.mm(P[2], P[2][:, 256:256 + 2 * NS], ones, ones[0:NS, :], dg, dg[:].rearrange("p a b -> p (a b)"), True, True)
        mr = s.sb("mr", [128, 2, NS], F32)
        self.vcopy(mr, mr[:].rearrange("p a b -> p (a b)"), P[2], P[2][:, 256:256 + 2 * NS])
        ddS = s.sb("ddS", [128, NS, NIT + 1], F32)
        self.tt(ddS, ddS[:], c["pow2"], c["pow2"][:].unsqueeze(1).to_broadcast([128, NS, NIT + 1]),
                mr, mr[:, 1, :].unsqueeze(2).to_broadcast([128, NS, NIT + 1]), OP.mult)
        mid = s.sb("mid", [128, NS], F32)
        self.tt(mid, mid[:], mr, mr[:, 0, :], ddS, ddS[:, :, 1], OP.add)
        jb, junk = self.getbuf(None, F32, [NS, 129], "junkd")
        cntp = s.sb("cntp", [128, NS], F32)
        sgn = s.sb("sgn", [128, NS], F32)
        for i in range(NIT):
            self.tt(jb, junk, Scs, Scs[:], mid, mid[:].unsqueeze(2).to_broadcast([128, NS, 129]), OP.is_ge)
            s.op("dve", lambda e: e.tensor_reduce(out=cntp[:], in_=junk, axis=AX.X, op=OP.add), R=[jb], W=[cntp])
            self.mm(P[2], P[2][:, 320:320 + NS], ones, ones[:], cntp, cntp[:], True, True)
            self.ts(sgn, sgn[:], P[2], P[2][:, 320:320 + NS], KT, -0.5, OP.is_ge, OP.add)
            self.tt(sgn, sgn[:], sgn, sgn[:], ddS, ddS[:, :, i + 1], OP.mult)
            self.tt(mid, mid[:], mid, mid[:], sgn, sgn[:], OP.add)
        lob = s.sb("lob", [128, NS], F32)
        self.tt(lob, lob[:], mid, mid[:], ddS, ddS[:, :, NIT], OP.subtract)
        Kcb, Kc = self.getbuf(None, F32, [16, 256], "Kcd")
        Vcb, Vc = self.getbuf(None, F32, [16, 256], "Vcd")
        tmb, tmp = self.getbuf(None, F32, [16, 64], "tmpd")
        geb, ge = self.getbuf(T["sq"], F32, [16, 31], "ged")
        tbb, tb = self.getbuf(T["rs"], F32, [16, 31], "tbd")
        qbb, q_bc = self.getbuf(T["rden"], F32, [512], "qbd")
        Wk = s.sb("Wk", [128, 129], F32)
        Wk2 = s.sb("Wk2", [128, 129], F32)
        m8 = s.sb("m8", [128, 16], F32)
        i8 = s.sb("i8", [128, 16], U32)
        cf = s.sb("cf", [128, 6, 16], F32)
        rowf = s.sb("rowf", [128, 16], F32)
        rowi = s.sb("rowi", [128, 16], I32)
        vnew = s.sb("vnew", [128, 1], F32)
        dgq = s.sb("dgq", [128, 128], BF16)
        dgk = s.sb("dgk", [128, 128], F32)
        knb = s.sb("knb", [128, 256], F32)
        vnb = s.sb("vnb", [128, 256], F32)
        lgr = s.sb("lgr", [128, 8, 16], F32)
        lg = s.sb("lg", [128, 8, 16], F32)
        lgn = s.sb("lgn", [128, 8], F32)
        tmn = s.sb("tmn", [128, 8, 64], F32)
        bp = s.sb("bp", [128, 16], F32)
        o4 = s.sb("o4", [8, 4, 64], F32)
        osel = s.sb("osel", [8, 64], F32)
        rd = s.sb("rd", [8, 1], F32)
        A2 = s.sb("A2", [8, 128], F32)
        ckv, cvv = I["ck"], I["cv"]
        qT, kn, tm, mixT = T["qT"], T["kn32"], T["tm"], T["mixT"]
        for si in range(NS):
            lo_s = lob[:, si:si + 1]
            self.vcopy(Wk, Wk[:], Scs, Scs[:, si, :])
            s.op("dve", lambda e: e.max(out=m8[:, 0:8], in_=Wk[:]), R=[Wk], W=[m8])
            s.op("dve", lambda e: e.max_index(out=i8[:, 0:8], in_max=m8[:, 0:8], in_values=Wk[:]), R=[Wk, m8], W=[i8])
            s.op("dve", lambda e: e.match_replace(out=Wk2[:], in_to_replace=m8[:, 0:8], in_values=Wk[:], imm_value=NEG), R=[Wk, m8], W=[Wk2])
            s.op("dve", lambda e: e.max(out=m8[:, 8:16], in_=Wk2[:]), R=[Wk2], W=[m8])
            s.op("dve", lambda e: e.max_index(out=i8[:, 8:16], in_max=m8[:, 8:16], in_values=Wk2[:]), R=[Wk2, m8], W=[i8])
            self.vcopy(cf, cf[:, 0, :], i8, i8[:])
            self.ts(cf, cf[:, 1, :], m8, m8[:], lo_s, None, OP.is_ge, R=[lob])
            self.ts(cf, cf[:, 2, :], cf, cf[:, 0, :], 127.5, None, OP.is_le)
            self.tt(cf, cf[:, 3, :], cf, cf[:, 1, :], cf, cf[:, 2, :], OP.mult)
            self.ts(vnew, vnew[:], Scs, Scs[:, si, 128:129], lo_s, None, OP.is_ge, R=[lob])
            self.ts(cf, cf[:, 4, :], cf, cf[:, 0, :], 127.0, None, OP.min)
            self.ts(rowf, rowf[:], cf, cf[:, 4, :], pb128[:, si:si + 1], None, OP.add, R=[pb128])
            self.vcopy(rowi, rowi[:], rowf, rowf[:])
            self.ts(cf, cf[:, 5, :], cf, cf[:, 4, :], -1.0, dbase, OP.mult, OP.add, R=[cs])
            for i in range(16):
                self.s.dma("pool", lambda e, i=i: e.indirect_dma_start(
                    out=Kc[:, i, :], out_offset=None, in_=ckv[:, :],
                    in_offset=bass.IndirectOffsetOnAxis(ap=rowi[:, i:i + 1], axis=0)), R=[ckv, rowi], W=[Kcb], key="Kc")
                self.s.dma("pool", lambda e, i=i: e.indirect_dma_start(
                    out=Vc[:, i, :], out_offset=None, in_=cvv[:, :],
                    in_offset=bass.IndirectOffsetOnAxis(ap=rowi[:, i:i + 1], axis=0)), R=[cvv, rowi], W=[Vcb], key="Vc")
            for ch in range(4):
                self.ts(dgq, dgq[:], identb, identb[:], qT[:, ch, si:si + 1], None, OP.mult, R=[qT])
                self.mm(P[3], P[3][:, ch * 128:(ch + 1) * 128], onesb, onesb[:], dgq, dgq[:], True, True)
            self.vcopy(qbb, q_bc, P[3], P[3][:, 0:512])
            for cc in range(2):
                self.ts(dgk, dgk[:], ident, ident[:], kn[:, cc, si:si + 1], None, OP.mult, R=[kn])
                self.mm(P[4], P[4][:, cc * 128:(cc + 1) * 128], ones, ones[:], dgk, dgk[:], True, True)
            self.mm(P[4], P[4][:, 256:512], sel, sel[0:NS, si, :], tm, tm[0:NS, 0:256], True, True)
            self.vcopy(knb, knb[:], P[4], P[4][:, 0:256])
            self.vcopy(vnb, vnb[:], P[4], P[4][:, 256:512])
            for pos in range(8):
                n_ = QPERM[pos] // 2
                self.tt(tmb, tmp, Kcb, Kc[:, :, n_ * 64:(n_ + 1) * 64], qbb,
                        q_bc[:, pos * 64:(pos + 1) * 64].unsqueeze(1).to_broadcast([128, 16, 64]), OP.mult)
                s.op("dve", lambda e, pos=pos: e.tensor_reduce(out=lgr[:, pos, :], in_=tmp, axis=AX.X, op=OP.add), R=[tmb], W=[lgr])
            for A in range(2):
                q4 = q_bc[:, A * 256:(A + 1) * 256].rearrange("p (b c d) -> p b c d", b=2, c=2)
                k2 = knb[:, A * 128:(A + 1) * 128].rearrange("p (c d) -> p c d", c=2).unsqueeze(1).to_broadcast([128, 2, 2, 64])
                self.tt(tmn, tmn[:, A * 4:(A + 1) * 4, :].rearrange("p (b c) d -> p b c d", b=2), qbb, q4, knb, k2, OP.mult)
            s.op("dve", lambda e: e.tensor_reduce(out=lgn[:], in_=tmn[:], axis=AX.X, op=OP.add), R=[tmn], W=[lgn])
            self.tt(geb, ge, cf, cf[:, 5, :].unsqueeze(2).to_broadcast([128, 16, 31]), thr, thr[:].unsqueeze(1).to_broadcast([128, 16, 31]), OP.is_ge)
            for pos in range(8):
                self.tt(tbb, tb, geb, ge, drel, drel[:, pos, :].unsqueeze(1).to_broadcast([128, 16, 31]), OP.mult)
                s.op("dve", lambda e: e.tensor_reduce(out=bp[:], in_=tb, axis=AX.X, op=OP.add), R=[tbb], W=[bp])
                self.stt(lg, lg[:, pos, :], bp, bp[:], rb0[:, pos:pos + 1], lgr, lgr[:, pos, :], OP.add, OP.add, R=[rb0])
            self.tt(lgn, lgn[:], lgn, lgn[:], rb0, rb0[:], OP.add)
            self.act(lg, lg[:], lg, lg[:], AF.Exp, bias=c["nshift"][:, 0:1], scale=1.0, R=[c["nshift"]])
            self.tt(lg, lg[:], lg, lg[:], cf, cf[:, 3, :].unsqueeze(1).to_broadcast([128, 8, 16]), OP.mult)
            self.act(lgn, lgn[:], lgn, lgn[:], AF.Exp, bias=c["nshift"][:, 0:1], scale=1.0, R=[c["nshift"]])
            self.ts(lgn, lgn[:], lgn, lgn[:], vnew[:, 0:1], None, OP.mult, R=[vnew])
            for i in range(16):
                self.mm(P[5], P[5][0:8, 0:256], lg, lg[:, :, i], Vcb, Vc[:, i, :], i == 0, False)
                self.mm(P[6], P[6][0:8, 0:1], lg, lg[:, :, i], ones, ones[:, 0:1], i == 0, False)
            self.mm(P[5], P[5][0:8, 0:256], lgn, lgn[:], vnb, vnb[:], False, True)
            self.mm(P[6], P[6][0:8, 0:1], lgn, lgn[:], ones, ones[:, 0:1], False, True)
            self.tt(o4, o4[:], P[5], P[5][0:8, 0:256].rearrange("p (n d) -> p n d", n=4), cs, nsel.unsqueeze(2).to_broadcast([8, 4, 64]), OP.mult)
            s.op("dve", lambda e: e.tensor_reduce(out=osel[:], in_=o4[:].rearrange("p n d -> p d n"), axis=AX.X, op=OP.add), R=[o4], W=[osel])
            s.op("dve", lambda e: e.reciprocal(out=rd[:], in_=P[6][0:8, 0:1]), R=[P[6]], W=[rd])
            self.ts(A2, A2[:, 0:64], osel, osel[:], rd[:, 0:1], even, OP.mult, OP.mult, R=[rd, cs])
            self.ts(A2, A2[:, 64:128], osel, osel[:], rd[:, 0:1], odd, OP.mult, OP.mult, R=[rd, cs])
            self.mm(P[7], P[7][:, 0:4], A2, A2[:], cs, pairsel, True, True)
            self.vcopy(mixT, mixT[:, 0:4, si], P[7], P[7][:, 0:4])

    def phase2(self, l):
        s, I, O, c = self.s, self.I, self.O, self.c
        S, NS = self.S, self.NS
        NF = DFF // 128
        WN = 256
        wu = s.sb("wu", [128, 8, 2 * DFF], BF16)
        wd = s.sb("wd", [128, NF, D], BF16)
        for k in range(8):
            for c0 in range(0, 2 * DFF, 2048):
                n = min(2048, 2 * DFF - c0)
                self.dma("pool", wu, wu[:, k, c0:c0 + n], I["w_up"], I["w_up"][l, k * 128:(k + 1) * 128, c0:c0 + n], key=None)
        for f in range(NF):
            self.dma("pool", wd, wd[:, f, :], I["w_down"], I["w_down"][l, f * 128:(f + 1) * 128, :], key=None)
        g2 = self.load_gvec("g2", I["norm2_g"], l)
        cw = s.sb("cw", [128, 44, 3], F32)
        cb = s.sb("cb", [128, 44], F32)
        for j in range(3):
            ap = bass.AP(I["conv_w"].t.tensor, (l * 3 + j) * 2 * DFF, [[1, 128], [128, 44]])
            self.s.dma("sp", lambda e, ap=ap, j=j: e.dma_start(out=cw[:, :, j], in_=ap, allow_slow_non_contiguous=True), R=[I["conv_w"]], W=[cw], key="cw%d" % j)
        apb = bass.AP(I["conv_b"].t.tensor, l * 2 * DFF, [[1, 128], [128, 44]])
        self.s.dma("sp", lambda e: e.dma_start(out=cb[:], in_=apb, allow_slow_non_contiguous=True), R=[I["conv_b"]], W=[cb], key="cb")
        P = [s.ps("Q%d" % i, [128, 512], F32) for i in range(8)]
        xw2 = [s.sb("xw%d" % i, [128, 8, WN], F32) for i in range(2)]
        sq = s.sb("sq8b", [128, 8, WN], F32)
        rs = s.sb("rsb", [128, WN], F32)
        h2 = s.sb("h2", [128, 8, WN], BF16)
        aT = s.sb("aT", [128, NF, WN], BF16)
        t1 = Ring([s.sb("t1_%d" % i, [128, WN], F32) for i in range(2)])
        t2 = Ring([s.sb("t2_%d" % i, [128, WN], F32) for i in range(2)])
        cg = Ring([s.sb("cg_%d" % i, [128, WN], F32) for i in range(2)])
        cv = Ring([s.sb("cv_%d" % i, [128, WN], F32) for i in range(2)])
        sg = Ring([s.sb("sg_%d" % i, [128, WN], F32) for i in range(2)])
        xo = Ring([s.sb("xo_%d" % i, [128, WN], F32) for i in range(2)])
        ytok = s.sb("ytok", [128, D], F32)
        ctk = Ring([s.sb("ctk_%d" % i, [NS, 512], F32) for i in range(2)])
        ident = c["ident"]

        def up_pair(f, n, Pg, Pv):
            for (Pb, ff) in ((Pg, f), (Pv, NF + f)):
                for k in range(8):
                    self.mm(Pb, Pb[:, 0:n], wu, wu[:, k, ff * 128:(ff + 1) * 128], h2, h2[:, k, 0:n], k == 0, k == 7)

        def norm2(xb, xap3, n):
            self.rms_feature_major(xb, xap3, n, g2, h2, P[6], sq, rs)

        def up_rows_out(col0, ncols, dst, dst_ap_fn):
            for ci, c0 in enumerate(range(0, 2 * DFF, 512)):
                Pb = P[4 + ci % 2]
                for k in range(8):
                    self.mm(Pb, Pb[0:ncols, 0:512], h2, h2[:, k, col0:col0 + ncols], wu, wu[:, k, c0:c0 + 512], k == 0, k == 7)
                ct = ctk.next()
                self.vcopy(ct, ct[0:ncols, :], Pb, Pb[0:ncols, 0:512])
                self.dma("sp", dst, dst_ap_fn(c0), ct, ct[0:ncols, :], key=ct.name)

        step = WN - 2
        starts = list(range(0, S, step))
        def load_norm(wi):
            xw = xw2[wi % 2]
            st0 = starts[wi]
            nnew = min(step, S - st0)
            n = nnew + 2
            if st0 == 0:
                self.memset(xw, xw[:, :, 0:2], 0.0, eng="dve")
                self.dma("sp", xw, xw[:, :, 2:n], self.XA, self.XA[:, 0:nnew].rearrange("(k p) t -> p k t", p=128))
            else:
                self.dma("sp", xw, xw[:, :, 0:n], self.XA, self.XA[:, st0 - 2:st0 + nnew].rearrange("(k p) t -> p k t", p=128))
            norm2(xw, xw[:, :, 0:n], n)

        load_norm(0)
        for wi, st0 in enumerate(starts):
            xw = xw2[wi % 2]
            nnew = min(step, S - st0)
            n = nnew + 2
            if wi == len(starts) - 1:
                up_rows_out(n - 2, 2, O["conv_p"], lambda c0: O["conv_p"][l, :, c0:c0 + 512])
            for f in range(NF):
                Pg, Pv = P[(f % 2) * 2], P[(f % 2) * 2 + 1]
                up_pair(f, n, Pg, Pv)
                outs = []
                for (Pb, ff, ring) in ((Pg, f, cg), (Pv, NF + f, cv)):
                    a1, a2, cc_ = t1.next(), t2.next(), ring.next()
                    self.act(a1, a1[:, 0:nnew], Pb, Pb[:, 2:n], AF.Identity, bias=cb[:, ff:ff + 1], scale=cw[:, ff, 2:3], R=[cb, cw])
                    self.stt(a2, a2[:, 0:nnew], Pb, Pb[:, 1:n - 1], cw[:, ff, 1:2], a1, a1[:, 0:nnew], OP.mult, OP.add, R=[cw])
                    self.stt(cc_, cc_[:, 0:nnew], Pb, Pb[:, 0:n - 2], cw[:, ff, 0:1], a2, a2[:, 0:nnew], OP.mult, OP.add, R=[cw])
                    outs.append(cc_)
                sgt = sg.next()
                self.act(sgt, sgt[:, 0:nnew], outs[0], outs[0][:, 0:nnew], AF.Silu)
                self.tt(aT, aT[:, f, 0:nnew], sgt, sgt[:, 0:nnew], outs[1], outs[1][:, 0:nnew], OP.mult)
            if wi + 1 < len(starts):
                load_norm(wi + 1)
            for dm in range(8):
                Pb = P[4 + dm % 2]
                for f in range(NF):
                    self.mm(Pb, Pb[:, 0:nnew], wd, wd[:, f, dm * 128:(dm + 1) * 128], aT, aT[:, f, 0:nnew], f == 0, f == NF - 1)
                if l == 0:
                    xt = xo.next()
                    self.tt(xt, xt[:, 0:nnew], Pb, Pb[:, 0:nnew], xw, xw[:, dm, 2:n], OP.add)
                    self.dma("sp", self.XB, self.XB[dm * 128:(dm + 1) * 128, st0:st0 + nnew], xt, xt[:, 0:nnew], key=xt.name)
                else:
                    self.tt(xw, xw[:, dm, 2:n], Pb, Pb[:, 0:nnew], xw, xw[:, dm, 2:n], OP.add)
            if l == 1:
                for b0 in range(0, nnew, 128):
                    nb = min(128, nnew - b0)
                    for k0 in (0, 4):
                        Pb = P[6 + (k0 // 4)]
                        for k in range(k0, k0 + 4):
                            self.tr(Pb, Pb[0:nb, (k - k0) * 128:(k - k0 + 1) * 128], xw, xw[:, k, 2 + b0:2 + b0 + nb], ident, ident[:])
                        self.acopy(ytok, ytok[0:nb, k0 * 128:(k0 + 4) * 128], Pb, Pb[0:nb, 0:512])
                    self.dma("sp", O["y_p"], O["y_p"][st0 + b0:st0 + b0 + nb, :], ytok, ytok[0:nb, :], key="ytok")

        xsT = c["xsT"]
        norm2(xsT, xsT[:], NS)
        up_rows_out(0, NS, O["conv_s"], lambda c0: O["conv_s"][l, :, 1, c0:c0 + 512])
        sT = [s.sb("sT%d" % i, [128, 44, NS], F32) for i in range(2)]
        for i in range(2):
            for ci, c0 in enumerate(range(0, 2 * DFF, 512)):
                ct = ctk.next()
                self.dma("sp", ct, ct[0:NS, :], I["sconv"], I["sconv"][l, :, i, c0:c0 + 512], key=ct.name)
                if i == 1:
                    self.dma("sp", O["conv_s"], O["conv_s"][l, :, 0, c0:c0 + 512], ct, ct[0:NS, :], key=ct.name + "o")
                Pb = P[6 + ci % 2]
                for j in range(4):
                    self.tr(Pb, Pb[:, j * NS:(j + 1) * NS], ct, ct[0:NS, j * 128:(j + 1) * 128], ident, ident[0:NS, 0:NS])
                f0 = c0 // 128
                self.vcopy(sT[i], sT[i][:, f0:f0 + 4, :], Pb, Pb[:, 0:4 * NS].rearrange("p (f s) -> p f s", s=NS))
        for f in range(NF):
            Pg, Pv = P[(f % 2) * 2], P[(f % 2) * 2 + 1]
            up_pair(f, NS, Pg, Pv)
            outs = []
            for (Pb, ff, ring) in ((Pg, f, cg), (Pv, NF + f, cv)):
                a1, a2, cc_ = t1.next(), t2.next(), ring.next()
                self.act(a1, a1[:, 0:NS], Pb, Pb[:, 0:NS], AF.Identity, bias=cb[:, ff:ff + 1], scale=cw[:, ff, 2:3], R=[cb, cw])
                self.stt(a2, a2[:, 0:NS], sT[1], sT[1][:, ff, :], cw[:, ff, 1:2], a1, a1[:, 0:NS], OP.mult, OP.add, R=[cw])
                self.stt(cc_, cc_[:, 0:NS], sT[0], sT[0][:, ff, :], cw[:, ff, 0:1], a2, a2[:, 0:NS], OP.mult, OP.add, R=[cw])
                outs.append(cc_)
            sgt = sg.next()
            self.act(sgt, sgt[:, 0:NS], outs[0], outs[0][:, 0:NS], AF.Silu)
            self.tt(aT, aT[:, f, 0:NS], sgt, sgt[:, 0:NS], outs[1], outs[1][:, 0:NS], OP.mult)
        for dm in range(8):
            Pb = P[4 + dm % 2]
            for f in range(NF):
                self.mm(Pb, Pb[:, 0:NS], wd, wd[:, f, dm * 128:(dm + 1) * 128], aT, aT[:, f, 0:NS], f == 0, f == NF - 1)
            self.tt(xsT, xsT[:, dm, :], Pb, Pb[:, 0:NS], xsT, xsT[:, dm, :], OP.add)
        if l == 1:
            for k0 in (0, 4):
                for k in range(k0, k0 + 4):
                    self.tr(P[6], P[6][0:NS, (k - k0) * 128:(k - k0 + 1) * 128], xsT, xsT[:, k, :], ident, ident[:])
                self.acopy(ytok, ytok[0:NS, k0 * 128:(k0 + 4) * 128], P[6], P[6][0:NS, 0:512])
            self.dma("sp", O["y_s"], O["y_s"][:, :], ytok, ytok[0:NS, :], key="ytok")


def make_consts(S, NPG=128):
    m = np.arange(TABLEN)
    d = m - 127
    oh = np.zeros((32, TABLEN), np.float32)
    b = t5_bucket_np(np.maximum(d, 0))
    valid = d >= 0
    oh[b[valid], m[valid]] = 1.0
    w = np.array([2, 4, 8, 16], np.float32)
    cinv = np.zeros((128, 2, 128), np.float32)
    winv = np.zeros((128, 2), np.float32)
    pos = np.arange(128, dtype=np.float32)
    for g in range(4):
        cc, hh = g // 2, g % 2
        cinv[hh * 64:(hh + 1) * 64, cc, :] = 1.0 / np.minimum(pos + 1.0, w[g])[None, :]
        winv[hh * 64:(hh + 1) * 64, cc] = 1.0 / w[g]
    pow2 = np.tile((2.0 ** -np.arange(NIT + 1, dtype=np.float64)).astype(np.float32)[None, :], (128, 1))
    dd = np.arange(0, 4096)
    bb = t5_bucket_np(dd)
    thr = np.array([dd[bb >= k].min() for k in range(1, 32)], np.float32)
    thr = np.tile(thr[None, :], (128, 1))
    iota = np.tile(np.arange(128, dtype=np.float32)[None, :], (128, 1))
    smp = np.zeros((128, 167), np.float32)
    k = np.arange(128)
    smp[:, 0:128] = (k[:, None] % 32 == k[None, :] % 32)
    for g in range(3):
        for h in range(8):
            smp[:, 128 + g * 8 + h] = (k < 96) & (h // 3 == g) & (k // 32 == h % 3)
    for r in range(4):
        smp[:, 152 + r] = (k // 32 == r)
    smp[:, 156] = np.where(k < NPG, (NPG - k) * 128, 0)
    for pos in range(8):
        for n in range(4):
            smp[pos, 157 + n] = float(n == QPERM[pos] // 2)
        smp[pos, 161] = float(pos % 2 == 0)
        smp[pos, 162] = float(pos % 2 == 1)
        for ch in range(4):
            smp[pos, 163 + ch] = float(pos // 2 == ch)
    return dict(c_oh=oh, c_cinv=cinv, c_winv=winv, c_pow2=pow2, c_thr=thr, c_iota=iota, c_smp=smp)


def core_inputs(inp, core, NS, consts):
    b = core % 4
    f = lambda a: np.ascontiguousarray(a)
    sl = slice(core * NS, (core + 1) * NS)
    m = {}
    m["xp"] = f(inp["x_prompt"][b])
    m["xs"] = f(inp["x_sample"][sl, 0])
    m["memp"] = f(inp["mem_prompt"][b])
    ck = inp["cache_k"]
    m["ck"] = ck.reshape(ck.shape[0] * ck.shape[1] * ck.shape[2], 256)
    cv = inp["cache_v"]
    m["cv"] = cv.reshape(cv.shape[0] * cv.shape[1] * cv.shape[2], 256)
    ci = inp["cache_idx_k"]
    m["cik"] = ci.reshape(ci.shape[0] * ci.shape[1] * ci.shape[2], 32)
    m["cmk"] = f(inp["cache_mem_k"][:, sl].reshape(2, NS, 256, 256))
    m["cmv"] = f(inp["cache_mem_v"][:, sl].reshape(2, NS, 256, 256))
    m["spool"] = f(inp["state_pool"][:, sl])
    m["sconv"] = f(inp["state_conv"][:, sl])
    m["ptab"] = f(inp["page_table"][sl].astype(np.int32))
    for k in ("rel_bias", "norm1_g", "w_in", "q_norm_g", "k_norm_g", "mem_norm_g", "w_mem_kv", "mq_norm_g", "mk_norm_g",
              "w_pool", "pool_scale", "w_out", "norm2_g", "w_up", "conv_w", "conv_b", "w_down"):
        m[k] = f(inp[k])
    m.update(consts)
    return m


_CACHE = {}


def run(inp, n_cores=8, NS=4, with_sample_dsa=True):
    inp = {k: np.asarray(v) for k, v in inp.items()}
    S = inp["x_prompt"].shape[1]
    NPG = inp["page_table"].shape[1]
    NPOOL = inp["cache_k"].shape[1]
    key = (S, NPG, NPOOL, NS, with_sample_dsa)
    if key not in _CACHE:
        _CACHE[key] = K(S, NPG, NPOOL, NS, with_sample_dsa).build()
    nc = _CACHE[key]
    consts = make_consts(S, NPG)
    maps = [core_inputs(inp, c, NS, consts) for c in range(n_cores)]
    res = run_bass_kernel_spmd(nc, maps, core_ids=list(range(n_cores))).results
    B = inp["x_prompt"].shape[0]
    nb = min(B, n_cores)
    DB = n_cores * NS
    st = lambda name, shp: np.stack([res[c][name] for c in range(nb)], axis=0)
    y_p = st("y_p", None)
    y_s = np.concatenate([res[c]["y_s"] for c in range(n_cores)], axis=0)[:, None, :]
    k_p = st("k_p", None).transpose(1, 0, 2, 3).reshape(2, nb, S, 4, 64)
    v_p = st("v_p", None).transpose(1, 0, 2, 3).reshape(2, nb, S, 4, 64)
    ik_p = st("ik_p", None).transpose(1, 0, 2, 3)
    pool_p = st("pool_p", None).transpose(1, 0, 2, 3)
    conv_p = st("conv_p", None).transpose(1, 0, 2, 3)
    mk_p = st("mk_p", None).transpose(1, 0, 2, 3).reshape(2, nb, 256, 4, 64)
    mv_p = st("mv_p", None).transpose(1, 0, 2, 3).reshape(2, nb, 256, 4, 64)
    cat = lambda name: np.concatenate([res[c][name] for c in range(n_cores)], axis=1)
    k_s = cat("k_s").reshape(2, DB, 1, 4, 64)
    v_s = cat("v_s").reshape(2, DB, 1, 4, 64)
    ik_s = cat("ik_s").reshape(2, DB, 1, 32)
    pool_s = cat("pool_s")
    conv_s = cat("conv_s")
    outs = (y_p, y_s, k_p, v_p, ik_p, pool_p, conv_p, mk_p, mv_p, k_s, v_s, ik_s, pool_s, conv_s)
    return tuple(np.ascontiguousarray(o.astype(np.float32)) for o in outs)


def kernel(**inputs):
    return run(inputs, n_cores=8, NS=4)
```
